# Optimizing a Trainium2 kernel written in Bass

```python
import math
import jax, jax.numpy as jnp
from jax import lax
import numpy as np

D_MODEL = 1024
BATCH = 8
SEQ = 4096
DEPTH = 1

GRID_W = 64
CTX_LEN = 256
N_MOD = 9
FFN_DIM = 2816
RMS_EPS = 1e-6
S5_WIDTH = 512
S5_GROUP = 16
S5_GROUPS = S5_WIDTH // S5_GROUP
S5_STATE = 64
RWKV_WIDTH = 512
RWKV_HEAD = 64
RWKV_HEADS = RWKV_WIDTH // RWKV_HEAD
DECAY_LORA = 64
AAA_LORA = 64
GATE_LORA = 128
LN_X_EPS = 64e-5
CONV_K = 3
IN_COLS = S5_WIDTH + 3 * RWKV_WIDTH + 2 * DECAY_LORA + 2 * AAA_LORA + GATE_LORA + 2 * D_MODEL

kernel_name = 'hybrid_s5_rwkv7_macaron_prefix_dit'


def rmsnorm(x, g):
    xf = x.astype(jnp.float32)
    y = xf * lax.rsqrt(jnp.mean(xf * xf, axis=-1, keepdims=True) + RMS_EPS)
    return (y * g.astype(jnp.float32)).astype(x.dtype)


def modulate(x, g, shift, scale):
    return rmsnorm(x, g) * (1.0 + scale) + shift


def half_ffn(x, mod, j, g, w_gate, w_up, w_down):
    h = modulate(x, g, mod[3 * j], mod[3 * j + 1])
    y = (jax.nn.silu(h @ w_gate) * (h @ w_up)) @ w_down
    return x + 0.5 * mod[3 * j + 2] * y


def centred_conv(x, w, rows):
    b, n, ch = x.shape
    img = x.reshape(b, rows, n // rows, ch)
    out = lax.conv_general_dilated(img, w[:, :, None, :].astype(x.dtype), window_strides=(1, 1), padding='SAME',
                                   dimension_numbers=('NHWC', 'HWIO', 'NHWC'), feature_group_count=ch)
    return out.reshape(b, n, ch)


def _cmul(ar, ai, br, bi):
    return ar * br - ai * bi, ar * bi + ai * br


def _s5_combine(e1, e2):
    a1r, a1i, b1r, b1i = e1
    a2r, a2i, b2r, b2i = e2
    ar, ai = _cmul(a2r, a2i, a1r, a1i)
    br, bi = _cmul(a2r, a2i, b1r, b1i)
    return ar, ai, br + b2r, bi + b2i


def s5_scan(u, lam_re, lam_im, log_dt, b_re, b_im, c_re, c_im, h0_re, h0_im, reverse):
    f32 = jnp.float32
    lam_re = lam_re.astype(f32)
    lam_im = lam_im.astype(f32)
    dt = jnp.exp(log_dt.astype(f32))[:, None]
    mag = jnp.exp(dt * lam_re)
    ab_re = mag * jnp.cos(dt * lam_im)
    ab_im = mag * jnp.sin(dt * lam_im)
    den = lam_re * lam_re + lam_im * lam_im
    z_re = ((ab_re - 1.0) * lam_re + ab_im * lam_im) / den
    z_im = (ab_im * lam_re - (ab_re - 1.0) * lam_im) / den
    bb_re, bb_im = _cmul(z_re[..., None], z_im[..., None], b_re.astype(f32), b_im.astype(f32))
    if reverse:
        u = u[:, ::-1]
    xr = jnp.einsum('blgi,gpi->lbgp', u, bb_re)
    xi = jnp.einsum('blgi,gpi->lbgp', u, bb_im)
    hr, hi = _cmul(ab_re, ab_im, h0_re, h0_im)
    xr = xr.at[0].add(hr)
    xi = xi.at[0].add(hi)
    n = u.shape[1]
    a_re = jnp.broadcast_to(ab_re, (n, 1) + ab_re.shape)
    a_im = jnp.broadcast_to(ab_im, (n, 1) + ab_im.shape)
    _, _, s_re, s_im = lax.associative_scan(_s5_combine, (a_re, a_im, xr, xi), axis=0)
    y = jnp.einsum('lbgp,gip->blgi', s_re, c_re.astype(f32)) - jnp.einsum('lbgp,gip->blgi', s_im, c_im.astype(f32))
    if reverse:
        y = y[:, ::-1]
    return y, s_re[-1], s_im[-1]


def rwkv_step(S, inp):
    r, w, k, v, aa, bb = inp
    sa = jnp.einsum('bhij,bhj->bhi', S, aa)
    S = S * w[:, :, None, :] + sa[..., None] * bb[:, :, None, :] + v[..., None] * k[:, :, None, :]
    return S, jnp.einsum('bhij,bhj->bhi', S, r)


def rwkv_scan(r, w, k, v, aa, bb, S0, reverse):
    xs = (r.transpose(1, 0, 2, 3), w.transpose(1, 0, 2, 3), k.transpose(1, 0, 2, 3),
          v.transpose(1, 0, 2, 3), aa.transpose(1, 0, 2, 3), bb.transpose(1, 0, 2, 3))
    S, ys = lax.scan(rwkv_step, S0, xs, reverse=reverse)
    return ys.transpose(1, 0, 2, 3), S


def zero_states(b):
    zs = jnp.zeros((b, S5_GROUPS, S5_STATE), jnp.float32)
    zr = jnp.zeros((b, RWKV_HEADS, RWKV_HEAD, RWKV_HEAD), jnp.float32)
    return (zs, zs, zs, zs, zr, zr)


def token_mixers(h, rows, init, p, need_out):
    b, n, _ = h.shape
    f32 = jnp.float32
    proj = h @ p['w_in']
    o1 = S5_WIDTH
    o2 = o1 + 3 * RWKV_WIDTH
    o3 = o2 + 2 * DECAY_LORA
    o4 = o3 + 2 * AAA_LORA
    o5 = o4 + GATE_LORA
    o6 = o5 + D_MODEL
    u = proj[..., :o1].astype(f32)
    rkv = proj[..., o1:o2]
    wd = proj[..., o2:o3].astype(f32).reshape(b, n, 2, DECAY_LORA)
    ad = proj[..., o3:o4].astype(f32).reshape(b, n, 2, AAA_LORA)
    gd = proj[..., o4:o5].astype(f32)
    gate_a = proj[..., o5:o6]
    gate_b = proj[..., o6:]

    ug = u.reshape(b, n, S5_GROUPS, S5_GROUP)
    ya_f, sf_re, sf_im = s5_scan(ug, p['s5_A_re'][0], p['s5_A_im'][0], p['s5_log_dt'][0], p['s5_B_re'][0],
                                 p['s5_B_im'][0], p['s5_C_re'][0], p['s5_C_im'][0], init[0], init[1], False)
    ya_b, sb_re, sb_im = s5_scan(ug, p['s5_A_re'][1], p['s5_A_im'][1], p['s5_log_dt'][1], p['s5_B_re'][1],
                                 p['s5_B_im'][1], p['s5_C_re'][1], p['s5_C_im'][1], init[2], init[3], True)

    rkv = centred_conv(rkv, p['rwkv_conv'], rows).astype(f32)
    r, k, v = jnp.split(rkv, 3, axis=-1)
    w_log = p['rwkv_w0'] + jnp.einsum('bldr,drc->bldc', jnp.tanh(wd), p['rwkv_w2'])
    decay = jnp.exp(-jnp.exp(-jax.nn.softplus(-w_log) - 0.5))
    iclr = jax.nn.sigmoid(p['rwkv_a0'] + jnp.einsum('bldr,drc->bldc', ad, p['rwkv_a2']))
    g = jax.nn.sigmoid(gd) @ p['rwkv_g2']

    def heads(t):
        return t.reshape(t.shape[:-1] + (RWKV_HEADS, RWKV_HEAD))

    kk = heads(k * p['rwkv_k_k'])
    kk = kk * lax.rsqrt(jnp.sum(kk * kk, axis=-1, keepdims=True) + 1e-12)
    k_t = heads(k[:, :, None] * (1.0 + (iclr - 1.0) * p['rwkv_k_a']))
    a_h = heads(iclr)
    d_h = heads(decay)
    rh = heads(r)
    vh = heads(v)
    yb_f, S_f = rwkv_scan(rh, d_h[:, :, 0], k_t[:, :, 0], vh, -kk, kk * a_h[:, :, 0], init[4], False)
    yb_b, S_b = rwkv_scan(rh, d_h[:, :, 1], k_t[:, :, 1], vh, -kk, kk * a_h[:, :, 1], init[5], True)
    states = (sf_re, sf_im, sb_re, sb_im, S_f, S_b)
    if not need_out:
        return None, states

    ya = (ya_f + ya_b).reshape(b, n, S5_WIDTH) + p['s5_D'] * u
    ya = jax.nn.gelu(ya)
    ya = ya * jax.nn.sigmoid(ya @ p['s5_w_glu'])

    yb = yb_f + yb_b
    mu = jnp.mean(yb, axis=-1, keepdims=True)
    var = jnp.mean(jnp.square(yb - mu), axis=-1, keepdims=True)
    yb = ((yb - mu) * lax.rsqrt(var + LN_X_EPS)).reshape(b, n, RWKV_WIDTH) * p['rwkv_ln_g'] + p['rwkv_ln_b']
    bonus = jnp.sum(jnp.sum(rh[:, :, None] * k_t * p['rwkv_r_k'], axis=-1, keepdims=True), axis=2) * vh
    yb = yb + bonus.reshape(b, n, RWKV_WIDTH)
    yb = (yb * g) @ p['rwkv_w_o']

    merged = jax.nn.sigmoid(gate_a) * (ya @ p['s5_w_proj']) + jax.nn.sigmoid(gate_b) * yb
    return merged.astype(h.dtype) @ p['w_out'], states


def setup_inputs(seed: int = 0) -> dict:
    key = jax.random.key(seed)
    ks = iter(jax.random.split(key, 48))

    def nrm(shape, scale):
        return scale * jax.random.normal(next(ks), shape, jnp.float32)

    D = D_MODEL
    G, P, GS = S5_GROUPS, S5_STATE, S5_GROUP
    W = RWKV_WIDTH
    x = nrm((BATCH, SEQ, D), 1.0)
    c = nrm((BATCH, D), 1.0)
    ctx = nrm((BATCH, CTX_LEN, D), 1.0)
    c_ctx = nrm((D,), 1.0)
    w_mod = nrm((DEPTH, D, N_MOD * D), 0.5 * D ** -0.5)
    b_mod = nrm((DEPTH, N_MOD * D), 0.01)
    norm_g = 1.0 + nrm((DEPTH, 3, D), 0.02)
    ffn_w_gate = nrm((DEPTH, 2, D, FFN_DIM), D ** -0.5)
    ffn_w_up = nrm((DEPTH, 2, D, FFN_DIM), D ** -0.5)
    ffn_w_down = nrm((DEPTH, 2, FFN_DIM, D), FFN_DIM ** -0.5)
    w_in = nrm((DEPTH, D, IN_COLS), D ** -0.5)
    n_idx = jnp.arange(P, dtype=jnp.float32)
    s5_A_re = -0.5 + nrm((DEPTH, 2, G, P), 0.01)
    s5_A_im = math.pi * n_idx + nrm((DEPTH, 2, G, P), 0.01)
    s5_log_dt = jax.random.uniform(next(ks), (DEPTH, 2, G), jnp.float32, math.log(1e-3), math.log(1e-1))
    s5_B_re = nrm((DEPTH, 2, G, P, GS), (2.0 * GS) ** -0.5)
    s5_B_im = nrm((DEPTH, 2, G, P, GS), (2.0 * GS) ** -0.5)
    s5_C_re = nrm((DEPTH, 2, G, GS, P), P ** -0.5)
    s5_C_im = nrm((DEPTH, 2, G, GS, P), P ** -0.5)
    s5_D = nrm((DEPTH, S5_WIDTH), 1.0)
    s5_w_glu = nrm((DEPTH, S5_WIDTH, S5_WIDTH), S5_WIDTH ** -0.5)
    s5_w_proj = nrm((DEPTH, S5_WIDTH, D), S5_WIDTH ** -0.5)
    rwkv_conv = nrm((DEPTH, CONV_K, CONV_K, 3 * W), 0.1).at[:, 1, 1].add(1.0)
    ramp = jnp.linspace(0.0, 1.0, W, dtype=jnp.float32)
    rwkv_w0 = -6.0 + 7.0 * ramp ** 0.8 + nrm((DEPTH, 2, W), 0.1)
    rwkv_w2 = nrm((DEPTH, 2, DECAY_LORA, W), 0.1 * DECAY_LORA ** -0.5)
    rwkv_a0 = nrm((DEPTH, 2, W), 0.1)
    rwkv_a2 = nrm((DEPTH, 2, AAA_LORA, W), AAA_LORA ** -0.5)
    rwkv_g2 = nrm((DEPTH, GATE_LORA, W), GATE_LORA ** -0.5)
    rwkv_k_k = 0.85 + nrm((DEPTH, W), 0.02)
    rwkv_k_a = 1.0 + nrm((DEPTH, W), 0.02)
    rwkv_r_k = nrm((DEPTH, RWKV_HEADS, RWKV_HEAD), 0.1)
    rwkv_ln_g = 1.0 + nrm((DEPTH, W), 0.02)
    rwkv_ln_b = nrm((DEPTH, W), 0.01)
    rwkv_w_o = nrm((DEPTH, W, D), W ** -0.5)
    w_out = nrm((DEPTH, D, D), D ** -0.5)
    final_g = 1.0 + nrm((D,), 0.02)
    return {'x': x, 'c': c, 'ctx': ctx, 'c_ctx': c_ctx, 'w_mod': w_mod, 'b_mod': b_mod, 'norm_g': norm_g,
            'ffn_w_gate': ffn_w_gate, 'ffn_w_up': ffn_w_up, 'ffn_w_down': ffn_w_down, 'w_in': w_in,
            's5_A_re': s5_A_re, 's5_A_im': s5_A_im, 's5_log_dt': s5_log_dt, 's5_B_re': s5_B_re, 's5_B_im': s5_B_im,
            's5_C_re': s5_C_re, 's5_C_im': s5_C_im, 's5_D': s5_D, 's5_w_glu': s5_w_glu, 's5_w_proj': s5_w_proj,
            'rwkv_conv': rwkv_conv, 'rwkv_w0': rwkv_w0, 'rwkv_w2': rwkv_w2, 'rwkv_a0': rwkv_a0, 'rwkv_a2': rwkv_a2,
            'rwkv_g2': rwkv_g2, 'rwkv_k_k': rwkv_k_k, 'rwkv_k_a': rwkv_k_a, 'rwkv_r_k': rwkv_r_k,
            'rwkv_ln_g': rwkv_ln_g, 'rwkv_ln_b': rwkv_ln_b, 'rwkv_w_o': rwkv_w_o, 'w_out': w_out, 'final_g': final_g}


def reference(x, c, ctx, c_ctx, w_mod, b_mod, norm_g, ffn_w_gate, ffn_w_up, ffn_w_down, w_in,
              s5_A_re, s5_A_im, s5_log_dt, s5_B_re, s5_B_im, s5_C_re, s5_C_im, s5_D, s5_w_glu, s5_w_proj,
              rwkv_conv, rwkv_w0, rwkv_w2, rwkv_a0, rwkv_a2, rwkv_g2, rwkv_k_k, rwkv_k_a, rwkv_r_k,
              rwkv_ln_g, rwkv_ln_b, rwkv_w_o, w_out, final_g):
    b = x.shape[0]
    rows = x.shape[1] // GRID_W
    for l in range(DEPTH):
        need_ctx = l < DEPTH - 1
        mod_x = (jax.nn.silu(c) @ w_mod[l] + b_mod[l]).reshape(b, N_MOD, D_MODEL).transpose(1, 0, 2)[:, :, None, :]
        mod_c = (jax.nn.silu(c_ctx) @ w_mod[l] + b_mod[l]).reshape(N_MOD, 1, 1, D_MODEL)
        p = {'w_in': w_in[l], 's5_A_re': s5_A_re[l], 's5_A_im': s5_A_im[l], 's5_log_dt': s5_log_dt[l],
             's5_B_re': s5_B_re[l], 's5_B_im': s5_B_im[l], 's5_C_re': s5_C_re[l], 's5_C_im': s5_C_im[l],
             's5_D': s5_D[l], 's5_w_glu': s5_w_glu[l], 's5_w_proj': s5_w_proj[l], 'rwkv_conv': rwkv_conv[l],
             'rwkv_w0': rwkv_w0[l], 'rwkv_w2': rwkv_w2[l], 'rwkv_a0': rwkv_a0[l], 'rwkv_a2': rwkv_a2[l],
             'rwkv_g2': rwkv_g2[l], 'rwkv_k_k': rwkv_k_k[l], 'rwkv_k_a': rwkv_k_a[l], 'rwkv_r_k': rwkv_r_k[l],
             'rwkv_ln_g': rwkv_ln_g[l], 'rwkv_ln_b': rwkv_ln_b[l], 'rwkv_w_o': rwkv_w_o[l], 'w_out': w_out[l]}
        x = half_ffn(x, mod_x, 0, norm_g[l, 0], ffn_w_gate[l, 0], ffn_w_up[l, 0], ffn_w_down[l, 0])
        ctx = half_ffn(ctx, mod_c, 0, norm_g[l, 0], ffn_w_gate[l, 0], ffn_w_up[l, 0], ffn_w_down[l, 0])
        hc = modulate(ctx, norm_g[l, 1], mod_c[3], mod_c[4])
        hx = modulate(x, norm_g[l, 1], mod_x[3], mod_x[4])
        out_c, ctx_states = token_mixers(hc, 1, zero_states(ctx.shape[0]), p, need_ctx)
        out_x, _ = token_mixers(hx, rows, ctx_states, p, True)
        x = x + mod_x[5] * out_x
        x = half_ffn(x, mod_x, 2, norm_g[l, 2], ffn_w_gate[l, 1], ffn_w_up[l, 1], ffn_w_down[l, 1])
        if need_ctx:
            ctx = ctx + mod_c[5] * out_c
            ctx = half_ffn(ctx, mod_c, 2, norm_g[l, 2], ffn_w_gate[l, 1], ffn_w_up[l, 1], ffn_w_down[l, 1])
    return rmsnorm(x, final_g)
```

```python
import math
import numpy as np
from contextlib import ExitStack
import concourse.bass as bass
import concourse.mybir as mybir
from concourse.bass_utils import run_bass_kernel_spmd

F32 = mybir.dt.float32
F32R = mybir.dt.float32r
BF16 = mybir.dt.bfloat16
AF = mybir.ActivationFunctionType
ALU = mybir.AluOpType
AX = mybir.AxisListType

D = 1024
FF = 2816
NFT = FF // 128
KT = D // 128
NX = 4096
NCTX = 256
NT = NX + NCTX
NTT = NT // 128
W = 512
H = 8
HD = 64
G = 32
PS = 64
EPS = 1e-6
IN_COLS = 4480


class Buf:
    __slots__ = ("name", "w", "r")

    def __init__(self, name=""):
        self.name = name
        self.w = None
        self.r = {}


class Sched:
    def __init__(self, nc, es, n_lanes=12):
        self.nc = nc
        self.eng = {"pe": nc.tensor, "act": nc.scalar, "dve": nc.vector, "pool": nc.gpsimd, "sp": nc.sync}
        self.sem = {}
        self.cnt = {}
        self.seen = {}
        for k in list(self.eng):
            self.sem[k] = es.enter_context(nc.semaphore("s_" + k))
            self.cnt[k] = 0
        self.lanes = []
        for i in range(n_lanes):
            k = "L%d" % i
            self.sem[k] = es.enter_context(nc.semaphore("s_" + k))
            self.cnt[k] = 0
            self.lanes.append(k)
        self.lane_rr = 0
        for k in self.eng:
            self.seen[k] = {}
        self.nwait = 0
        self.ninst = 0

    def _need(self, e, deps):
        best = {}
        for (src, c) in deps:
            if src == "pe" and e == "pe":
                continue
            if c > best.get(src, 0):
                best[src] = c
        for src, c in best.items():
            if self.seen[e].get(src, 0) >= c:
                continue
            self.eng[e].wait_ge(self.sem[src], c)
            self.seen[e][src] = c
            self.nwait += 1

    def _deps(self, e, reads, writes):
        deps = []
        for b in reads:
            if b.w is not None:
                deps.append(b.w)
        for b in writes:
            if b.w is not None and b.w[0] != e:
                deps.append(b.w)
            for src, c in b.r.items():
                if src != e:
                    deps.append((src, c))
        return deps

    def op(self, e, fn, reads=(), writes=()):
        self._need(e, self._deps(e, reads, writes))
        ins = fn(self.eng[e])
        self.cnt[e] += 1
        c = self.cnt[e]
        ins.then_inc(self.sem[e], 1)
        self.ninst += 1
        for b in reads:
            b.r[e] = c
        for b in writes:
            b.w = (e, c)
            b.r = {}
        return ins

    def dma(self, q, out, in_, reads=(), writes=(), slow=False):
        lane = self.lanes[self.lane_rr]
        self.lane_rr = (self.lane_rr + 1) % len(self.lanes)
        deps = self._deps(lane, reads, writes)
        if self.cnt[lane] > 0:
            deps.append((lane, self.cnt[lane]))
        self._need(q, deps)
        if slow:
            ins = self.eng[q].dma_start(out=out, in_=in_, allow_slow_non_contiguous=True)
        else:
            ins = self.eng[q].dma_start(out=out, in_=in_)
        self.cnt[lane] += 16
        c = self.cnt[lane]
        ins.then_inc(self.sem[lane], 16)
        self.ninst += 1
        for b in reads:
            b.r[lane] = c
        for b in writes:
            b.w = (lane, c)
            b.r = {}
        return ins

    def barrier(self):
        for e in self.eng:
            deps = [(k, self.cnt[k]) for k in self.cnt if k != e and self.cnt[k] > 0]
            if e != "pe" and self.cnt[e] > 0:
                deps.append((e, self.cnt[e]))
            self._need(e, deps)

    def drain_all(self, e="sp"):
        for ln in self.lanes:
            if self.cnt[ln]:
                self.eng[e].wait_ge(self.sem[ln], self.cnt[ln])
        for k in self.eng:
            if k != e and self.cnt[k]:
                self.eng[e].wait_ge(self.sem[k], self.cnt[k])


def rev(ap_):
    a = [list(x) for x in ap_.ap]
    st, n = a[-1]
    off = ap_.offset + st * (n - 1)
    a[-1] = [-st, n]
    return bass.AP(ap_.tensor, off, a)


class K:
    def __init__(self, stage=99):
        self.stage = stage
        self.nc = nc = bass.Bass("TRN2", target_bir_lowering=False)
        self.es = ExitStack()
        self.S = Sched(nc, self.es)
        self.din = {}
        self.bufs = {}
        self.rot = {}
        self.scope = None

    def inp(self, name, shape, dt=F32):
        t = self.nc.dram_tensor(name, list(shape), dt, kind="ExternalInput").ap()
        self.din[name] = t
        return t

    def scratch(self, name, shape, dt):
        t = self.nc.dram_tensor(name, list(shape), dt, kind="Internal").ap()
        self.bufs[name] = Buf(name)
        return t

    def sb(self, name, shape, dt=F32):
        es = self.scope if self.scope is not None else self.es
        self.uid = getattr(self, "uid", 0) + 1
        t = es.enter_context(self.nc.sbuf_tensor("%s_u%d" % (name, self.uid), list(shape), dt))
        self.bufs[name] = Buf(name)
        return t

    def begin_phase(self):
        assert self.scope is None
        self.scope = ExitStack()

    def end_phase(self):
        self.S.barrier()
        self.scope.close()
        self.scope = None

    def B(self, name):
        if name not in self.bufs:
            self.bufs[name] = Buf(name)
        return self.bufs[name]

    def op(self, e, fn, r=(), w=()):
        return self.S.op(e, fn, [self.B(x) if isinstance(x, str) else x for x in r],
                         [self.B(x) if isinstance(x, str) else x for x in w])

    def dma(self, out, in_, r=(), w=(), q="sp", slow=False):
        return self.S.dma(q, out, in_, [self.B(x) if isinstance(x, str) else x for x in r],
                          [self.B(x) if isinstance(x, str) else x for x in w], slow=slow)

    def pbank(self):
        i = self.rot.get("ps", 0)
        self.rot["ps"] = (i + 1) % 8
        return self.ps[i], "ps%d" % i

    def build(self):
        nc = self.nc
        x = self.inp("x", [NX, D])
        ctx = self.inp("ctx", [NCTX, D])
        cvec = self.inp("cvec", [2, D])
        w_mod = self.inp("w_mod", [D, 9 * D])
        b_mod = self.inp("b_mod", [1, 9 * D])
        norm_g = self.inp("norm_g", [3, D])
        final_g = self.inp("final_g", [1, D])
        wg = self.inp("ffn_w_gate", [2, D, FF])
        wu = self.inp("ffn_w_up", [2, D, FF])
        wd = self.inp("ffn_w_down", [2, FF, D])
        w_in = self.inp("w_in", [D, IN_COLS])
        ident = self.inp("ident", [128, 128])
        out = nc.dram_tensor("out", [NX, D], F32, kind="ExternalOutput").ap()
        self.dbg = {}

        x1s = self.scratch("x1s", [NT, D], F32)

        self.ps = [self.es.enter_context(nc.psum_tensor("ps%d" % i, [128, 512], F32)) for i in range(8)]

        idf = self.sb("idf", [128, 128], F32)
        idb = self.sb("idb", [128, 128], BF16)
        ones1 = self.sb("ones1", [1, 128], F32)
        self.dma(idf[:], ident[:, :], w=["idf"])
        self.op("dve", lambda e: e.tensor_copy(out=idb[:], in_=idf[:]), r=["idf"], w=["idb"])
        self.op("dve", lambda e: e.memset(ones1[:], 1.0), w=["ones1"])
        epsc = self.sb("epsc", [128, 1], F32)
        self.op("dve", lambda e: e.memset(epsc[:], EPS), w=["epsc"])

        NSTG = 3
        stg = [self.sb("stg%d" % i, [128, 2048], F32) for i in range(NSTG)]

        def wload(dst_ap, src_ap, dstbuf, shape3=None):
            i = self.rot.get("stg", 0)
            self.rot["stg"] = (i + 1) % NSTG
            n = 1
            for s_ in dst_ap.shape[1:]:
                n *= s_
            sv = stg[i][:, 0:n]
            if shape3 is not None:
                sv = sv.rearrange("p (a b) -> p a b", a=shape3[0])
            self.dma(sv, src_ap, w=["stg%d" % i])
            self.op("pool", lambda e: e.tensor_copy(out=dst_ap, in_=sv), r=["stg%d" % i], w=[dstbuf])

        modscr = self.scratch("modscr", [2, 9 * D], F32)
        self.begin_phase()
        cT = self.sb("cT", [128, 2, KT, 1], F32)
        for r_ in range(2):
            src = bass.AP(cvec.tensor, cvec.offset + r_ * D, [[1, 128], [128, KT], [1, 1]])
            self.dma(cT[:, r_, :, :], src, w=["cT"], slow=True)
        scT = self.sb("scT", [128, 2, KT], F32)
        self.op("act", lambda e: e.activation(out=scT[:], in_=cT[:, :, :, 0], func=AF.Silu), r=["cT"], w=["scT"])
        mrow = [self.sb("mrow%d" % i, [1, 512], F32) for i in range(4)]
        bmod = self.sb("bmod", [1, 9 * D], F32)
        self.dma(bmod[:], b_mod[:, :], w=["bmod"])
        wm_st = [self.sb("wm_st%d" % i, [128, KT, 512], F32) for i in range(2)]
        wm_v = w_mod.rearrange("(k p) n -> p k n", p=128)
        for cb in range(18):
            sl = slice(cb * 512, (cb + 1) * 512)
            st_ = wm_st[cb % 2]
            sn = "wm_st%d" % (cb % 2)
            self.dma(st_[:], wm_v[:, :, sl], w=[sn])
            for r_ in range(2):
                pt, pn = self.pbank()
                for k in range(KT):
                    self.op("pe", lambda e: e.matmul(pt[0:1, :], lhsT=scT[:, r_, k:k + 1], rhs=st_[:, k, :],
                                                     start=(k == 0), stop=(k == KT - 1)), r=["scT", sn], w=[pn])
                mi = (cb * 2 + r_) % 4
                self.op("dve", lambda e: e.tensor_tensor(out=mrow[mi][:, :], in0=pt[0:1, :], in1=bmod[:, sl], op=ALU.add),
                        r=[pn, "bmod"], w=["mrow%d" % mi])
                self.dma(modscr[r_:r_ + 1, sl], mrow[mi][:, :], r=["mrow%d" % mi], w=["modscr"])
        self.end_phase()

        NBC = 7
        bct = [self.sb("bc%d" % i, [128, D], F32) for i in range(NBC)]
        rowA = self.sb("rowA", [1, D], F32)
        rowB = self.sb("rowB", [1, D], F32)

        def bcast_row(dst_i, row_ap, rbufs):
            for hlf in range(2):
                pt, pn = self.pbank()
                self.op("pe", lambda e: e.matmul(pt[:, :], lhsT=ones1[:, :], rhs=row_ap[:, hlf * 512:(hlf + 1) * 512],
                                                 start=True, stop=True), r=["ones1"] + rbufs, w=[pn])
                self.op("act", lambda e: e.copy(out=bct[dst_i][:, hlf * 512:(hlf + 1) * 512], in_=pt[:, :]),
                        r=[pn], w=["bc%d" % dst_i])

        def bcast_dram_row(dst_i, dram_row_ap, rb):
            self.dma(rowA[:, :], dram_row_ap, r=[rb], w=["rowA"])
            bcast_row(dst_i, rowA, ["rowA"])

        def make_mod_tiles(r_, j, gi, base, mscale):
            self.dma(rowA[:, :], modscr[r_:r_ + 1, (3 * j + 1) * D:(3 * j + 2) * D], r=["modscr"], w=["rowA"])
            self.dma(rowB[:, :], norm_g[gi:gi + 1, :], w=["rowB"])
            self.op("dve", lambda e: e.scalar_tensor_tensor(out=rowA[:, :], in0=rowA[:, :], scalar=1.0, in1=rowB[:, :],
                                                            op0=ALU.add, op1=ALU.mult), r=["rowA", "rowB"], w=["rowA"])
            bcast_row(base + 0, rowA, ["rowA"])
            self.dma(rowB[:, :], modscr[r_:r_ + 1, (3 * j) * D:(3 * j + 1) * D], r=["modscr"], w=["rowB"])
            bcast_row(base + 1, rowB, ["rowB"])
            self.dma(rowA[:, :], modscr[r_:r_ + 1, (3 * j + 2) * D:(3 * j + 3) * D], r=["modscr"], w=["rowA"])
            self.op("dve", lambda e: e.tensor_scalar(out=rowA[:, :], in0=rowA[:, :], scalar1=mscale, scalar2=None, op0=ALU.mult),
                    r=["rowA"], w=["rowA"])
            bcast_row(base + 2, rowA, ["rowA"])

        self.modscr = modscr
        self.bct = bct
        self.make_mod_tiles = make_mod_tiles
        self.bcast_dram_row = bcast_dram_row
        self.idf, self.idb, self.ones1, self.epsc = idf, idb, ones1, epsc
        self.wload = wload
        self.x, self.ctx, self.out = x, ctx, out
        self.x1s = x1s
        self.wg, self.wu, self.wd, self.w_in = wg, wu, wd, w_in

        supers = [(0, 2)] + [(NCTX + i * 512, 4) for i in range(NX // 512)]
        self.supers = supers

        def src_A1(t0, n):
            if t0 < NCTX:
                return ctx[t0:t0 + n, :], None
            return x[t0 - NCTX:t0 - NCTX + n, :], None

        make_mod_tiles(1, 0, 0, 0, 0.5)
        make_mod_tiles(0, 0, 0, 3, 0.5)

        def epi_A1(xs, si, t0, tt, norm_T):
            self.dma(x1s[t0 + tt * 128:t0 + (tt + 1) * 128, :], xs[:, tt, :], r=["xs"], w=["x1s"])

        self.ffn_phase(0, src_A1, supers, lambda si: (0, 1, 2) if si == 0 else (3, 4, 5), epi_A1, "a")

        if self.stage == 1:
            self.begin_phase()
            xs = self.sb("xsd", [128, D], F32)
            for t in range(NX // 128):
                self.dma(xs[:, :], x1s[NCTX + t * 128:NCTX + (t + 1) * 128, :], r=["x1s"], w=["xsd"])
                self.dma(out[t * 128:(t + 1) * 128, :], xs[:, :], r=["xsd"], w=["out"])
            self.finish()
            return

        self.phase_A2()
        if self.stage == 21:
            self.finish()
            return
        self.phase_B1()
        if self.stage == 22:
            self.finish()
            return
        if self.stage in (3, 31):
            self.phase_S5()
            if self.stage == 31:
                self.finish()
                return
            self.dbg_copy("YAs", self.YAs, [W, NX], F32)
            self.finish()
            return
        self.phase_B2()
        if self.stage == 23:
            self.finish()
            return
        self.phase_S5()
        if self.stage == 2:
            self.dbg_copy("YTs", self.YTs.rearrange("d s c -> (d s) c"), [2 * NT, W], BF16)
            for d in range(2):
                self.dbg_copy("KTs%d" % d, self.KTs[d], [NT, W], BF16)
                self.dbg_copy("BTs%d" % d, self.BTs[d], [NT, W], BF16)
                self.dbg_copy("VTs%d" % d, self.VTs[d], [NT, W], BF16)
                self.dbg_copy("AFs%d" % d, self.AFs[d], [W, NT], F32)
                self.dbg_copy("RFs%d" % d, self.RFs[d], [W, NT], F32)
                self.dbg_copy("WFs%d" % d, self.WFs[d], [W, NT], F32)
            self.dbg_copy("BONs", self.BONs, [NT, W], BF16)
            self.dbg_copy("projT", self.projT, [2304, NT], BF16)
            self.finish()
            return
        self.phase_C1()
        if self.stage == 4:
            self.dbg_copy("x2s", self.x2s, [NX, D], F32)
            self.finish()
            return
        self.phase_C2()
        self.finish()

    def dbg_copy(self, name, src_ap, shape, dt):
        o = self.nc.dram_tensor("dbg_" + name, list(shape), dt, kind="ExternalOutput").ap()
        self.S.barrier()
        self.dma(o, src_ap, r=[name], w=["dbg_" + name])

    def norm_tiles(self, sfx):
        hb = [self.sb("hb%d%s" % (i, sfx), [128, D], BF16) for i in range(2)]
        htmp = self.sb("htmp" + sfx, [128, D], F32)
        junk = self.sb("junk" + sfx, [128, D], BF16)
        ssq = self.sb("ssq" + sfx, [128, 4], F32)
        hT = self.sb("hT" + sfx, [128, KT, 512], BF16)
        bct, idb, epsc = self.bct, self.idb, self.epsc

        def norm_T(xs_ap, xbuf, Gi, Si, col0):
            self.op("act", lambda e: e.activation(out=junk[:], in_=xs_ap, func=AF.Square, accum_out=ssq[:, 0:1]),
                    r=[xbuf], w=["junk" + sfx, "ssq" + sfx])
            self.op("act", lambda e: e.activation(out=ssq[:, 1:2], in_=ssq[:, 0:1], func=AF.Sqrt, scale=1.0 / D, bias=epsc[:, 0:1]),
                    r=["ssq" + sfx, "epsc"], w=["ssq" + sfx])
            self.op("dve", lambda e: e.reciprocal(out=ssq[:, 2:3], in_=ssq[:, 1:2]), r=["ssq" + sfx], w=["ssq" + sfx])
            i = self.rot.get("hb", 0)
            self.rot["hb"] = 1 - i
            hbn = "hb%d%s" % (i, sfx)
            if Gi is None:
                return ssq
            self.op("dve", lambda e: e.scalar_tensor_tensor(out=htmp[:], in0=xs_ap, scalar=ssq[:, 2:3], in1=bct[Gi][:],
                                                            op0=ALU.mult, op1=ALU.mult),
                    r=[xbuf, "ssq" + sfx, "bc%d" % Gi], w=["htmp" + sfx])
            self.op("pool", lambda e: e.tensor_tensor(out=hb[i][:], in0=htmp[:], in1=bct[Si][:], op=ALU.add),
                    r=["htmp" + sfx, "bc%d" % Si], w=[hbn])
            pt, pn = self.pbank()
            ptb = pt[:].bitcast(BF16)
            for k in range(KT):
                self.op("pe", lambda e: e.transpose(out=ptb[:, k * 128:(k + 1) * 128], in_=hb[i][:, k * 128:(k + 1) * 128],
                                                    identity=idb[:]), r=[hbn, "idb"], w=[pn])
            self.op("act", lambda e: e.copy(out=hT[:, :, col0:col0 + 128],
                                            in_=ptb.rearrange("p (k c) -> p k c", k=KT)), r=[pn], w=["hT" + sfx])
            return ssq

        return norm_T, hT, "hT" + sfx

    def ffn_phase(self, li, src_of, supers, tiles_of, epilogue, sfx):
        self.begin_phase()
        bct, wload = self.bct, self.wload
        wg, wu, wd = self.wg, self.wu, self.wd
        norm_T, hT, hTn = self.norm_tiles(sfx)
        xs = self.sb("xs", [128, 4, D], F32)
        actT = self.sb("actT", [128, NFT, 512], BF16)
        wd_bf = self.sb("wd_bf", [128, NFT, D], BF16)
        FB = 256
        NFB = FF // FB
        wg_bf = [self.sb("wg_bf%d" % i, [128, KT, FB], BF16) for i in range(2)]
        wu_bf = [self.sb("wu_bf%d" % i, [128, KT, FB], BF16) for i in range(2)]
        sg = [self.sb("sg%d" % i, [128, 512], F32) for i in range(2)]
        ytmp = self.sb("ytmp", [128, 512], F32)
        wdv = wd[li].rearrange("(f p) n -> p f n", p=128)
        for f in range(0, NFT, 2):
            wload(wd_bf[:, f:f + 2, :], wdv[:, f:f + 2, :], "wd_bf", shape3=(2, D))
        wgv = wg[li].rearrange("(k p) f -> p k f", p=128)
        wuv = wu[li].rearrange("(k p) f -> p k f", p=128)
        for si, (t0, ntt) in enumerate(supers):
            ntok = ntt * 128
            Gi, Si, Mi = tiles_of(si)
            src, srcbuf = src_of(t0, ntok)
            self.dma(xs[:, 0:ntt, :], src.rearrange("(t p) d -> p t d", p=128), r=[srcbuf] if srcbuf else [], w=["xs"])
            for tt in range(ntt):
                norm_T(xs[:, tt, :], "xs", Gi, Si, tt * 128)
            for fb in range(NFB):
                bi = self.rot.get("wgu", 0)
                self.rot["wgu"] = 1 - bi
                wload(wg_bf[bi][:], wgv[:, :, fb * FB:(fb + 1) * FB], "wg_bf%d" % bi, shape3=(KT, FB))
                wload(wu_bf[bi][:], wuv[:, :, fb * FB:(fb + 1) * FB], "wu_bf%d" % bi, shape3=(KT, FB))
                for fl in range(FB // 128):
                    f = fb * (FB // 128) + fl
                    pg, pgn = self.pbank()
                    pu, pun = self.pbank()
                    for k in range(KT):
                        self.op("pe", lambda e: e.matmul(pg[:, 0:ntok], lhsT=wg_bf[bi][:, k, fl * 128:(fl + 1) * 128],
                                                         rhs=hT[:, k, 0:ntok], start=(k == 0), stop=(k == KT - 1)),
                                r=["wg_bf%d" % bi, hTn], w=[pgn])
                    for k in range(KT):
                        self.op("pe", lambda e: e.matmul(pu[:, 0:ntok], lhsT=wu_bf[bi][:, k, fl * 128:(fl + 1) * 128],
                                                         rhs=hT[:, k, 0:ntok], start=(k == 0), stop=(k == KT - 1)),
                                r=["wu_bf%d" % bi, hTn], w=[pun])
                    gi_ = self.rot.get("sg", 0)
                    self.rot["sg"] = 1 - gi_
                    self.op("act", lambda e: e.activation(out=sg[gi_][:, 0:ntok], in_=pg[:, 0:ntok], func=AF.Silu),
                            r=[pgn], w=["sg%d" % gi_])
                    self.op("dve", lambda e: e.tensor_tensor(out=actT[:, f, 0:ntok], in0=sg[gi_][:, 0:ntok],
                                                             in1=pu[:, 0:ntok], op=ALU.mult),
                            r=["sg%d" % gi_, pun], w=["actT"])
            for tt in range(ntt):
                for hlf in range(2):
                    py, pyn = self.pbank()
                    for f in range(NFT):
                        self.op("pe", lambda e: e.matmul(py[:, :], lhsT=actT[:, f, tt * 128:(tt + 1) * 128],
                                                         rhs=wd_bf[:, f, hlf * 512:(hlf + 1) * 512],
                                                         start=(f == 0), stop=(f == NFT - 1)),
                                r=["actT", "wd_bf"], w=[pyn])
                    self.op("dve", lambda e: e.tensor_tensor(out=ytmp[:], in0=py[:, :], in1=bct[Mi][:, hlf * 512:(hlf + 1) * 512],
                                                             op=ALU.mult), r=[pyn, "bc%d" % Mi], w=["ytmp"])
                    self.op("pool", lambda e: e.tensor_tensor(out=xs[:, tt, hlf * 512:(hlf + 1) * 512], in0=ytmp[:],
                                                              in1=xs[:, tt, hlf * 512:(hlf + 1) * 512], op=ALU.add),
                            r=["ytmp", "xs"], w=["xs"])
                epilogue(xs, si, t0, tt, norm_T)
        self.end_phase()

    def phase_A2(self):
        nc = self.nc
        NPC = 2304
        self.projT = projT = self.scratch("projT", [NPC, NT], BF16)
        self.make_mod_tiles(1, 1, 1, 0, 1.0)
        self.make_mod_tiles(0, 1, 1, 3, 1.0)
        self.begin_phase()
        norm_T, hT, hTn = self.norm_tiles("b")
        win_bf = self.sb("win_bf", [128, KT, NPC], BF16)
        wv = self.w_in.rearrange("(k p) n -> p k n", p=128)
        for k in range(KT):
            for c0 in range(0, NPC, 1152):
                self.wload(win_bf[:, k, c0:c0 + 1152], wv[:, k, c0:c0 + 1152], "win_bf")
        xs = self.sb("xs2", [128, 4, D], F32)
        pst = [self.sb("pst%d" % i, [128, 512], BF16) for i in range(3)]
        for si, (t0, ntt) in enumerate(self.supers):
            ntok = ntt * 128
            Gi, Si = (0, 1) if si == 0 else (3, 4)
            self.dma(xs[:, 0:ntt, :], self.x1s[t0:t0 + ntok, :].rearrange("(t p) d -> p t d", p=128), r=["x1s"], w=["xs2"])
            for tt in range(ntt):
                norm_T(xs[:, tt, :], "xs2", Gi, Si, tt * 128)
            for ct in range(NPC // 128):
                pt, pn = self.pbank()
                for k in range(KT):
                    self.op("pe", lambda e: e.matmul(pt[:, 0:ntok], lhsT=win_bf[:, k, ct * 128:(ct + 1) * 128],
                                                     rhs=hT[:, k, 0:ntok], start=(k == 0), stop=(k == KT - 1)),
                            r=["win_bf", hTn], w=[pn])
                pi = self.rot.get("pst", 0)
                self.rot["pst"] = (pi + 1) % 3
                eng = "act" if ct % 2 == 0 else "dve"
                if eng == "act":
                    self.op("act", lambda e: e.copy(out=pst[pi][:, 0:ntok], in_=pt[:, 0:ntok]), r=[pn], w=["pst%d" % pi])
                else:
                    self.op("dve", lambda e: e.tensor_copy(out=pst[pi][:, 0:ntok], in_=pt[:, 0:ntok]), r=[pn], w=["pst%d" % pi])
                self.dma(projT[ct * 128:(ct + 1) * 128, t0:t0 + ntok], pst[pi][:, 0:ntok], r=["pst%d" % pi], w=["projT"])
        self.end_phase()

    def phase_B1(self):
        nc = self.nc
        projT, idb = self.projT, self.idb
        convw = self.inp("convw", [128, 12, 9])
        w0T_d = self.inp("w0T", [128, 8])
        a0T_d = self.inp("a0T", [128, 8])
        vec4_d = self.inp("vec4", [128, 5, 4])
        w2_d = self.inp("w2", [128, W])
        a2_d = self.inp("a2", [128, W])
        blk1_d = self.inp("blk1", [128, 128])
        self.AFs = [self.scratch("AFs%d" % d, [W, NT], F32) for d in range(2)]
        self.RFs = [self.scratch("RFs%d" % d, [W, NT], F32) for d in range(2)]
        self.WFs = [self.scratch("WFs%d" % d, [W, NT], F32) for d in range(2)]
        self.KTs = [self.scratch("KTs%d" % d, [NT, W], BF16) for d in range(2)]
        self.BTs = [self.scratch("BTs%d" % d, [NT, W], BF16) for d in range(2)]
        self.VTs = [self.scratch("VTs%d" % d, [NT, W], BF16) for d in range(2)]
        self.BONs = self.scratch("BONs", [NT, W], BF16)
        self.begin_phase()
        cw = self.sb("cw", [128, 12, 9], F32)
        w0T = self.sb("w0Ts", [128, 8], F32)
        a0T = self.sb("a0Ts", [128, 8], F32)
        vec4 = self.sb("vec4s", [128, 5, 4], F32)
        blk1 = self.sb("blk1s", [128, 128], F32)
        w2b = self.sb("w2b", [128, W], BF16)
        a2b = self.sb("a2b", [128, W], BF16)
        self.dma(cw[:], convw[:, :, :], w=["cw"])
        self.dma(w0T[:], w0T_d[:, :], w=["w0Ts"])
        self.dma(a0T[:], a0T_d[:, :], w=["a0Ts"])
        self.dma(vec4[:], vec4_d[:, :, :], w=["vec4s"])
        self.dma(blk1[:], blk1_d[:, :], w=["blk1s"])
        self.wload(w2b[:], w2_d[:, :], "w2b")
        self.wload(a2b[:], a2_d[:, :], "a2b")
        eps12 = self.sb("eps12", [128, 1], F32)
        self.op("dve", lambda e: e.memset(eps12[:], 1e-12), w=["eps12"])
        twd = self.sb("twd", [128, NT], BF16)
        adb = self.sb("adb", [128, NT], BF16)
        self.dma(twd[:], projT[2048:2176, :], r=["projT"], w=["twd"])
        self.dma(adb[:], projT[2176:2304, :], r=["projT"], w=["adb"])
        self.op("act", lambda e: e.activation(out=twd[:], in_=twd[:], func=AF.Tanh), r=["twd"], w=["twd"])
        raw = [self.sb("raw%d" % a, [128, NT], BF16) for a in range(3)]
        cv = [self.sb("cv%d" % a, [128, NT], F32) for a in range(3)]
        NB = 256
        tnames = ["sig", "dec0", "dec1", "icl0", "icl1", "kx", "sq", "rn", "kk", "t1", "t2", "kt0", "kt1", "rvt"]
        tf = {n: self.sb("t_" + n, [128, NB], F32) for n in tnames}
        bnames = ["ab", "b0", "b1", "k0", "k1", "vb", "rvb", "bon"]
        tb_ = {n: self.sb("tb_" + n, [128, NB], BF16) for n in bnames}
        tmo = [self.sb("tmo%d" % i, [128, 2, 128], BF16) for i in range(2)]

        def s0_of(d, t0):
            if d == 0 or t0 < NCTX:
                return t0
            return NT - t0

        def emit_fm(scr, sname, d, src_ap, srcbuf, ct, t0):
            if d == 0:
                self.dma(scr[ct * 128:(ct + 1) * 128, t0:t0 + NB], src_ap, r=[srcbuf], w=[sname])
            else:
                self.op("pool", lambda e: e.tensor_copy(out=tf["rvt"][:], in_=rev(src_ap)), r=[srcbuf], w=["t_rvt"])
                s0 = s0_of(1, t0)
                self.dma(scr[ct * 128:(ct + 1) * 128, s0:s0 + NB], tf["rvt"][:], r=["t_rvt"], w=[sname])

        def emit_tm(scr, sname, d, src_t, srcbuf, ct, t0):
            src = src_t
            sb_ = srcbuf
            if d == 1:
                self.op("pool", lambda e: e.tensor_copy(out=tb_["rvb"][:], in_=rev(src_t[:, :])), r=[srcbuf], w=["tb_rvb"])
                src = tb_["rvb"]
                sb_ = "tb_rvb"
            pt, pn = self.pbank()
            ptb = pt[:].bitcast(BF16)
            for j in range(2):
                self.op("pe", lambda e: e.transpose(out=ptb[:, j * 128:(j + 1) * 128], in_=src[:, j * 128:(j + 1) * 128],
                                                    identity=idb[:]), r=[sb_, "idb"], w=[pn])
            oi = self.rot.get("tmo", 0)
            self.rot["tmo"] = 1 - oi
            self.op("act", lambda e: e.copy(out=tmo[oi][:], in_=ptb[:, 0:256].rearrange("p (j c) -> p j c", j=2)),
                    r=[pn], w=["tmo%d" % oi])
            s0 = s0_of(d, t0)
            self.dma(scr[s0:s0 + NB, ct * 128:(ct + 1) * 128].rearrange("(j p) c -> p j c", p=128), tmo[oi][:],
                     r=["tmo%d" % oi], w=[sname])

        def conv(dst, dn, src, sn, cwi):
            self.op("dve", lambda e: e.tensor_scalar(out=dst[:, :], in0=src[:, :], scalar1=cw[:, cwi, 4:5], scalar2=None,
                                                     op0=ALU.mult), r=[sn, "cw"], w=[dn])
            for tap, sh in ((3, -1), (5, 1)):
                if sh == -1:
                    o, i_ = dst[:, 1:NCTX], src[:, 0:NCTX - 1]
                else:
                    o, i_ = dst[:, 0:NCTX - 1], src[:, 1:NCTX]
                self.op("dve", lambda e: e.scalar_tensor_tensor(out=o, in0=i_, scalar=cw[:, cwi, tap:tap + 1], in1=o,
                                                                op0=ALU.mult, op1=ALU.add), r=[sn, dn, "cw"], w=[dn])
            gd = dst[:, NCTX:NT].rearrange("p (r c) -> p r c", c=64)
            gs = src[:, NCTX:NT].rearrange("p (r c) -> p r c", c=64)
            for dy in range(3):
                for dx in range(3):
                    if dy == 1 and dx == 1:
                        continue
                    oy, ox = dy - 1, dx - 1
                    r0, r1 = max(0, -oy), 64 - max(0, oy)
                    c0, c1 = max(0, -ox), 64 - max(0, ox)
                    o = gd[:, r0:r1, c0:c1]
                    i_ = gs[:, r0 + oy:r1 + oy, c0 + ox:c1 + ox]
                    tap = dy * 3 + dx
                    self.op("dve", lambda e: e.scalar_tensor_tensor(out=o, in0=i_, scalar=cw[:, cwi, tap:tap + 1], in1=o,
                                                                    op0=ALU.mult, op1=ALU.add), r=[sn, dn, "cw"], w=[dn])

        def mm_lora(wb, wbn, src, srcn, d, ct, sl):
            pt, pn = self.pbank()
            self.op("pe", lambda e: e.matmul(pt[:, 0:NB], lhsT=wb[d * 64:(d + 1) * 64, ct * 128:(ct + 1) * 128],
                                             rhs=src[d * 64:(d + 1) * 64, sl], start=True, stop=True), r=[wbn, srcn], w=[pn])
            return pt, pn

        def headsum(src_t, srcn):
            pt, pn = self.pbank()
            self.op("pe", lambda e: e.matmul(pt[:, 0:NB], lhsT=blk1[:, :], rhs=src_t[:, :], start=True, stop=True),
                    r=["blk1s", srcn], w=[pn])
            return pt, pn

        T = lambda n: tf[n]
        for ct in range(4):
            for a in range(3):
                self.dma(raw[a][:], projT[512 + a * 512 + ct * 128:512 + a * 512 + (ct + 1) * 128, :], r=["projT"], w=["raw%d" % a])
                conv(cv[a], "cv%d" % a, raw[a], "raw%d" % a, a * 4 + ct)
            rc, kc, vc = cv
            for bi in range(NT // NB):
                t0 = bi * NB
                sl = slice(t0, t0 + NB)
                for d in range(2):
                    pt, pn = mm_lora(w2b, "w2b", twd, "twd", d, ct, sl)
                    self.op("act", lambda e: e.activation(out=T("sig")[:], in_=pt[:, 0:NB], func=AF.Sigmoid,
                                                          bias=w0T[:, d * 4 + ct:d * 4 + ct + 1]), r=[pn, "w0Ts"], w=["t_sig"])
                    dn = "dec%d" % d
                    self.op("act", lambda e: e.activation(out=T(dn)[:], in_=T("sig")[:], func=AF.Exp, scale=-math.exp(-0.5)),
                            r=["t_sig"], w=["t_" + dn])
                    emit_fm(self.WFs[d], "WFs%d" % d, d, T(dn)[:], "t_" + dn, ct, t0)
                    pt, pn = mm_lora(a2b, "a2b", adb, "adb", d, ct, sl)
                    inm = "icl%d" % d
                    self.op("act", lambda e: e.activation(out=T(inm)[:], in_=pt[:, 0:NB], func=AF.Sigmoid,
                                                          bias=a0T[:, d * 4 + ct:d * 4 + ct + 1]), r=[pn, "a0Ts"], w=["t_" + inm])
                self.op("dve", lambda e: e.tensor_scalar(out=T("kx")[:], in0=kc[:, sl], scalar1=vec4[:, 0, ct:ct + 1], scalar2=None,
                                                         op0=ALU.mult), r=["cv1", "vec4s"], w=["t_kx"])
                self.op("pool", lambda e: e.tensor_tensor(out=T("sq")[:], in0=T("kx")[:], in1=T("kx")[:], op=ALU.mult),
                        r=["t_kx"], w=["t_sq"])
                pt, pn = headsum(T("sq"), "t_sq")
                self.op("act", lambda e: e.activation(out=T("rn")[:], in_=pt[:, 0:NB], func=AF.Sqrt, bias=eps12[:, 0:1]),
                        r=[pn, "eps12"], w=["t_rn"])
                self.op("dve", lambda e: e.reciprocal(out=T("rn")[:], in_=T("rn")[:]), r=["t_rn"], w=["t_rn"])
                self.op("dve", lambda e: e.tensor_tensor(out=T("kk")[:], in0=T("kx")[:], in1=T("rn")[:], op=ALU.mult),
                        r=["t_kx", "t_rn"], w=["t_kk"])
                self.op("pool", lambda e: e.tensor_scalar(out=T("t1")[:], in0=T("kk")[:], scalar1=-1.0, scalar2=None, op0=ALU.mult),
                        r=["t_kk"], w=["t_t1"])
                for d in range(2):
                    emit_fm(self.AFs[d], "AFs%d" % d, d, T("t1")[:], "t_t1", ct, t0)
                    emit_fm(self.RFs[d], "RFs%d" % d, d, rc[:, sl], "cv0", ct, t0)
                self.op("act", lambda e: e.copy(out=tb_["vb"][:], in_=vc[:, sl]), r=["cv2"], w=["tb_vb"])
                for d in range(2):
                    emit_tm(self.VTs[d], "VTs%d" % d, d, tb_["vb"], "tb_vb", ct, t0)
                for d in range(2):
                    bn, kn, ktn, inm = "b%d" % d, "k%d" % d, "kt%d" % d, "icl%d" % d
                    self.op("dve", lambda e: e.tensor_tensor(out=tb_[bn][:], in0=T("kk")[:], in1=T(inm)[:], op=ALU.mult),
                            r=["t_kk", "t_" + inm], w=["tb_" + bn])
                    emit_tm(self.BTs[d], "BTs%d" % d, d, tb_[bn], "tb_" + bn, ct, t0)
                    self.op("dve", lambda e: e.tensor_scalar(out=T("t2")[:], in0=T(inm)[:], scalar1=-1.0,
                                                             scalar2=vec4[:, 1, ct:ct + 1], op0=ALU.add, op1=ALU.mult),
                            r=["t_" + inm, "vec4s"], w=["t_t2"])
                    self.op("dve", lambda e: e.scalar_tensor_tensor(out=T(ktn)[:], in0=T("t2")[:], scalar=1.0, in1=kc[:, sl],
                                                                    op0=ALU.add, op1=ALU.mult), r=["t_t2", "cv1"], w=["t_" + ktn])
                    self.op("act", lambda e: e.copy(out=tb_[kn][:], in_=T(ktn)[:]), r=["t_" + ktn], w=["tb_" + kn])
                    emit_tm(self.KTs[d], "KTs%d" % d, d, tb_[kn], "tb_" + kn, ct, t0)
                self.op("pool", lambda e: e.tensor_tensor(out=T("t2")[:], in0=T("kt0")[:], in1=T("kt1")[:], op=ALU.add),
                        r=["t_kt0", "t_kt1"], w=["t_t2"])
                self.op("dve", lambda e: e.scalar_tensor_tensor(out=T("sq")[:], in0=rc[:, sl], scalar=vec4[:, 2, ct:ct + 1],
                                                                in1=T("t2")[:], op0=ALU.mult, op1=ALU.mult),
                        r=["cv0", "vec4s", "t_t2"], w=["t_sq"])
                pt, pn = headsum(T("sq"), "t_sq")
                self.op("dve", lambda e: e.tensor_tensor(out=tb_["bon"][:], in0=pt[:, 0:NB], in1=vc[:, sl], op=ALU.mult),
                        r=[pn, "cv2"], w=["tb_bon"])
                emit_tm(self.BONs, "BONs", 0, tb_["bon"], "tb_bon", ct, t0)
        self.end_phase()

    def phase_B2(self):
        nc = self.nc
        mask_d = self.inp("mask16", [16, W])
        self.YTs = self.scratch("YTs", [2, NT, W], BF16)
        YTs = self.YTs
        self.begin_phase()
        SBK = 128
        SBS = 8
        Tst = self.sb("Tst", [128, W], F32)
        self.op("dve", lambda e: e.memset(Tst[:], 0.0), w=["Tst"])
        mask16 = self.sb("mask16s", [16, W], F32)
        self.dma(mask16[:], mask_d[:, :], w=["mask16s"])
        mask48 = self.sb("mask48", [48, W], BF16)
        self.dma(Tst[32:48, :], mask_d[:, :], w=["Tst"])
        self.op("dve", lambda e: e.tensor_copy(out=mask48[32:48, :], in_=Tst[32:48, :]), r=["Tst"], w=["mask48"])
        self.op("dve", lambda e: e.memset(Tst[:], 0.0), r=["mask48"], w=["Tst"])
        Ap = self.sb("Ap", [128, H, SBK], F32)
        Rp = self.sb("Rp", [128, H, SBK], F32)
        Wp = self.sb("Wp", [128, H, SBK], F32)
        Ablk = self.sb("Ablk", [128, SBK, 16], BF16)
        Rblk = self.sb("Rblk", [128, SBK, 16], BF16)
        Tbf = self.sb("Tbf", [128, W], BF16)
        self.op("pool", lambda e: e.memset(Tbf[:], 0.0), w=["Tbf"])
        self.op("pool", lambda e: e.memset(Ablk[:], 0.0), w=["Ablk"])
        self.op("pool", lambda e: e.memset(Rblk[:], 0.0), w=["Rblk"])
        KB = [self.sb("KB%d" % i, [64, SBS, 128], BF16) for i in range(2)]
        UV = [self.sb("UV%d" % i, [64, SBS, W], BF16) for i in range(2)]
        Vr = [self.sb("Vr%d" % i, [48, SBS, W], BF16) for i in range(2)]
        Yf = [self.sb("Yf%d" % i, [16, SBS, W], BF16) for i in range(2)]
        for i in range(2):
            self.op("pool", lambda e: e.memset(KB[i][:], 0.0), w=["KB%d" % i])
            self.op("pool", lambda e: e.memset(UV[i][:], 0.0), w=["UV%d" % i])
        nsteps = NT if self.stage != 23 else 320
        for s in range(nsteps):
            if s % SBK == 0:
                s0 = s
                for d in range(2):
                    for (tile_, tn, scr, sn) in ((Ap, "Ap", self.AFs[d], "AFs%d" % d), (Rp, "Rp", self.RFs[d], "RFs%d" % d),
                                                 (Wp, "Wp", self.WFs[d], "WFs%d" % d)):
                        self.dma(tile_[d * 64:(d + 1) * 64, :, :],
                                 scr.rearrange("(h j) s -> j h s", j=64)[:, :, s0:s0 + SBK], r=[sn], w=[tn])
                    self.op("pool", lambda e: e.tensor_copy(out=Ablk[d * 64:(d + 1) * 64, :, d * 8:(d + 1) * 8],
                                                            in_=Ap[d * 64:(d + 1) * 64, :, :].rearrange("p h s -> p s h")),
                            r=["Ap"], w=["Ablk"])
                    self.op("pool", lambda e: e.tensor_copy(out=Rblk[d * 64:(d + 1) * 64, :, d * 8:(d + 1) * 8],
                                                            in_=Rp[d * 64:(d + 1) * 64, :, :].rearrange("p h s -> p s h")),
                            r=["Rp"], w=["Rblk"])
            if s % SBS == 0:
                sb_i = (s // SBS) % 2
                s1 = s
                for d in range(2):
                    self.dma(KB[sb_i][32 + d * 8:32 + d * 8 + 8, :, d * 64:(d + 1) * 64],
                             self.KTs[d][s1:s1 + SBS, :].rearrange("s (h j) -> h s j", j=64), r=["KTs%d" % d], w=["KB%d" % sb_i])
                    self.dma(KB[sb_i][d * 8:d * 8 + 8, :, d * 64:(d + 1) * 64],
                             self.BTs[d][s1:s1 + SBS, :].rearrange("s (h j) -> h s j", j=64), r=["BTs%d" % d], w=["KB%d" % sb_i])
                    vsrc = self.VTs[d]
                    src = bass.AP(vsrc.tensor, vsrc.offset + s1 * W, [[0, 8], [W, SBS], [1, W]])
                    self.dma(Vr[sb_i][32 + d * 8:32 + d * 8 + 8, :, :], src, r=["VTs%d" % d], w=["Vr%d" % sb_i])
                self.op("pool", lambda e: e.tensor_tensor(
                    out=UV[sb_i][32:48, :, :], in0=Vr[sb_i][32:48, :, :],
                    in1=bass.AP(mask48[32:48, :].tensor, mask48[32:48, :].offset, [list(mask48[32:48, :].ap[0]), [0, SBS], [1, W]]),
                    op=ALU.mult), r=["Vr%d" % sb_i, "mask48"], w=["UV%d" % sb_i])
            sl = s % SBK
            ss = s % SBS
            sb_i = (s // SBS) % 2
            par = s % 2
            pu, pun = self.ps[par], "ps%d" % par
            pd, pdn = self.ps[2 + par], "ps%d" % (2 + par)
            py, pyn = self.ps[4 + par], "ps%d" % (4 + par)
            self.op("pe", lambda e: e.matmul(pu[0:16, :], lhsT=Ablk[:, sl, :], rhs=Tbf[:, :],
                                             start=True, stop=True), r=["Ablk", "Tbf"], w=[pun])
            self.op("dve", lambda e: e.tensor_tensor(out=UV[sb_i][0:16, ss, :], in0=pu[0:16, :], in1=mask16[:, :], op=ALU.mult),
                    r=[pun, "mask16s"], w=["UV%d" % sb_i])
            self.op("pe", lambda e: e.matmul(pd[:, :], lhsT=KB[sb_i][0:64, ss, :], rhs=UV[sb_i][0:64, ss, :],
                                             start=True, stop=True), r=["KB%d" % sb_i, "UV%d" % sb_i], w=[pdn])
            for h in range(H):
                self.op("dve", lambda e: e.scalar_tensor_tensor(out=Tst[:, h * 64:(h + 1) * 64], in0=Tst[:, h * 64:(h + 1) * 64],
                                                                scalar=Wp[:, h, sl:sl + 1], in1=pd[:, h * 64:(h + 1) * 64],
                                                                op0=ALU.mult, op1=ALU.add), r=["Tst", "Wp", pdn], w=["Tst"])
            self.op("act", lambda e: e.copy(out=Tbf[:, :], in_=Tst[:, :]), r=["Tst"], w=["Tbf"])
            if s >= NCTX:
                self.op("pe", lambda e: e.matmul(py[0:16, :], lhsT=Rblk[:, sl, :], rhs=Tbf[:, :],
                                                 start=True, stop=True), r=["Rblk", "Tbf"], w=[pyn])
                self.op("act", lambda e: e.copy(out=Yf[sb_i][:, ss, :], in_=py[0:16, :]), r=[pyn], w=["Yf%d" % sb_i])
                if ss == SBS - 1:
                    s1 = s - ss
                    for h in range(H):
                        src = Yf[sb_i][h:16:8, :, h * 64:(h + 1) * 64]
                        dst = YTs[:, s1:s1 + SBS, h * 64:(h + 1) * 64]
                        self.dma(dst, src, r=["Yf%d" % sb_i], w=["YTs"])
        self.end_phase()

    def phase_S5(self):
        nc = self.nc
        idf, idb = self.idf, self.idb
        s5p_d = self.inp("s5p", [128, 5, 512])
        s5s_d = self.inp("s5s", [128, 3, 64])
        ctn_d = self.inp("ctn", [128, 64, 16])
        s5D_d = self.inp("s5D", [128, 4])
        jsw_d = self.inp("jsw", [128, 128])
        gmask_d = self.inp("gmask", [128, 8])
        self.YAs = YAs = self.scratch("YAs", [W, NX], F32)
        self.begin_phase()
        TWO_PI = 2.0 * math.pi
        jsw = self.sb("jsws", [128, 128], F32)
        gmask = self.sb("gmasks", [128, 8], F32)
        s5D = self.sb("s5Ds", [128, 4], F32)
        self.dma(jsw[:], jsw_d[:, :], w=["jsws"])
        self.dma(gmask[:], gmask_d[:, :], w=["gmasks"])
        self.dma(s5D[:], s5D_d[:, :], w=["s5Ds"])
        pb = self.sb("s5pb", [128, 5, 512], F32)
        psm = self.sb("s5ps", [128, 3, 64], F32)
        self.dma(pb[:], s5p_d[:, :, :], w=["s5pb"])
        self.dma(psm[:], s5s_d[:, :, :], w=["s5ps"])
        tmp = [self.sb("s5t%d" % i, [128, 512], F32) for i in range(8)]
        tmi = self.sb("s5ti", [128, 512], mybir.dt.int32)

        def dv(fn, r, w):
            self.op("dve", fn, r=r, w=w)

        def frac_sin(dst, dn, t_ap, tn, n, tA, tAn, tB, tBn, add):
            dv(lambda e: e.tensor_scalar(out=tA[:, 0:n], in0=t_ap, scalar1=add, scalar2=None, op0=ALU.add), [tn], [tAn])
            dv(lambda e: e.tensor_copy(out=tmi[:, 0:n], in_=tA[:, 0:n]), [tAn], ["s5ti"])
            dv(lambda e: e.tensor_copy(out=tB[:, 0:n], in_=tmi[:, 0:n]), ["s5ti"], [tBn])
            dv(lambda e: e.tensor_tensor(out=tA[:, 0:n], in0=tA[:, 0:n], in1=tB[:, 0:n], op=ALU.subtract), [tAn, tBn], [tAn])
            dv(lambda e: e.tensor_scalar(out=tB[:, 0:n], in0=tA[:, 0:n], scalar1=0.5, scalar2=None, op0=ALU.is_gt), [tAn], [tBn])
            dv(lambda e: e.tensor_tensor(out=tA[:, 0:n], in0=tA[:, 0:n], in1=tB[:, 0:n], op=ALU.subtract), [tAn, tBn], [tAn])
            dv(lambda e: e.tensor_scalar(out=tB[:, 0:n], in0=tA[:, 0:n], scalar1=-0.5, scalar2=None, op0=ALU.is_lt), [tAn], [tBn])
            dv(lambda e: e.tensor_tensor(out=tA[:, 0:n], in0=tA[:, 0:n], in1=tB[:, 0:n], op=ALU.add), [tAn, tBn], [tAn])
            self.op("act", lambda e: e.activation(out=dst, in_=tA[:, 0:n], func=AF.Sin, scale=TWO_PI), r=[tAn], w=[dn])

        def abar(lre, lim, ldt, srcn, n, ar, arn, ai, ain):
            t0_, t1_, t2_, t3_ = tmp[0], tmp[1], tmp[2], tmp[3]
            self.op("act", lambda e: e.activation(out=t0_[:, 0:n], in_=ldt, func=AF.Exp), r=[srcn], w=["s5t0"])
            dv(lambda e: e.tensor_tensor(out=t1_[:, 0:n], in0=t0_[:, 0:n], in1=lre, op=ALU.mult), ["s5t0", srcn], ["s5t1"])
            self.op("act", lambda e: e.activation(out=t1_[:, 0:n], in_=t1_[:, 0:n], func=AF.Exp), r=["s5t1"], w=["s5t1"])
            dv(lambda e: e.scalar_tensor_tensor(out=t0_[:, 0:n], in0=t0_[:, 0:n], scalar=1.0 / TWO_PI, in1=lim,
                                                op0=ALU.mult, op1=ALU.mult), ["s5t0", srcn], ["s5t0"])
            frac_sin(ai, ain, t0_[:, 0:n], "s5t0", n, t2_, "s5t2", t3_, "s5t3", 0.0)
            frac_sin(ar, arn, t0_[:, 0:n], "s5t0", n, t2_, "s5t2", t3_, "s5t3", 0.25)
            dv(lambda e: e.tensor_tensor(out=ai, in0=ai, in1=t1_[:, 0:n], op=ALU.mult), [ain, "s5t1"], [ain])
            dv(lambda e: e.tensor_tensor(out=ar, in0=ar, in1=t1_[:, 0:n], op=ALU.mult), [arn, "s5t1"], [arn])

        arc = self.sb("arc", [128, 64], F32)
        ais = self.sb("ais", [128, 64], F32)
        abar(psm[:, 0, :], psm[:, 1, :], psm[:, 2, :], "s5ps", 64, arc[:, :], "arc", ais[:, :], "ais")
        dv(lambda e: e.tensor_scalar(out=ais[64:128, :], in0=ais[64:128, :], scalar1=-1.0, scalar2=None, op0=ALU.mult), ["ais"], ["ais"])
        arB = self.sb("arB", [128, 512], F32)
        aiB = self.sb("aiB", [128, 512], F32)
        abar(pb[:, 0, :], pb[:, 1, :], pb[:, 2, :], "s5pb", 512, arB[:, :], "arB", aiB[:, :], "aiB")
        lre, lim, bre, bim = pb[:, 0, :], pb[:, 1, :], pb[:, 3, :], pb[:, 4, :]
        den, zre, zim, t6 = tmp[4], tmp[5], tmp[6], tmp[7]
        dv(lambda e: e.tensor_tensor(out=den[:], in0=lre, in1=lre, op=ALU.mult), ["s5pb"], ["s5t4"])
        dv(lambda e: e.tensor_tensor(out=t6[:], in0=lim, in1=lim, op=ALU.mult), ["s5pb"], ["s5t7"])
        dv(lambda e: e.tensor_tensor(out=den[:], in0=den[:], in1=t6[:], op=ALU.add), ["s5t4", "s5t7"], ["s5t4"])
        dv(lambda e: e.reciprocal(out=den[:], in_=den[:]), ["s5t4"], ["s5t4"])
        dv(lambda e: e.tensor_scalar(out=arB[:], in0=arB[:], scalar1=-1.0, scalar2=None, op0=ALU.add), ["arB"], ["arB"])
        dv(lambda e: e.tensor_tensor(out=zre[:], in0=arB[:], in1=lre, op=ALU.mult), ["arB", "s5pb"], ["s5t5"])
        dv(lambda e: e.tensor_tensor(out=t6[:], in0=aiB[:], in1=lim, op=ALU.mult), ["aiB", "s5pb"], ["s5t7"])
        dv(lambda e: e.tensor_tensor(out=zre[:], in0=zre[:], in1=t6[:], op=ALU.add), ["s5t5", "s5t7"], ["s5t5"])
        dv(lambda e: e.tensor_tensor(out=zre[:], in0=zre[:], in1=den[:], op=ALU.mult), ["s5t5", "s5t4"], ["s5t5"])
        dv(lambda e: e.tensor_tensor(out=zim[:], in0=aiB[:], in1=lre, op=ALU.mult), ["aiB", "s5pb"], ["s5t6"])
        dv(lambda e: e.tensor_tensor(out=t6[:], in0=arB[:], in1=lim, op=ALU.mult), ["arB", "s5pb"], ["s5t7"])
        dv(lambda e: e.tensor_tensor(out=zim[:], in0=zim[:], in1=t6[:], op=ALU.subtract), ["s5t6", "s5t7"], ["s5t6"])
        dv(lambda e: e.tensor_tensor(out=zim[:], in0=zim[:], in1=den[:], op=ALU.mult), ["s5t6", "s5t4"], ["s5t6"])
        BT = self.sb("BTcat", [128, 8, 128], F32)
        z3 = lambda t: t[:].rearrange("p (a b) -> p a b", a=8)
        p3 = lambda ap_: ap_.rearrange("p (a b) -> p a b", a=8)
        t0_, t1_ = tmp[0], tmp[1]
        dv(lambda e: e.tensor_tensor(out=t0_[:], in0=zre[:], in1=bre, op=ALU.mult), ["s5t5", "s5pb"], ["s5t0"])
        dv(lambda e: e.tensor_tensor(out=t1_[:], in0=zim[:], in1=bim, op=ALU.mult), ["s5t6", "s5pb"], ["s5t1"])
        dv(lambda e: e.tensor_tensor(out=BT[:, :, 0:64], in0=z3(t0_), in1=z3(t1_), op=ALU.subtract), ["s5t0", "s5t1"], ["BTcat"])
        dv(lambda e: e.tensor_tensor(out=t0_[:], in0=zre[:], in1=bim, op=ALU.mult), ["s5t5", "s5pb", "BTcat"], ["s5t0"])
        dv(lambda e: e.tensor_tensor(out=t1_[:], in0=zim[:], in1=bre, op=ALU.mult), ["s5t6", "s5pb", "BTcat"], ["s5t1"])
        dv(lambda e: e.tensor_tensor(out=BT[:, :, 64:128], in0=z3(t0_), in1=z3(t1_), op=ALU.add), ["s5t0", "s5t1"], ["BTcat"])
        CT = self.sb("CTs", [128, 64, 16], F32)
        self.dma(CT[:], ctn_d[:, :, :], w=["CTs"])
        dv(lambda e: e.tensor_scalar(out=CT[64:128, :, :], in0=CT[64:128, :, :], scalar1=-1.0, scalar2=None, op0=ALU.mult), ["CTs"], ["CTs"])
        CTpad = [self.sb("CTpad%d" % i, [128, 128], BF16) for i in range(8)]
        for i in range(8):
            self.op("pool", lambda e: e.memset(CTpad[i][:], 0.0), w=["CTpad%d" % i])
        if self.stage == 31:
            self.end_phase()
            return
        uT = self.sb("uT", [128, 4, NT], BF16)
        for gt in range(4):
            self.dma(uT[:, gt, :], self.projT[gt * 128:(gt + 1) * 128, :], r=["projT"], w=["uT"])
        Sp = [self.sb("S5S%d" % i, [128, NT], BF16) for i in range(2)]
        yacc = self.sb("yacc", [128, NX], F32)
        lhsB = self.sb("lhsB", [128, 128], BF16)
        Qf = [self.sb("Qf%d" % i, [128, 128], F32) for i in range(2)]
        Pf = [self.sb("Pf%d" % i, [128, 128], F32) for i in range(2)]
        Qb = self.sb("Qb", [128, 13, 128], BF16)
        import os
        NLEV = 13
        LEVRUN = int(os.environ.get('S5_LEV', '13'))
        MAXIT = int(os.environ.get('S5_MAXIT', '64'))
        SKIPSQ = int(os.environ.get('S5_SKIPSQ', '0'))
        itc = [0]
        blocks = [(c0, min(c0 + 512, NT)) for c0 in range(0, NT, 512)]
        evi = [0]

        def evac(dst_ap, dn, src_ap, sn):
            evi[0] ^= 1
            if evi[0]:
                self.op("act", lambda e: e.copy(out=dst_ap, in_=src_ap), r=[sn], w=[dn])
            else:
                self.op("dve", lambda e: e.tensor_copy(out=dst_ap, in_=src_ap), r=[sn], w=[dn])

        for gt in range(4):
            dv(lambda e: e.tensor_scalar(out=yacc[:, :], in0=uT[:, gt, NCTX:NT], scalar1=s5D[:, gt:gt + 1], scalar2=None, op0=ALU.mult),
               ["uT", "s5Ds"], ["yacc"])
            for d in range(2):
                for gl in range(8):
                    g = gt * 8 + gl
                    dg = d * 32 + g
                    itc[0] += 1
                    if itc[0] > MAXIT:
                        continue
                    dgt = d * 4 + gt
                    dv(lambda e: e.tensor_scalar(out=lhsB[:, :], in0=BT[:, dgt, :], scalar1=gmask[:, gl:gl + 1], scalar2=None, op0=ALU.mult),
                       ["BTcat", "gmasks"], ["lhsB"])
                    self.op("pool", lambda e: e.tensor_copy(out=CTpad[gl][:, gl * 16:(gl + 1) * 16], in_=CT[:, dg, :]),
                            r=["CTs"], w=["CTpad%d" % gl])
                    tj = tmp[2]
                    dv(lambda e: e.tensor_scalar(out=tj[:, 0:128], in0=jsw[:, :], scalar1=ais[:, dg:dg + 1], scalar2=None, op0=ALU.mult),
                       ["jsws", "ais"], ["s5t2"])
                    dv(lambda e: e.scalar_tensor_tensor(out=Qf[0][:, :], in0=idf[:, :], scalar=arc[:, dg:dg + 1], in1=tj[:, 0:128],
                                                        op0=ALU.mult, op1=ALU.add), ["idf", "arc", "s5t2"], ["Qf0"])
                    dv(lambda e: e.scalar_tensor_tensor(out=Pf[0][:, :], in0=idf[:, :], scalar=arc[:, dg:dg + 1], in1=tj[:, 0:128],
                                                        op0=ALU.mult, op1=ALU.subtract), ["idf", "arc", "s5t2"], ["Pf0"])
                    self.op("act", lambda e: e.copy(out=Qb[:, 0, :], in_=Qf[0][:, :]), r=["Qf0"], w=["Qb"])
                    for m in range(1, NLEV if not SKIPSQ else 1):
                        a_, b_ = (m - 1) % 2, m % 2
                        pq, pqn = self.pbank()
                        pq2, pqn2 = self.pbank()
                        self.op("pe", lambda e: e.matmul(pq[:, 0:128], lhsT=Pf[a_][:, :], rhs=Qf[a_][:, :], start=True, stop=True),
                                r=["Pf%d" % a_, "Qf%d" % a_], w=[pqn])
                        self.op("pe", lambda e: e.matmul(pq2[:, 0:128], lhsT=Qf[a_][:, :], rhs=Pf[a_][:, :], start=True, stop=True),
                                r=["Pf%d" % a_, "Qf%d" % a_], w=[pqn2])
                        self.op("act", lambda e: e.copy(out=Qf[b_][:, :], in_=pq[:, 0:128]), r=[pqn], w=["Qf%d" % b_])
                        self.op("dve", lambda e: e.tensor_copy(out=Pf[b_][:, :], in_=pq2[:, 0:128]), r=[pqn2], w=["Pf%d" % b_])
                        self.op("dve", lambda e: e.tensor_copy(out=Qb[:, m, :], in_=pq[:, 0:128]), r=[pqn], w=["Qb"])
                    cur = 0
                    for (c0, c1) in [(0, NCTX)] + [(NCTX + i * 512, NCTX + (i + 1) * 512) for i in range(8)]:
                        n = c1 - c0
                        if d == 0:
                            o0 = c0
                        else:
                            o0 = NX + c0 if c0 < NCTX else c0 - NCTX
                        px, pxn = self.pbank()
                        self.op("pe", lambda e: e.matmul(px[:, 0:n], lhsT=lhsB[:, :], rhs=uT[:, gt, c0:c1], start=True, stop=True),
                                r=["lhsB", "uT"], w=[pxn])
                        evac(Sp[cur][:, o0:o0 + n], "S5S%d" % cur, px[:, 0:n], pxn)
                    for m in range(LEVRUN):
                        sh = 1 << m
                        nxt = 1 - cur
                        for (c0, c1) in blocks:
                            n = c1 - c0
                            pl, pln = self.pbank()
                            if d == 0:
                                lo = max(c0, sh)
                                has = lo < c1
                                self.op("pe", lambda e: e.matmul(pl[:, 0:n], lhsT=idb[:, :], rhs=Sp[cur][:, c0:c1], start=True, stop=not has),
                                        r=["idb", "S5S%d" % cur], w=[pln])
                                if has:
                                    self.op("pe", lambda e: e.matmul(pl[:, lo - c0:n], lhsT=Qb[:, m, :], rhs=Sp[cur][:, lo - sh:c1 - sh],
                                                                     start=False, stop=True), r=["Qb", "S5S%d" % cur], w=[pln])
                            else:
                                hi = min(c1, NT - sh)
                                has = hi > c0
                                self.op("pe", lambda e: e.matmul(pl[:, 0:n], lhsT=idb[:, :], rhs=Sp[cur][:, c0:c1], start=True, stop=not has),
                                        r=["idb", "S5S%d" % cur], w=[pln])
                                if has:
                                    self.op("pe", lambda e: e.matmul(pl[:, 0:hi - c0], lhsT=Qb[:, m, :], rhs=Sp[cur][:, c0 + sh:hi + sh],
                                                                     start=False, stop=True), r=["Qb", "S5S%d" % cur], w=[pln])
                            evac(Sp[nxt][:, c0:c1], "S5S%d" % nxt, pl[:, 0:n], pln)
                        cur = nxt
                    for tb in range(8):
                        c0 = (NCTX if d == 0 else 0) + tb * 512
                        py, pyn = self.pbank()
                        self.op("pe", lambda e: e.matmul(py[:, :], lhsT=CTpad[gl][:, :], rhs=Sp[cur][:, c0:c0 + 512], start=True, stop=True),
                                r=["CTpad%d" % gl, "S5S%d" % cur], w=[pyn])
                        dv(lambda e: e.tensor_tensor(out=yacc[:, tb * 512:(tb + 1) * 512], in0=py[:, :], in1=yacc[:, tb * 512:(tb + 1) * 512],
                                                     op=ALU.add), [pyn, "yacc"], ["yacc"])
            self.dma(YAs[gt * 128:(gt + 1) * 128, :], yacc[:, :], r=["yacc"], w=["YAs"])
        self.end_phase()

    def phase_C1(self):
        nc = self.nc
        idb, bct = self.idb, self.bct
        wglu_d = self.inp("s5_w_glu", [W, W])
        wproj_d = self.inp("s5_w_proj", [W, D])
        wo_d = self.inp("rwkv_w_o", [W, D])
        wout_d = self.inp("w_out", [D, D])
        g2_d = self.inp("rwkv_g2", [128, W])
        ln_d = self.inp("lnrows", [2, D])
        jrev_d = self.inp("jrev", [128, 128])
        self.x2s = x2s = self.scratch("x2s", [NX, D], F32)
        self.bcast_dram_row(0, ln_d[0:1, :], "ln_in")
        self.bcast_dram_row(1, ln_d[1:2, :], "ln_in")
        self.begin_phase()
        norm_T, hT, hTn = self.norm_tiles("c")
        NG = IN_COLS - 2304
        wgate = self.sb("wgate", [128, KT, NG], BF16)
        wv = self.w_in.rearrange("(k p) n -> p k n", p=128)
        for k in range(KT):
            for c0 in range(0, NG, NG // 2):
                self.wload(wgate[:, k, c0:c0 + NG // 2], wv[:, k, 2304 + c0:2304 + c0 + NG // 2], "wgate")
        wglu = self.sb("wglu", [128, 4, W], BF16)
        self.wload(wglu[:], wglu_d.rearrange("(k p) n -> p k n", p=128), "wglu", shape3=(4, W))
        wproj = self.sb("wproj", [128, 4, D], BF16)
        wo = self.sb("wo", [128, 4, D], BF16)
        for k in range(0, 4, 2):
            self.wload(wproj[:, k:k + 2, :], wproj_d.rearrange("(k p) n -> p k n", p=128)[:, k:k + 2, :], "wproj", shape3=(2, D))
            self.wload(wo[:, k:k + 2, :], wo_d.rearrange("(k p) n -> p k n", p=128)[:, k:k + 2, :], "wo", shape3=(2, D))
        wout = self.sb("wout", [128, KT, D], BF16)
        for k in range(0, KT, 2):
            self.wload(wout[:, k:k + 2, :], wout_d.rearrange("(k p) n -> p k n", p=128)[:, k:k + 2, :], "wout", shape3=(2, D))
        g2b = self.sb("g2b", [128, W], BF16)
        self.wload(g2b[:], g2_d[:, :], "g2b")
        jrf = self.sb("jrf", [128, 128], F32)
        jrb = self.sb("jrb", [128, 128], BF16)
        self.dma(jrf[:], jrev_d[:, :], w=["jrf"])
        self.op("dve", lambda e: e.tensor_copy(out=jrb[:], in_=jrf[:]), r=["jrf"], w=["jrb"])
        epsln = self.sb("epsln", [128, 1], F32)
        self.op("dve", lambda e: e.memset(epsln[:], 64e-5), w=["epsln"])
        xs1 = [self.sb("xs1_%d" % i, [128, D], F32) for i in range(2)]
        sgate = self.sb("sgate", [128, 2048], F32)
        sgdT = self.sb("sgdT", [128, 128], BF16)
        gsb = self.sb("gsb", [128, W], F32)
        ya4 = self.sb("ya4", [128, W], F32)
        gt1 = self.sb("gt1", [128, W], F32)
        guh = self.sb("guh", [128, W], F32)
        yag = self.sb("yag", [128, W], BF16)
        sig = self.sb("sigz", [128, W], F32)
        ya2T = self.sb("ya2T", [128, W], BF16)
        ma = self.sb("ma", [128, D], F32)
        yf = self.sb("yf", [128, W], BF16)
        yb = self.sb("yb", [128, W], BF16)
        bon = self.sb("bon", [128, W], BF16)
        ysum = self.sb("ysum", [128, W], F32)
        cen = self.sb("cen", [128, W], F32)
        sq = self.sb("sqc", [128, W], F32)
        st8 = self.sb("st8", [128, 4, 8], F32)
        ynb = self.sb("ynb", [128, W], BF16)
        ybT = self.sb("ybT", [128, W], BF16)
        tmpm = self.sb("tmpm", [128, W], F32)
        mg = self.sb("mg", [128, D], BF16)
        mT = self.sb("mT", [128, D], BF16)
        x2t = self.sb("x2t", [128, D], F32)
        GC = math.sqrt(2.0 / math.pi)

        def bc_last(ap2, n):
            return bass.AP(ap2.tensor, ap2.offset, [list(x_) for x_ in ap2.ap] + [[0, n]])

        v3 = lambda t: t[:, :].rearrange("p (h j) -> p h j", j=64)
        k3 = lambda t: t[:, :].rearrange("p (k c) -> p k c", c=128)
        YAv = self.YAs.rearrange("(k p) t -> p k t", p=128)
        for tt in range(NX // 128):
            xt0 = tt * 128
            t0 = NCTX + xt0
            xi = tt % 2
            xsn = "xs1_%d" % xi
            xs_ = xs1[xi]
            self.dma(xs_[:, :], self.x1s[t0:t0 + 128, :], r=["x1s"], w=[xsn])
            norm_T(xs_[:, :], xsn, 3, 4, 0)
            pt, pn = self.pbank()
            for k in range(KT):
                self.op("pe", lambda e: e.matmul(pt[:, 0:128], lhsT=wgate[:, k, 0:128], rhs=hT[:, k, 0:128],
                                                 start=(k == 0), stop=(k == KT - 1)), r=["wgate", hTn], w=[pn])
            self.op("act", lambda e: e.activation(out=sgdT[:, :], in_=pt[:, 0:128], func=AF.Sigmoid), r=[pn], w=["sgdT"])
            pt, pn = self.pbank()
            self.op("pe", lambda e: e.matmul(pt[:, :], lhsT=sgdT[:, :], rhs=g2b[:, :], start=True, stop=True), r=["sgdT", "g2b"], w=[pn])
            self.op("act", lambda e: e.copy(out=gsb[:, :], in_=pt[:, :]), r=[pn], w=["gsb"])
            for cb in range(4):
                pt, pn = self.pbank()
                for k in range(KT):
                    self.op("pe", lambda e: e.matmul(pt[:, :], lhsT=hT[:, k, 0:128], rhs=wgate[:, k, 128 + cb * 512:128 + (cb + 1) * 512],
                                                     start=(k == 0), stop=(k == KT - 1)), r=["wgate", hTn], w=[pn])
                self.op("act", lambda e: e.activation(out=sgate[:, cb * 512:(cb + 1) * 512], in_=pt[:, :], func=AF.Sigmoid),
                        r=[pn], w=["sgate"])
            self.dma(k3(ya4), YAv[:, :, xt0:xt0 + 128], r=["YAs"], w=["ya4"])
            self.op("pool", lambda e: e.tensor_tensor(out=gt1[:, :], in0=ya4[:, :], in1=ya4[:, :], op=ALU.mult), r=["ya4"], w=["gt1"])
            self.op("dve", lambda e: e.tensor_scalar(out=gt1[:, :], in0=gt1[:, :], scalar1=0.044715, scalar2=1.0, op0=ALU.mult, op1=ALU.add),
                    r=["gt1"], w=["gt1"])
            self.op("dve", lambda e: e.tensor_tensor(out=gt1[:, :], in0=gt1[:, :], in1=ya4[:, :], op=ALU.mult), r=["gt1", "ya4"], w=["gt1"])
            self.op("act", lambda e: e.activation(out=gt1[:, :], in_=gt1[:, :], func=AF.Tanh, scale=GC), r=["gt1"], w=["gt1"])
            self.op("pool", lambda e: e.tensor_scalar(out=guh[:, :], in0=ya4[:, :], scalar1=0.5, scalar2=None, op0=ALU.mult), r=["ya4"], w=["guh"])
            self.op("dve", lambda e: e.scalar_tensor_tensor(out=yag[:, :], in0=gt1[:, :], scalar=1.0, in1=guh[:, :], op0=ALU.add, op1=ALU.mult),
                    r=["gt1", "guh"], w=["yag"])
            for ct in range(4):
                pt, pn = self.pbank()
                for k in range(4):
                    self.op("pe", lambda e: e.matmul(pt[:, 0:128], lhsT=wglu[:, k, ct * 128:(ct + 1) * 128], rhs=k3(yag)[:, k, :],
                                                     start=(k == 0), stop=(k == 3)), r=["wglu", "yag"], w=[pn])
                self.op("act", lambda e: e.activation(out=sig[:, ct * 128:(ct + 1) * 128], in_=pt[:, 0:128], func=AF.Sigmoid), r=[pn], w=["sigz"])
            self.op("dve", lambda e: e.tensor_tensor(out=ya2T[:, :], in0=yag[:, :], in1=sig[:, :], op=ALU.mult), r=["yag", "sigz"], w=["ya2T"])
            for hlf in range(2):
                pt, pn = self.pbank()
                for k in range(4):
                    self.op("pe", lambda e: e.matmul(pt[:, :], lhsT=k3(ya2T)[:, k, :], rhs=wproj[:, k, hlf * 512:(hlf + 1) * 512],
                                                     start=(k == 0), stop=(k == 3)), r=["ya2T", "wproj"], w=[pn])
                self.op("dve", lambda e: e.tensor_tensor(out=ma[:, hlf * 512:(hlf + 1) * 512], in0=pt[:, :],
                                                         in1=sgate[:, hlf * 512:(hlf + 1) * 512], op=ALU.mult), r=[pn, "sgate"], w=["ma"])
            sA = NT + 128 - t0
            self.dma(yf[:, :], self.YTs[0, t0:t0 + 128, :], r=["YTs"], w=["yf"])
            self.dma(yb[:, :], self.YTs[1, sA:sA + 128, :], r=["YTs"], w=["yb"])
            self.dma(bon[:, :], self.BONs[t0:t0 + 128, :], r=["BONs"], w=["bon"])
            pt, pn = self.pbank()
            self.op("pe", lambda e: e.matmul(pt[:, :], lhsT=jrb[:, :], rhs=yb[:, :], start=True, stop=True), r=["jrb", "yb"], w=[pn])
            self.op("dve", lambda e: e.tensor_tensor(out=ysum[:, :], in0=pt[:, :], in1=yf[:, :], op=ALU.add), r=[pn, "yf"], w=["ysum"])
            self.op("dve", lambda e: e.tensor_reduce(out=st8[:, 0, :], in_=v3(ysum), axis=AX.X, op=ALU.add), r=["ysum"], w=["st8"])
            self.op("dve", lambda e: e.tensor_scalar(out=st8[:, 1, :], in0=st8[:, 0, :], scalar1=1.0 / 64, scalar2=None, op0=ALU.mult),
                    r=["st8"], w=["st8"])
            self.op("dve", lambda e: e.tensor_tensor(out=v3(cen), in0=v3(ysum), in1=bc_last(st8[:, 1, :], 64), op=ALU.subtract),
                    r=["ysum", "st8"], w=["cen"])
            self.op("pool", lambda e: e.tensor_tensor(out=sq[:, :], in0=cen[:, :], in1=cen[:, :], op=ALU.mult), r=["cen"], w=["sqc"])
            self.op("dve", lambda e: e.tensor_reduce(out=st8[:, 2, :], in_=v3(sq), axis=AX.X, op=ALU.add), r=["sqc", "st8"], w=["st8"])
            self.op("act", lambda e: e.activation(out=st8[:, 3, :], in_=st8[:, 2, :], func=AF.Sqrt, scale=1.0 / 64, bias=epsln[:, 0:1]),
                    r=["st8", "epsln"], w=["st8"])
            self.op("dve", lambda e: e.reciprocal(out=st8[:, 3, :], in_=st8[:, 3, :]), r=["st8"], w=["st8"])
            self.op("dve", lambda e: e.tensor_tensor(out=v3(cen), in0=v3(cen), in1=bc_last(st8[:, 3, :], 64), op=ALU.mult),
                    r=["cen", "st8"], w=["cen"])
            self.op("pool", lambda e: e.tensor_tensor(out=cen[:, :], in0=cen[:, :], in1=bct[0][:, 0:W], op=ALU.mult), r=["cen", "bc0"], w=["cen"])
            self.op("pool", lambda e: e.tensor_tensor(out=cen[:, :], in0=cen[:, :], in1=bct[1][:, 0:W], op=ALU.add), r=["cen", "bc1"], w=["cen"])
            self.op("dve", lambda e: e.tensor_tensor(out=cen[:, :], in0=cen[:, :], in1=bon[:, :], op=ALU.add), r=["cen", "bon"], w=["cen"])
            self.op("dve", lambda e: e.tensor_tensor(out=ynb[:, :], in0=cen[:, :], in1=gsb[:, :], op=ALU.mult), r=["cen", "gsb"], w=["ynb"])
            pt, pn = self.pbank()
            ptb = pt[:].bitcast(BF16)
            for k in range(4):
                self.op("pe", lambda e: e.transpose(out=ptb[:, k * 128:(k + 1) * 128], in_=ynb[:, k * 128:(k + 1) * 128], identity=idb[:]),
                        r=["ynb", "idb"], w=[pn])
            self.op("act", lambda e: e.copy(out=ybT[:, :], in_=ptb[:, 0:W]), r=[pn], w=["ybT"])
            for hlf in range(2):
                pt, pn = self.pbank()
                for k in range(4):
                    self.op("pe", lambda e: e.matmul(pt[:, :], lhsT=k3(ybT)[:, k, :], rhs=wo[:, k, hlf * 512:(hlf + 1) * 512],
                                                     start=(k == 0), stop=(k == 3)), r=["ybT", "wo"], w=[pn])
                self.op("dve", lambda e: e.tensor_tensor(out=tmpm[:, :], in0=pt[:, :], in1=sgate[:, 1024 + hlf * 512:1024 + (hlf + 1) * 512],
                                                         op=ALU.mult), r=[pn, "sgate"], w=["tmpm"])
                self.op("pool", lambda e: e.tensor_tensor(out=mg[:, hlf * 512:(hlf + 1) * 512], in0=tmpm[:, :],
                                                          in1=ma[:, hlf * 512:(hlf + 1) * 512], op=ALU.add), r=["tmpm", "ma"], w=["mg"])
            pt, pn = self.pbank()
            ptb = pt[:].bitcast(BF16)
            for k in range(KT):
                self.op("pe", lambda e: e.transpose(out=ptb[:, k * 128:(k + 1) * 128], in_=mg[:, k * 128:(k + 1) * 128], identity=idb[:]),
                        r=["mg", "idb"], w=[pn])
            self.op("act", lambda e: e.copy(out=mT[:, :], in_=ptb[:, :]), r=[pn], w=["mT"])
            for hlf in range(2):
                pt, pn = self.pbank()
                for k in range(KT):
                    self.op("pe", lambda e: e.matmul(pt[:, :], lhsT=k3(mT)[:, k, :], rhs=wout[:, k, hlf * 512:(hlf + 1) * 512],
                                                     start=(k == 0), stop=(k == KT - 1)), r=["mT", "wout"], w=[pn])
                self.op("dve", lambda e: e.tensor_tensor(out=tmpm[:, :], in0=pt[:, :], in1=bct[5][:, hlf * 512:(hlf + 1) * 512], op=ALU.mult),
                        r=[pn, "bc5"], w=["tmpm"])
                self.op("pool", lambda e: e.tensor_tensor(out=x2t[:, hlf * 512:(hlf + 1) * 512], in0=tmpm[:, :],
                                                          in1=xs_[:, hlf * 512:(hlf + 1) * 512], op=ALU.add), r=["tmpm", xsn], w=["x2t"])
            self.dma(x2s[xt0:xt0 + 128, :], x2t[:, :], r=["x2t"], w=["x2s"])
        self.end_phase()

    def phase_C2(self):
        out, x2s, bct = self.out, self.x2s, self.bct
        self.make_mod_tiles(0, 2, 2, 0, 0.5)
        self.bcast_dram_row(6, self.din["final_g"][:, :], "fg_in")
        supers = [(i * 512, 4) for i in range(NX // 512)]
        st = {}

        def epi(xs, si, t0, tt, norm_T):
            if "ot" not in st:
                st["ot"] = [self.sb("ot%d" % i, [128, D], F32) for i in range(2)]
            oi = self.rot.get("ot", 0)
            self.rot["ot"] = 1 - oi
            ot = st["ot"][oi]
            ssq = norm_T(xs[:, tt, :], "xs", None, None, 0)
            self.op("dve", lambda e: e.scalar_tensor_tensor(out=ot[:, :], in0=xs[:, tt, :], scalar=ssq[:, 2:3], in1=bct[6][:, :],
                                                            op0=ALU.mult, op1=ALU.mult), r=["xs", "ssqd", "bc6"], w=["ot%d" % oi])
            self.dma(out[t0 + tt * 128:t0 + (tt + 1) * 128, :], ot[:, :], r=["ot%d" % oi], w=["out"])

        self.ffn_phase(1, lambda t0, n: (x2s[t0:t0 + n, :], "x2s"), supers, lambda si: (0, 1, 2), epi, "d")

    def finish(self):
        self.S.drain_all("sp")
        print("ninst", self.S.ninst, "nwait", self.S.nwait)
        if self.scope is not None:
            self.scope.close()
            self.scope = None
        self.es.close()


def host_inputs(inputs, b):
    f = lambda a: np.ascontiguousarray(np.asarray(a, dtype=np.float32))
    m = {
        "x": f(inputs["x"][b]),
        "ctx": f(inputs["ctx"][b]),
        "cvec": f(np.stack([np.asarray(inputs["c"])[b], np.asarray(inputs["c_ctx"])], 0)),
        "w_mod": f(inputs["w_mod"][0]),
        "b_mod": f(np.asarray(inputs["b_mod"])[0][None, :]),
        "norm_g": f(inputs["norm_g"][0]),
        "final_g": f(np.asarray(inputs["final_g"])[None, :]),
        "ffn_w_gate": f(inputs["ffn_w_gate"][0]),
        "ffn_w_up": f(inputs["ffn_w_up"][0]),
        "ffn_w_down": f(inputs["ffn_w_down"][0]),
        "w_in": f(inputs["w_in"][0]),
        "ident": np.eye(128, dtype=np.float32),
    }
    cv = np.asarray(inputs["rwkv_conv"], np.float32)[0].reshape(9, 3, 4, 128)
    m["convw"] = f(cv.transpose(3, 1, 2, 0).reshape(128, 12, 9))
    m["w0T"] = f(np.asarray(inputs["rwkv_w0"], np.float32)[0].reshape(2, 4, 128).transpose(2, 0, 1).reshape(128, 8))
    m["a0T"] = f(np.asarray(inputs["rwkv_a0"], np.float32)[0].reshape(2, 4, 128).transpose(2, 0, 1).reshape(128, 8))
    v4 = np.stack([np.asarray(inputs[k], np.float32)[0].reshape(-1) for k in
                   ("rwkv_k_k", "rwkv_k_a", "rwkv_r_k", "rwkv_ln_g", "rwkv_ln_b")], 0)
    m["vec4"] = f(v4.reshape(5, 4, 128).transpose(2, 0, 1))
    m["w2"] = f(np.asarray(inputs["rwkv_w2"], np.float32)[0].reshape(128, 512))
    m["a2"] = f(np.asarray(inputs["rwkv_a2"], np.float32)[0].reshape(128, 512))
    blk = np.zeros((128, 128), np.float32); blk[:64, :64] = 1; blk[64:, 64:] = 1
    m["blk1"] = blk
    mk = np.zeros((16, 512), np.float32)
    for dh in range(16):
        h_ = dh % 8
        mk[dh, h_ * 64:(h_ + 1) * 64] = 1
    m["mask16"] = mk
    def big(a):
        a = np.asarray(a, np.float32)[0]
        if a.ndim == 3:
            a = np.broadcast_to(a[..., None], a.shape + (16,))
        a = a.reshape(2, 4, 8, 64, 16)
        return a.transpose(2, 4, 0, 1, 3).reshape(128, 512)
    ldt = np.broadcast_to(np.asarray(inputs["s5_log_dt"], np.float32)[:, :, :, None], (1, 2, 32, 64))
    m["s5p"] = f(np.stack([big(inputs["s5_A_re"]), big(inputs["s5_A_im"]), big(ldt), big(inputs["s5_B_re"]), big(inputs["s5_B_im"])], 1))
    def small(a):
        a = np.asarray(a, np.float32)[0].reshape(64, 64).T
        return np.concatenate([a, a], 0)
    m["s5s"] = f(np.stack([small(inputs["s5_A_re"]), small(inputs["s5_A_im"]), small(ldt)], 1))
    cre = np.asarray(inputs["s5_C_re"], np.float32)[0].reshape(64, 16, 64).transpose(2, 0, 1)
    cim = np.asarray(inputs["s5_C_im"], np.float32)[0].reshape(64, 16, 64).transpose(2, 0, 1)
    m["ctn"] = f(np.concatenate([cre, cim], 0))
    m["s5D"] = f(np.asarray(inputs["s5_D"], np.float32)[0].reshape(4, 128).T)
    js = np.zeros((128, 128), np.float32)
    for p_ in range(64):
        js[p_, 64 + p_] = 1; js[64 + p_, p_] = 1
    m["jsw"] = js
    gm = np.zeros((128, 8), np.float32)
    for p_ in range(128):
        gm[p_, p_ // 16] = 1
    m["gmask"] = gm
    m["s5_w_glu"] = f(inputs["s5_w_glu"][0])
    m["s5_w_proj"] = f(inputs["s5_w_proj"][0])
    m["rwkv_w_o"] = f(inputs["rwkv_w_o"][0])
    m["w_out"] = f(inputs["w_out"][0])
    m["rwkv_g2"] = f(inputs["rwkv_g2"][0])
    ln = np.zeros((2, 1024), np.float32)
    ln[0, :512] = np.asarray(inputs["rwkv_ln_g"], np.float32)[0]
    ln[1, :512] = np.asarray(inputs["rwkv_ln_b"], np.float32)[0]
    m["lnrows"] = ln
    m["jrev"] = np.ascontiguousarray(np.eye(128, dtype=np.float32)[::-1])
    return m


def run(inputs, stage=99, cores=8):
    kb = K(stage)
    kb.build()
    in_maps = []
    for b in range(cores):
        m = host_inputs(inputs, b)
        in_maps.append({k: v for k, v in m.items() if k in kb.din})
    res = run_bass_kernel_spmd(kb.nc, in_maps, core_ids=list(range(cores)))
    return res


def kernel(**inputs):
    res = run(inputs)
    outs = [np.asarray(r["out"], dtype=np.float32) for r in res.results]
    return np.stack(outs, 0)
```

```python
import math
import numpy as np
from contextlib import ExitStack
import concourse.bass as bass
import concourse.mybir as mybir
from concourse.bass_utils import run_bass_kernel_spmd

F32 = mybir.dt.float32
F32R = mybir.dt.float32r
BF16 = mybir.dt.bfloat16
AF = mybir.ActivationFunctionType
ALU = mybir.AluOpType
AX = mybir.AxisListType

D = 1024
FF = 2816
NFT = FF // 128
KT = D // 128
NX = 4096
NCTX = 256
NT = NX + NCTX
NTT = NT // 128
W = 512
H = 8
HD = 64
G = 32
PS = 64
EPS = 1e-6
IN_COLS = 4480


class Buf:
    __slots__ = ("name", "w", "r")

    def __init__(self, name=""):
        self.name = name
        self.w = None
        self.r = {}


class Sched:
    def __init__(self, nc, es, n_lanes=12):
        self.nc = nc
        self.eng = {"pe": nc.tensor, "act": nc.scalar, "dve": nc.vector, "pool": nc.gpsimd, "sp": nc.sync}
        self.sem = {}
        self.cnt = {}
        self.seen = {}
        for k in list(self.eng):
            self.sem[k] = es.enter_context(nc.semaphore("s_" + k))
            self.cnt[k] = 0
        self.lanes = []
        for i in range(n_lanes):
            k = "L%d" % i
            self.sem[k] = es.enter_context(nc.semaphore("s_" + k))
            self.cnt[k] = 0
            self.lanes.append(k)
        self.lane_rr = 0
        for k in self.eng:
            self.seen[k] = {}
        self.nwait = 0
        self.ninst = 0

    def _need(self, e, deps):
        best = {}
        for (src, c) in deps:
            if src == "pe" and e == "pe":
                continue
            if c > best.get(src, 0):
                best[src] = c
        for src, c in best.items():
            if self.seen[e].get(src, 0) >= c:
                continue
            self.eng[e].wait_ge(self.sem[src], c)
            self.seen[e][src] = c
            self.nwait += 1

    def _deps(self, e, reads, writes):
        deps = []
        for b in reads:
            if b.w is not None:
                deps.append(b.w)
        for b in writes:
            if b.w is not None and b.w[0] != e:
                deps.append(b.w)
            for src, c in b.r.items():
                if src != e:
                    deps.append((src, c))
        return deps

    def op(self, e, fn, reads=(), writes=()):
        self._need(e, self._deps(e, reads, writes))
        ins = fn(self.eng[e])
        self.cnt[e] += 1
        c = self.cnt[e]
        ins.then_inc(self.sem[e], 1)
        self.ninst += 1
        for b in reads:
            b.r[e] = c
        for b in writes:
            b.w = (e, c)
            b.r = {}
        return ins

    def dma(self, q, out, in_, reads=(), writes=(), slow=False):
        lane = self.lanes[self.lane_rr]
        self.lane_rr = (self.lane_rr + 1) % len(self.lanes)
        deps = self._deps(lane, reads, writes)
        if self.cnt[lane] > 0:
            deps.append((lane, self.cnt[lane]))
        self._need(q, deps)
        if slow:
            ins = self.eng[q].dma_start(out=out, in_=in_, allow_slow_non_contiguous=True)
        else:
            ins = self.eng[q].dma_start(out=out, in_=in_)
        self.cnt[lane] += 16
        c = self.cnt[lane]
        ins.then_inc(self.sem[lane], 16)
        self.ninst += 1
        for b in reads:
            b.r[lane] = c
        for b in writes:
            b.w = (lane, c)
            b.r = {}
        return ins

    def barrier(self):
        for e in self.eng:
            deps = [(k, self.cnt[k]) for k in self.cnt if k != e and self.cnt[k] > 0]
            if e != "pe" and self.cnt[e] > 0:
                deps.append((e, self.cnt[e]))
            self._need(e, deps)

    def drain_all(self, e="sp"):
        for ln in self.lanes:
            if self.cnt[ln]:
                self.eng[e].wait_ge(self.sem[ln], self.cnt[ln])
        for k in self.eng:
            if k != e and self.cnt[k]:
                self.eng[e].wait_ge(self.sem[k], self.cnt[k])


def rev(ap_):
    a = [list(x) for x in ap_.ap]
    st, n = a[-1]
    off = ap_.offset + st * (n - 1)
    a[-1] = [-st, n]
    return bass.AP(ap_.tensor, off, a)


class K:
    def __init__(self, stage=99):
        self.stage = stage
        self.nc = nc = bass.Bass("TRN2", target_bir_lowering=False)
        self.es = ExitStack()
        self.S = Sched(nc, self.es)
        self.din = {}
        self.bufs = {}
        self.rot = {}
        self.scope = None

    def inp(self, name, shape, dt=F32):
        t = self.nc.dram_tensor(name, list(shape), dt, kind="ExternalInput").ap()
        self.din[name] = t
        return t

    def scratch(self, name, shape, dt):
        t = self.nc.dram_tensor(name, list(shape), dt, kind="Internal").ap()
        self.bufs[name] = Buf(name)
        return t

    def sb(self, name, shape, dt=F32):
        es = self.scope if self.scope is not None else self.es
        self.uid = getattr(self, "uid", 0) + 1
        t = es.enter_context(self.nc.sbuf_tensor("%s_u%d" % (name, self.uid), list(shape), dt))
        self.bufs[name] = Buf(name)
        return t

    def begin_phase(self):
        assert self.scope is None
        self.scope = ExitStack()

    def end_phase(self):
        self.S.barrier()
        self.scope.close()
        self.scope = None

    def B(self, name):
        if name not in self.bufs:
            self.bufs[name] = Buf(name)
        return self.bufs[name]

    def op(self, e, fn, r=(), w=()):
        return self.S.op(e, fn, [self.B(x) if isinstance(x, str) else x for x in r],
                         [self.B(x) if isinstance(x, str) else x for x in w])

    def dma(self, out, in_, r=(), w=(), q="sp", slow=False):
        return self.S.dma(q, out, in_, [self.B(x) if isinstance(x, str) else x for x in r],
                          [self.B(x) if isinstance(x, str) else x for x in w], slow=slow)

    def pbank(self):
        i = self.rot.get("ps", 0)
        self.rot["ps"] = (i + 1) % 8
        return self.ps[i], "ps%d" % i

    def build(self):
        nc = self.nc
        x = self.inp("x", [NX, D])
        ctx = self.inp("ctx", [NCTX, D])
        cvec = self.inp("cvec", [2, D])
        w_mod = self.inp("w_mod", [D, 9 * D])
        b_mod = self.inp("b_mod", [1, 9 * D])
        norm_g = self.inp("norm_g", [3, D])
        final_g = self.inp("final_g", [1, D])
        wg = self.inp("ffn_w_gate", [2, D, FF])
        wu = self.inp("ffn_w_up", [2, D, FF])
        wd = self.inp("ffn_w_down", [2, FF, D])
        w_in = self.inp("w_in", [D, IN_COLS])
        ident = self.inp("ident", [128, 128])
        out = nc.dram_tensor("out", [NX, D], F32, kind="ExternalOutput").ap()
        self.dbg = {}

        x1s = self.scratch("x1s", [NT, D], F32)

        self.ps = [self.es.enter_context(nc.psum_tensor("ps%d" % i, [128, 512], F32)) for i in range(8)]

        idf = self.sb("idf", [128, 128], F32)
        idb = self.sb("idb", [128, 128], BF16)
        ones1 = self.sb("ones1", [1, 128], F32)
        self.dma(idf[:], ident[:, :], w=["idf"])
        self.op("dve", lambda e: e.tensor_copy(out=idb[:], in_=idf[:]), r=["idf"], w=["idb"])
        self.op("dve", lambda e: e.memset(ones1[:], 1.0), w=["ones1"])
        epsc = self.sb("epsc", [128, 1], F32)
        self.op("dve", lambda e: e.memset(epsc[:], EPS), w=["epsc"])

        NSTG = 3
        stg = [self.sb("stg%d" % i, [128, 2048], F32) for i in range(NSTG)]

        def wload(dst_ap, src_ap, dstbuf, shape3=None):
            i = self.rot.get("stg", 0)
            self.rot["stg"] = (i + 1) % NSTG
            n = 1
            for s_ in dst_ap.shape[1:]:
                n *= s_
            sv = stg[i][:, 0:n]
            if shape3 is not None:
                sv = sv.rearrange("p (a b) -> p a b", a=shape3[0])
            self.dma(sv, src_ap, w=["stg%d" % i])
            self.op("pool", lambda e: e.tensor_copy(out=dst_ap, in_=sv), r=["stg%d" % i], w=[dstbuf])

        modscr = self.scratch("modscr", [2, 9 * D], F32)
        self.begin_phase()
        cT = self.sb("cT", [128, 2, KT, 1], F32)
        for r_ in range(2):
            src = bass.AP(cvec.tensor, cvec.offset + r_ * D, [[1, 128], [128, KT], [1, 1]])
            self.dma(cT[:, r_, :, :], src, w=["cT"], slow=True)
        scT = self.sb("scT", [128, 2, KT], F32)
        self.op("act", lambda e: e.activation(out=scT[:], in_=cT[:, :, :, 0], func=AF.Silu), r=["cT"], w=["scT"])
        mrow = [self.sb("mrow%d" % i, [1, 512], F32) for i in range(4)]
        bmod = self.sb("bmod", [1, 9 * D], F32)
        self.dma(bmod[:], b_mod[:, :], w=["bmod"])
        wm_st = [self.sb("wm_st%d" % i, [128, KT, 512], F32) for i in range(2)]
        wm_v = w_mod.rearrange("(k p) n -> p k n", p=128)
        for cb in range(18):
            sl = slice(cb * 512, (cb + 1) * 512)
            st_ = wm_st[cb % 2]
            sn = "wm_st%d" % (cb % 2)
            self.dma(st_[:], wm_v[:, :, sl], w=[sn])
            for r_ in range(2):
                pt, pn = self.pbank()
                for k in range(KT):
                    self.op("pe", lambda e: e.matmul(pt[0:1, :], lhsT=scT[:, r_, k:k + 1], rhs=st_[:, k, :],
                                                     start=(k == 0), stop=(k == KT - 1)), r=["scT", sn], w=[pn])
                mi = (cb * 2 + r_) % 4
                self.op("dve", lambda e: e.tensor_tensor(out=mrow[mi][:, :], in0=pt[0:1, :], in1=bmod[:, sl], op=ALU.add),
                        r=[pn, "bmod"], w=["mrow%d" % mi])
                self.dma(modscr[r_:r_ + 1, sl], mrow[mi][:, :], r=["mrow%d" % mi], w=["modscr"])
        self.end_phase()

        NBC = 7
        bct = [self.sb("bc%d" % i, [128, D], F32) for i in range(NBC)]
        rowA = self.sb("rowA", [1, D], F32)
        rowB = self.sb("rowB", [1, D], F32)

        def bcast_row(dst_i, row_ap, rbufs):
            for hlf in range(2):
                pt, pn = self.pbank()
                self.op("pe", lambda e: e.matmul(pt[:, :], lhsT=ones1[:, :], rhs=row_ap[:, hlf * 512:(hlf + 1) * 512],
                                                 start=True, stop=True), r=["ones1"] + rbufs, w=[pn])
                self.op("act", lambda e: e.copy(out=bct[dst_i][:, hlf * 512:(hlf + 1) * 512], in_=pt[:, :]),
                        r=[pn], w=["bc%d" % dst_i])

        def bcast_dram_row(dst_i, dram_row_ap, rb):
            self.dma(rowA[:, :], dram_row_ap, r=[rb], w=["rowA"])
            bcast_row(dst_i, rowA, ["rowA"])

        def make_mod_tiles(r_, j, gi, base, mscale):
            self.dma(rowA[:, :], modscr[r_:r_ + 1, (3 * j + 1) * D:(3 * j + 2) * D], r=["modscr"], w=["rowA"])
            self.dma(rowB[:, :], norm_g[gi:gi + 1, :], w=["rowB"])
            self.op("dve", lambda e: e.scalar_tensor_tensor(out=rowA[:, :], in0=rowA[:, :], scalar=1.0, in1=rowB[:, :],
                                                            op0=ALU.add, op1=ALU.mult), r=["rowA", "rowB"], w=["rowA"])
            bcast_row(base + 0, rowA, ["rowA"])
            self.dma(rowB[:, :], modscr[r_:r_ + 1, (3 * j) * D:(3 * j + 1) * D], r=["modscr"], w=["rowB"])
            bcast_row(base + 1, rowB, ["rowB"])
            self.dma(rowA[:, :], modscr[r_:r_ + 1, (3 * j + 2) * D:(3 * j + 3) * D], r=["modscr"], w=["rowA"])
            self.op("dve", lambda e: e.tensor_scalar(out=rowA[:, :], in0=rowA[:, :], scalar1=mscale, scalar2=None, op0=ALU.mult),
                    r=["rowA"], w=["rowA"])
            bcast_row(base + 2, rowA, ["rowA"])

        self.modscr = modscr
        self.bct = bct
        self.make_mod_tiles = make_mod_tiles
        self.bcast_dram_row = bcast_dram_row
        self.idf, self.idb, self.ones1, self.epsc = idf, idb, ones1, epsc
        self.wload = wload
        self.x, self.ctx, self.out = x, ctx, out
        self.x1s = x1s
        self.wg, self.wu, self.wd, self.w_in = wg, wu, wd, w_in

        supers = [(0, 2)] + [(NCTX + i * 512, 4) for i in range(NX // 512)]
        self.supers = supers

        def src_A1(t0, n):
            if t0 < NCTX:
                return ctx[t0:t0 + n, :], None
            return x[t0 - NCTX:t0 - NCTX + n, :], None

        make_mod_tiles(1, 0, 0, 0, 0.5)
        make_mod_tiles(0, 0, 0, 3, 0.5)

        def epi_A1(xs, si, t0, tt, norm_T):
            self.dma(x1s[t0 + tt * 128:t0 + (tt + 1) * 128, :], xs[:, tt, :], r=["xs"], w=["x1s"])

        self.ffn_phase(0, src_A1, supers, lambda si: (0, 1, 2) if si == 0 else (3, 4, 5), epi_A1, "a")

        if self.stage == 1:
            self.begin_phase()
            xs = self.sb("xsd", [128, D], F32)
            for t in range(NX // 128):
                self.dma(xs[:, :], x1s[NCTX + t * 128:NCTX + (t + 1) * 128, :], r=["x1s"], w=["xsd"])
                self.dma(out[t * 128:(t + 1) * 128, :], xs[:, :], r=["xsd"], w=["out"])
            self.finish()
            return

        self.phase_A2()
        if self.stage == 21:
            self.finish()
            return
        self.phase_B1()
        if self.stage == 22:
            self.finish()
            return
        if self.stage in (3, 31):
            self.phase_S5()
            if self.stage == 31:
                self.finish()
                return
            self.dbg_copy("YAs", self.YAs, [W, NX], F32)
            self.finish()
            return
        self.phase_B2()
        if self.stage == 23:
            self.finish()
            return
        self.phase_S5()
        if self.stage == 2:
            self.dbg_copy("YTs", self.YTs.rearrange("d s c -> (d s) c"), [2 * NT, W], BF16)
            for d in range(2):
                self.dbg_copy("KTs%d" % d, self.KTs[d], [NT, W], BF16)
                self.dbg_copy("BTs%d" % d, self.BTs[d], [NT, W], BF16)
                self.dbg_copy("VTs%d" % d, self.VTs[d], [NT, W], BF16)
                self.dbg_copy("AFs%d" % d, self.AFs[d], [W, NT], F32)
                self.dbg_copy("RFs%d" % d, self.RFs[d], [W, NT], F32)
                self.dbg_copy("WFs%d" % d, self.WFs[d], [W, NT], F32)
            self.dbg_copy("BONs", self.BONs, [NT, W], BF16)
            self.dbg_copy("projT", self.projT, [2304, NT], BF16)
            self.finish()
            return
        self.phase_C1()
        if self.stage == 4:
            self.dbg_copy("x2s", self.x2s, [NX, D], F32)
            self.finish()
            return
        self.phase_C2()
        self.finish()

    def dbg_copy(self, name, src_ap, shape, dt):
        o = self.nc.dram_tensor("dbg_" + name, list(shape), dt, kind="ExternalOutput").ap()
        self.S.barrier()
        self.dma(o, src_ap, r=[name], w=["dbg_" + name])

    def norm_tiles(self, sfx):
        hb = [self.sb("hb%d%s" % (i, sfx), [128, D], BF16) for i in range(2)]
        htmp = self.sb("htmp" + sfx, [128, D], F32)
        junk = self.sb("junk" + sfx, [128, D], BF16)
        ssq = self.sb("ssq" + sfx, [128, 4], F32)
        hT = self.sb("hT" + sfx, [128, KT, 512], BF16)
        bct, idb, epsc = self.bct, self.idb, self.epsc

        def norm_T(xs_ap, xbuf, Gi, Si, col0):
            self.op("act", lambda e: e.activation(out=junk[:], in_=xs_ap, func=AF.Square, accum_out=ssq[:, 0:1]),
                    r=[xbuf], w=["junk" + sfx, "ssq" + sfx])
            self.op("act", lambda e: e.activation(out=ssq[:, 1:2], in_=ssq[:, 0:1], func=AF.Sqrt, scale=1.0 / D, bias=epsc[:, 0:1]),
                    r=["ssq" + sfx, "epsc"], w=["ssq" + sfx])
            self.op("dve", lambda e: e.reciprocal(out=ssq[:, 2:3], in_=ssq[:, 1:2]), r=["ssq" + sfx], w=["ssq" + sfx])
            i = self.rot.get("hb", 0)
            self.rot["hb"] = 1 - i
            hbn = "hb%d%s" % (i, sfx)
            if Gi is None:
                return ssq
            self.op("dve", lambda e: e.scalar_tensor_tensor(out=htmp[:], in0=xs_ap, scalar=ssq[:, 2:3], in1=bct[Gi][:],
                                                            op0=ALU.mult, op1=ALU.mult),
                    r=[xbuf, "ssq" + sfx, "bc%d" % Gi], w=["htmp" + sfx])
            self.op("pool", lambda e: e.tensor_tensor(out=hb[i][:], in0=htmp[:], in1=bct[Si][:], op=ALU.add),
                    r=["htmp" + sfx, "bc%d" % Si], w=[hbn])
            pt, pn = self.pbank()
            ptb = pt[:].bitcast(BF16)
            for k in range(KT):
                self.op("pe", lambda e: e.transpose(out=ptb[:, k * 128:(k + 1) * 128], in_=hb[i][:, k * 128:(k + 1) * 128],
                                                    identity=idb[:]), r=[hbn, "idb"], w=[pn])
            self.op("act", lambda e: e.copy(out=hT[:, :, col0:col0 + 128],
                                            in_=ptb.rearrange("p (k c) -> p k c", k=KT)), r=[pn], w=["hT" + sfx])
            return ssq

        return norm_T, hT, "hT" + sfx

    def ffn_phase(self, li, src_of, supers, tiles_of, epilogue, sfx):
        self.begin_phase()
        bct, wload = self.bct, self.wload
        wg, wu, wd = self.wg, self.wu, self.wd
        norm_T, hT, hTn = self.norm_tiles(sfx)
        xs = self.sb("xs", [128, 4, D], F32)
        actT = self.sb("actT", [128, NFT, 512], BF16)
        wd_bf = self.sb("wd_bf", [128, NFT, D], BF16)
        FB = 256
        NFB = FF // FB
        wg_bf = [self.sb("wg_bf%d" % i, [128, KT, FB], BF16) for i in range(2)]
        wu_bf = [self.sb("wu_bf%d" % i, [128, KT, FB], BF16) for i in range(2)]
        sg = [self.sb("sg%d" % i, [128, 512], F32) for i in range(2)]
        ytmp = self.sb("ytmp", [128, 512], F32)
        wdv = wd[li].rearrange("(f p) n -> p f n", p=128)
        for f in range(0, NFT, 2):
            wload(wd_bf[:, f:f + 2, :], wdv[:, f:f + 2, :], "wd_bf", shape3=(2, D))
        wgv = wg[li].rearrange("(k p) f -> p k f", p=128)
        wuv = wu[li].rearrange("(k p) f -> p k f", p=128)
        for si, (t0, ntt) in enumerate(supers):
            ntok = ntt * 128
            Gi, Si, Mi = tiles_of(si)
            src, srcbuf = src_of(t0, ntok)
            self.dma(xs[:, 0:ntt, :], src.rearrange("(t p) d -> p t d", p=128), r=[srcbuf] if srcbuf else [], w=["xs"])
            for tt in range(ntt):
                norm_T(xs[:, tt, :], "xs", Gi, Si, tt * 128)
            for fb in range(NFB):
                bi = self.rot.get("wgu", 0)
                self.rot["wgu"] = 1 - bi
                wload(wg_bf[bi][:], wgv[:, :, fb * FB:(fb + 1) * FB], "wg_bf%d" % bi, shape3=(KT, FB))
                wload(wu_bf[bi][:], wuv[:, :, fb * FB:(fb + 1) * FB], "wu_bf%d" % bi, shape3=(KT, FB))
                for fl in range(FB // 128):
                    f = fb * (FB // 128) + fl
                    pg, pgn = self.pbank()
                    pu, pun = self.pbank()
                    for k in range(KT):
                        self.op("pe", lambda e: e.matmul(pg[:, 0:ntok], lhsT=wg_bf[bi][:, k, fl * 128:(fl + 1) * 128],
                                                         rhs=hT[:, k, 0:ntok], start=(k == 0), stop=(k == KT - 1)),
                                r=["wg_bf%d" % bi, hTn], w=[pgn])
                    for k in range(KT):
                        self.op("pe", lambda e: e.matmul(pu[:, 0:ntok], lhsT=wu_bf[bi][:, k, fl * 128:(fl + 1) * 128],
                                                         rhs=hT[:, k, 0:ntok], start=(k == 0), stop=(k == KT - 1)),
                                r=["wu_bf%d" % bi, hTn], w=[pun])
                    gi_ = self.rot.get("sg", 0)
                    self.rot["sg"] = 1 - gi_
                    self.op("act", lambda e: e.activation(out=sg[gi_][:, 0:ntok], in_=pg[:, 0:ntok], func=AF.Silu),
                            r=[pgn], w=["sg%d" % gi_])
                    self.op("dve", lambda e: e.tensor_tensor(out=actT[:, f, 0:ntok], in0=sg[gi_][:, 0:ntok],
                                                             in1=pu[:, 0:ntok], op=ALU.mult),
                            r=["sg%d" % gi_, pun], w=["actT"])
            for tt in range(ntt):
                for hlf in range(2):
                    py, pyn = self.pbank()
                    for f in range(NFT):
                        self.op("pe", lambda e: e.matmul(py[:, :], lhsT=actT[:, f, tt * 128:(tt + 1) * 128],
                                                         rhs=wd_bf[:, f, hlf * 512:(hlf + 1) * 512],
                                                         start=(f == 0), stop=(f == NFT - 1)),
                                r=["actT", "wd_bf"], w=[pyn])
                    self.op("dve", lambda e: e.tensor_tensor(out=ytmp[:], in0=py[:, :], in1=bct[Mi][:, hlf * 512:(hlf + 1) * 512],
                                                             op=ALU.mult), r=[pyn, "bc%d" % Mi], w=["ytmp"])
                    self.op("pool", lambda e: e.tensor_tensor(out=xs[:, tt, hlf * 512:(hlf + 1) * 512], in0=ytmp[:],
                                                              in1=xs[:, tt, hlf * 512:(hlf + 1) * 512], op=ALU.add),
                            r=["ytmp", "xs"], w=["xs"])
                epilogue(xs, si, t0, tt, norm_T)
        self.end_phase()

    def phase_A2(self):
        nc = self.nc
        NPC = 2304
        self.projT = projT = self.scratch("projT", [NPC, NT], BF16)
        self.make_mod_tiles(1, 1, 1, 0, 1.0)
        self.make_mod_tiles(0, 1, 1, 3, 1.0)
        self.begin_phase()
        norm_T, hT, hTn = self.norm_tiles("b")
        win_bf = self.sb("win_bf", [128, KT, NPC], BF16)
        wv = self.w_in.rearrange("(k p) n -> p k n", p=128)
        for k in range(KT):
            for c0 in range(0, NPC, 1152):
                self.wload(win_bf[:, k, c0:c0 + 1152], wv[:, k, c0:c0 + 1152], "win_bf")
        xs = self.sb("xs2", [128, 4, D], F32)
        pst = [self.sb("pst%d" % i, [128, 512], BF16) for i in range(3)]
        for si, (t0, ntt) in enumerate(self.supers):
            ntok = ntt * 128
            Gi, Si = (0, 1) if si == 0 else (3, 4)
            self.dma(xs[:, 0:ntt, :], self.x1s[t0:t0 + ntok, :].rearrange("(t p) d -> p t d", p=128), r=["x1s"], w=["xs2"])
            for tt in range(ntt):
                norm_T(xs[:, tt, :], "xs2", Gi, Si, tt * 128)
            for ct in range(NPC // 128):
                pt, pn = self.pbank()
                for k in range(KT):
                    self.op("pe", lambda e: e.matmul(pt[:, 0:ntok], lhsT=win_bf[:, k, ct * 128:(ct + 1) * 128],
                                                     rhs=hT[:, k, 0:ntok], start=(k == 0), stop=(k == KT - 1)),
                            r=["win_bf", hTn], w=[pn])
                pi = self.rot.get("pst", 0)
                self.rot["pst"] = (pi + 1) % 3
                eng = "act" if ct % 2 == 0 else "dve"
                if eng == "act":
                    self.op("act", lambda e: e.copy(out=pst[pi][:, 0:ntok], in_=pt[:, 0:ntok]), r=[pn], w=["pst%d" % pi])
                else:
                    self.op("dve", lambda e: e.tensor_copy(out=pst[pi][:, 0:ntok], in_=pt[:, 0:ntok]), r=[pn], w=["pst%d" % pi])
                self.dma(projT[ct * 128:(ct + 1) * 128, t0:t0 + ntok], pst[pi][:, 0:ntok], r=["pst%d" % pi], w=["projT"])
        self.end_phase()

    def phase_B1(self):
        nc = self.nc
        projT, idb = self.projT, self.idb
        convw = self.inp("convw", [128, 12, 9])
        w0T_d = self.inp("w0T", [128, 8])
        a0T_d = self.inp("a0T", [128, 8])
        vec4_d = self.inp("vec4", [128, 5, 4])
        w2_d = self.inp("w2", [128, W])
        a2_d = self.inp("a2", [128, W])
        blk1_d = self.inp("blk1", [128, 128])
        self.AFs = [self.scratch("AFs%d" % d, [W, NT], F32) for d in range(2)]
        self.RFs = [self.scratch("RFs%d" % d, [W, NT], F32) for d in range(2)]
        self.WFs = [self.scratch("WFs%d" % d, [W, NT], F32) for d in range(2)]
        self.KTs = [self.scratch("KTs%d" % d, [NT, W], BF16) for d in range(2)]
        self.BTs = [self.scratch("BTs%d" % d, [NT, W], BF16) for d in range(2)]
        self.VTs = [self.scratch("VTs%d" % d, [NT, W], BF16) for d in range(2)]
        self.BONs = self.scratch("BONs", [NT, W], BF16)
        self.begin_phase()
        cw = self.sb("cw", [128, 12, 9], F32)
        w0T = self.sb("w0Ts", [128, 8], F32)
        a0T = self.sb("a0Ts", [128, 8], F32)
        vec4 = self.sb("vec4s", [128, 5, 4], F32)
        blk1 = self.sb("blk1s", [128, 128], F32)
        w2b = self.sb("w2b", [128, W], BF16)
        a2b = self.sb("a2b", [128, W], BF16)
        self.dma(cw[:], convw[:, :, :], w=["cw"])
        self.dma(w0T[:], w0T_d[:, :], w=["w0Ts"])
        self.dma(a0T[:], a0T_d[:, :], w=["a0Ts"])
        self.dma(vec4[:], vec4_d[:, :, :], w=["vec4s"])
        self.dma(blk1[:], blk1_d[:, :], w=["blk1s"])
        self.wload(w2b[:], w2_d[:, :], "w2b")
        self.wload(a2b[:], a2_d[:, :], "a2b")
        eps12 = self.sb("eps12", [128, 1], F32)
        self.op("dve", lambda e: e.memset(eps12[:], 1e-12), w=["eps12"])
        twd = self.sb("twd", [128, NT], BF16)
        adb = self.sb("adb", [128, NT], BF16)
        self.dma(twd[:], projT[2048:2176, :], r=["projT"], w=["twd"])
        self.dma(adb[:], projT[2176:2304, :], r=["projT"], w=["adb"])
        self.op("act", lambda e: e.activation(out=twd[:], in_=twd[:], func=AF.Tanh), r=["twd"], w=["twd"])
        raw = [self.sb("raw%d" % a, [128, NT], BF16) for a in range(3)]
        cv = [self.sb("cv%d" % a, [128, NT], F32) for a in range(3)]
        NB = 256
        tnames = ["sig", "dec0", "dec1", "icl0", "icl1", "kx", "sq", "rn", "kk", "t1", "t2", "kt0", "kt1", "rvt"]
        tf = {n: self.sb("t_" + n, [128, NB], F32) for n in tnames}
        bnames = ["ab", "b0", "b1", "k0", "k1", "vb", "rvb", "bon"]
        tb_ = {n: self.sb("tb_" + n, [128, NB], BF16) for n in bnames}
        tmo = [self.sb("tmo%d" % i, [128, 2, 128], BF16) for i in range(2)]

        def s0_of(d, t0):
            if d == 0 or t0 < NCTX:
                return t0
            return NT - t0

        def emit_fm(scr, sname, d, src_ap, srcbuf, ct, t0):
            if d == 0:
                self.dma(scr[ct * 128:(ct + 1) * 128, t0:t0 + NB], src_ap, r=[srcbuf], w=[sname])
            else:
                self.op("pool", lambda e: e.tensor_copy(out=tf["rvt"][:], in_=rev(src_ap)), r=[srcbuf], w=["t_rvt"])
                s0 = s0_of(1, t0)
                self.dma(scr[ct * 128:(ct + 1) * 128, s0:s0 + NB], tf["rvt"][:], r=["t_rvt"], w=[sname])

        def emit_tm(scr, sname, d, src_t, srcbuf, ct, t0):
            src = src_t
            sb_ = srcbuf
            if d == 1:
                self.op("pool", lambda e: e.tensor_copy(out=tb_["rvb"][:], in_=rev(src_t[:, :])), r=[srcbuf], w=["tb_rvb"])
                src = tb_["rvb"]
                sb_ = "tb_rvb"
            pt, pn = self.pbank()
            ptb = pt[:].bitcast(BF16)
            for j in range(2):
                self.op("pe", lambda e: e.transpose(out=ptb[:, j * 128:(j + 1) * 128], in_=src[:, j * 128:(j + 1) * 128],
                                                    identity=idb[:]), r=[sb_, "idb"], w=[pn])
            oi = self.rot.get("tmo", 0)
            self.rot["tmo"] = 1 - oi
            self.op("act", lambda e: e.copy(out=tmo[oi][:], in_=ptb[:, 0:256].rearrange("p (j c) -> p j c", j=2)),
                    r=[pn], w=["tmo%d" % oi])
            s0 = s0_of(d, t0)
            self.dma(scr[s0:s0 + NB, ct * 128:(ct + 1) * 128].rearrange("(j p) c -> p j c", p=128), tmo[oi][:],
                     r=["tmo%d" % oi], w=[sname])

        def conv(dst, dn, src, sn, cwi):
            self.op("dve", lambda e: e.tensor_scalar(out=dst[:, :], in0=src[:, :], scalar1=cw[:, cwi, 4:5], scalar2=None,
                                                     op0=ALU.mult), r=[sn, "cw"], w=[dn])
            for tap, sh in ((3, -1), (5, 1)):
                if sh == -1:
                    o, i_ = dst[:, 1:NCTX], src[:, 0:NCTX - 1]
                else:
                    o, i_ = dst[:, 0:NCTX - 1], src[:, 1:NCTX]
                self.op("dve", lambda e: e.scalar_tensor_tensor(out=o, in0=i_, scalar=cw[:, cwi, tap:tap + 1], in1=o,
                                                                op0=ALU.mult, op1=ALU.add), r=[sn, dn, "cw"], w=[dn])
            gd = dst[:, NCTX:NT].rearrange("p (r c) -> p r c", c=64)
            gs = src[:, NCTX:NT].rearrange("p (r c) -> p r c", c=64)
            for dy in range(3):
                for dx in range(3):
                    if dy == 1 and dx == 1:
                        continue
                    oy, ox = dy - 1, dx - 1
                    r0, r1 = max(0, -oy), 64 - max(0, oy)
                    c0, c1 = max(0, -ox), 64 - max(0, ox)
                    o = gd[:, r0:r1, c0:c1]
                    i_ = gs[:, r0 + oy:r1 + oy, c0 + ox:c1 + ox]
                    tap = dy * 3 + dx
                    self.op("dve", lambda e: e.scalar_tensor_tensor(out=o, in0=i_, scalar=cw[:, cwi, tap:tap + 1], in1=o,
                                                                    op0=ALU.mult, op1=ALU.add), r=[sn, dn, "cw"], w=[dn])

        def mm_lora(wb, wbn, src, srcn, d, ct, sl):
            pt, pn = self.pbank()
            self.op("pe", lambda e: e.matmul(pt[:, 0:NB], lhsT=wb[d * 64:(d + 1) * 64, ct * 128:(ct + 1) * 128],
                                             rhs=src[d * 64:(d + 1) * 64, sl], start=True, stop=True), r=[wbn, srcn], w=[pn])
            return pt, pn

        def headsum(src_t, srcn):
            pt, pn = self.pbank()
            self.op("pe", lambda e: e.matmul(pt[:, 0:NB], lhsT=blk1[:, :], rhs=src_t[:, :], start=True, stop=True),
                    r=["blk1s", srcn], w=[pn])
            return pt, pn

        T = lambda n: tf[n]
        for ct in range(4):
            for a in range(3):
                self.dma(raw[a][:], projT[512 + a * 512 + ct * 128:512 + a * 512 + (ct + 1) * 128, :], r=["projT"], w=["raw%d" % a])
                conv(cv[a], "cv%d" % a, raw[a], "raw%d" % a, a * 4 + ct)
            rc, kc, vc = cv
            for bi in range(NT // NB):
                t0 = bi * NB
                sl = slice(t0, t0 + NB)
                for d in range(2):
                    pt, pn = mm_lora(w2b, "w2b", twd, "twd", d, ct, sl)
                    self.op("act", lambda e: e.activation(out=T("sig")[:], in_=pt[:, 0:NB], func=AF.Sigmoid,
                                                          bias=w0T[:, d * 4 + ct:d * 4 + ct + 1]), r=[pn, "w0Ts"], w=["t_sig"])
                    dn = "dec%d" % d
                    self.op("act", lambda e: e.activation(out=T(dn)[:], in_=T("sig")[:], func=AF.Exp, scale=-math.exp(-0.5)),
                            r=["t_sig"], w=["t_" + dn])
                    emit_fm(self.WFs[d], "WFs%d" % d, d, T(dn)[:], "t_" + dn, ct, t0)
                    pt, pn = mm_lora(a2b, "a2b", adb, "adb", d, ct, sl)
                    inm = "icl%d" % d
                    self.op("act", lambda e: e.activation(out=T(inm)[:], in_=pt[:, 0:NB], func=AF.Sigmoid,
                                                          bias=a0T[:, d * 4 + ct:d * 4 + ct + 1]), r=[pn, "a0Ts"], w=["t_" + inm])
                self.op("dve", lambda e: e.tensor_scalar(out=T("kx")[:], in0=kc[:, sl], scalar1=vec4[:, 0, ct:ct + 1], scalar2=None,
                                                         op0=ALU.mult), r=["cv1", "vec4s"], w=["t_kx"])
                self.op("pool", lambda e: e.tensor_tensor(out=T("sq")[:], in0=T("kx")[:], in1=T("kx")[:], op=ALU.mult),
                        r=["t_kx"], w=["t_sq"])
                pt, pn = headsum(T("sq"), "t_sq")
                self.op("act", lambda e: e.activation(out=T("rn")[:], in_=pt[:, 0:NB], func=AF.Sqrt, bias=eps12[:, 0:1]),
                        r=[pn, "eps12"], w=["t_rn"])
                self.op("dve", lambda e: e.reciprocal(out=T("rn")[:], in_=T("rn")[:]), r=["t_rn"], w=["t_rn"])
                self.op("dve", lambda e: e.tensor_tensor(out=T("kk")[:], in0=T("kx")[:], in1=T("rn")[:], op=ALU.mult),
                        r=["t_kx", "t_rn"], w=["t_kk"])
                self.op("pool", lambda e: e.tensor_scalar(out=T("t1")[:], in0=T("kk")[:], scalar1=-1.0, scalar2=None, op0=ALU.mult),
                        r=["t_kk"], w=["t_t1"])
                for d in range(2):
                    emit_fm(self.AFs[d], "AFs%d" % d, d, T("t1")[:], "t_t1", ct, t0)
                    emit_fm(self.RFs[d], "RFs%d" % d, d, rc[:, sl], "cv0", ct, t0)
                self.op("act", lambda e: e.copy(out=tb_["vb"][:], in_=vc[:, sl]), r=["cv2"], w=["tb_vb"])
                for d in range(2):
                    emit_tm(self.VTs[d], "VTs%d" % d, d, tb_["vb"], "tb_vb", ct, t0)
                for d in range(2):
                    bn, kn, ktn, inm = "b%d" % d, "k%d" % d, "kt%d" % d, "icl%d" % d
                    self.op("dve", lambda e: e.tensor_tensor(out=tb_[bn][:], in0=T("kk")[:], in1=T(inm)[:], op=ALU.mult),
                            r=["t_kk", "t_" + inm], w=["tb_" + bn])
                    emit_tm(self.BTs[d], "BTs%d" % d, d, tb_[bn], "tb_" + bn, ct, t0)
                    self.op("dve", lambda e: e.tensor_scalar(out=T("t2")[:], in0=T(inm)[:], scalar1=-1.0,
                                                             scalar2=vec4[:, 1, ct:ct + 1], op0=ALU.add, op1=ALU.mult),
                            r=["t_" + inm, "vec4s"], w=["t_t2"])
                    self.op("dve", lambda e: e.scalar_tensor_tensor(out=T(ktn)[:], in0=T("t2")[:], scalar=1.0, in1=kc[:, sl],
                                                                    op0=ALU.add, op1=ALU.mult), r=["t_t2", "cv1"], w=["t_" + ktn])
                    self.op("act", lambda e: e.copy(out=tb_[kn][:], in_=T(ktn)[:]), r=["t_" + ktn], w=["tb_" + kn])
                    emit_tm(self.KTs[d], "KTs%d" % d, d, tb_[kn], "tb_" + kn, ct, t0)
                self.op("pool", lambda e: e.tensor_tensor(out=T("t2")[:], in0=T("kt0")[:], in1=T("kt1")[:], op=ALU.add),
                        r=["t_kt0", "t_kt1"], w=["t_t2"])
                self.op("dve", lambda e: e.scalar_tensor_tensor(out=T("sq")[:], in0=rc[:, sl], scalar=vec4[:, 2, ct:ct + 1],
                                                                in1=T("t2")[:], op0=ALU.mult, op1=ALU.mult),
                        r=["cv0", "vec4s", "t_t2"], w=["t_sq"])
                pt, pn = headsum(T("sq"), "t_sq")
                self.op("dve", lambda e: e.tensor_tensor(out=tb_["bon"][:], in0=pt[:, 0:NB], in1=vc[:, sl], op=ALU.mult),
                        r=[pn, "cv2"], w=["tb_bon"])
                emit_tm(self.BONs, "BONs", 0, tb_["bon"], "tb_bon", ct, t0)
        self.end_phase()

    def phase_B2(self):
        nc = self.nc
        mask_d = self.inp("mask16", [16, W])
        self.YTs = self.scratch("YTs", [2, NT, W], BF16)
        YTs = self.YTs
        self.begin_phase()
        SBK = 128
        SBS = 8
        Tst = self.sb("Tst", [128, W], F32)
        self.op("dve", lambda e: e.memset(Tst[:], 0.0), w=["Tst"])
        mask16 = self.sb("mask16s", [16, W], F32)
        self.dma(mask16[:], mask_d[:, :], w=["mask16s"])
        mask48 = self.sb("mask48", [48, W], BF16)
        self.dma(Tst[32:48, :], mask_d[:, :], w=["Tst"])
        self.op("dve", lambda e: e.tensor_copy(out=mask48[32:48, :], in_=Tst[32:48, :]), r=["Tst"], w=["mask48"])
        self.op("dve", lambda e: e.memset(Tst[:], 0.0), r=["mask48"], w=["Tst"])
        Ap = self.sb("Ap", [128, H, SBK], F32)
        Rp = self.sb("Rp", [128, H, SBK], F32)
        Wp = self.sb("Wp", [128, H, SBK], F32)
        Ablk = self.sb("Ablk", [128, SBK, 16], BF16)
        Rblk = self.sb("Rblk", [128, SBK, 16], BF16)
        Tbf = self.sb("Tbf", [128, W], BF16)
        self.op("pool", lambda e: e.memset(Tbf[:], 0.0), w=["Tbf"])
        self.op("pool", lambda e: e.memset(Ablk[:], 0.0), w=["Ablk"])
        self.op("pool", lambda e: e.memset(Rblk[:], 0.0), w=["Rblk"])
        KB = [self.sb("KB%d" % i, [64, SBS, 128], BF16) for i in range(2)]
        UV = [self.sb("UV%d" % i, [64, SBS, W], BF16) for i in range(2)]
        Vr = [self.sb("Vr%d" % i, [48, SBS, W], BF16) for i in range(2)]
        Yf = [self.sb("Yf%d" % i, [16, SBS, W], BF16) for i in range(2)]
        for i in range(2):
            self.op("pool", lambda e: e.memset(KB[i][:], 0.0), w=["KB%d" % i])
            self.op("pool", lambda e: e.memset(UV[i][:], 0.0), w=["UV%d" % i])
        nsteps = NT if self.stage != 23 else 320
        Tw = self.sb("Tw", [128, W], F32)

        def emit_y(sp):
            sbp = (sp // SBS) % 2
            ssp = sp % SBS
            py, pyn = self.ps[4 + sp % 2], "ps%d" % (4 + sp % 2)
            self.op("pe", lambda e: e.matmul(py[0:16, :], lhsT=Rblk[:, sp % SBK, :], rhs=Tbf[:, :],
                                             start=True, stop=True), r=["Rblk", "Tbf"], w=[pyn])
            self.op("act", lambda e: e.copy(out=Yf[sbp][:, ssp, :], in_=py[0:16, :]), r=[pyn], w=["Yf%d" % sbp])
            if ssp == SBS - 1:
                s1_ = sp - ssp
                for h in range(H):
                    src = Yf[sbp][h:16:8, :, h * 64:(h + 1) * 64]
                    dst = YTs[:, s1_:s1_ + SBS, h * 64:(h + 1) * 64]
                    self.dma(dst, src, r=["Yf%d" % sbp], w=["YTs"])

        for s in range(nsteps):
            deferred = (s - 1) >= NCTX
            if s % SBK == 0:
                if deferred:
                    emit_y(s - 1)
                    deferred = False
                s0 = s
                for d in range(2):
                    for (tile_, tn, scr, sn) in ((Ap, "Ap", self.AFs[d], "AFs%d" % d), (Rp, "Rp", self.RFs[d], "RFs%d" % d),
                                                 (Wp, "Wp", self.WFs[d], "WFs%d" % d)):
                        self.dma(tile_[d * 64:(d + 1) * 64, :, :],
                                 scr.rearrange("(h j) s -> j h s", j=64)[:, :, s0:s0 + SBK], r=[sn], w=[tn])
                    self.op("pool", lambda e: e.tensor_copy(out=Ablk[d * 64:(d + 1) * 64, :, d * 8:(d + 1) * 8],
                                                            in_=Ap[d * 64:(d + 1) * 64, :, :].rearrange("p h s -> p s h")),
                            r=["Ap"], w=["Ablk"])
                    self.op("pool", lambda e: e.tensor_copy(out=Rblk[d * 64:(d + 1) * 64, :, d * 8:(d + 1) * 8],
                                                            in_=Rp[d * 64:(d + 1) * 64, :, :].rearrange("p h s -> p s h")),
                            r=["Rp"], w=["Rblk"])
            if s % SBS == 0:
                sb_i = (s // SBS) % 2
                s1 = s
                for d in range(2):
                    self.dma(KB[sb_i][32 + d * 8:32 + d * 8 + 8, :, d * 64:(d + 1) * 64],
                             self.KTs[d][s1:s1 + SBS, :].rearrange("s (h j) -> h s j", j=64), r=["KTs%d" % d], w=["KB%d" % sb_i])
                    self.dma(KB[sb_i][d * 8:d * 8 + 8, :, d * 64:(d + 1) * 64],
                             self.BTs[d][s1:s1 + SBS, :].rearrange("s (h j) -> h s j", j=64), r=["BTs%d" % d], w=["KB%d" % sb_i])
                    vsrc = self.VTs[d]
                    src = bass.AP(vsrc.tensor, vsrc.offset + s1 * W, [[0, 8], [W, SBS], [1, W]])
                    self.dma(Vr[sb_i][32 + d * 8:32 + d * 8 + 8, :, :], src, r=["VTs%d" % d], w=["Vr%d" % sb_i])
                self.op("pool", lambda e: e.tensor_tensor(
                    out=UV[sb_i][32:48, :, :], in0=Vr[sb_i][32:48, :, :],
                    in1=bass.AP(mask48[32:48, :].tensor, mask48[32:48, :].offset, [list(mask48[32:48, :].ap[0]), [0, SBS], [1, W]]),
                    op=ALU.mult), r=["Vr%d" % sb_i, "mask48"], w=["UV%d" % sb_i])
            sl = s % SBK
            ss = s % SBS
            sb_i = (s // SBS) % 2
            par = s % 2
            pu, pun = self.ps[par], "ps%d" % par
            pd, pdn = self.ps[2 + par], "ps%d" % (2 + par)
            self.op("pe", lambda e: e.matmul(pu[0:16, :], lhsT=Ablk[:, sl, :], rhs=Tbf[:, :],
                                             start=True, stop=True), r=["Ablk", "Tbf"], w=[pun])
            if deferred:
                emit_y(s - 1)
            wsl = Wp[:, :, sl:sl + 1]
            wbc = bass.AP(wsl.tensor, wsl.offset, [list(wsl.ap[0]), list(wsl.ap[1]), [0, 64]])
            self.op("pool", lambda e: e.tensor_tensor(out=Tw[:, :].rearrange("p (h j) -> p h j", j=64),
                                                      in0=Tst[:, :].rearrange("p (h j) -> p h j", j=64), in1=wbc, op=ALU.mult),
                    r=["Tst", "Wp"], w=["Tw"])
            self.op("dve", lambda e: e.tensor_tensor(out=UV[sb_i][0:16, ss, :], in0=pu[0:16, :], in1=mask16[:, :], op=ALU.mult),
                    r=[pun, "mask16s"], w=["UV%d" % sb_i])
            self.op("pe", lambda e: e.matmul(pd[:, :], lhsT=KB[sb_i][0:64, ss, :], rhs=UV[sb_i][0:64, ss, :],
                                             start=True, stop=True), r=["KB%d" % sb_i, "UV%d" % sb_i], w=[pdn])
            self.op("dve", lambda e: e.tensor_tensor(out=Tst[:, :], in0=Tw[:, :], in1=pd[:, :], op=ALU.add), r=["Tw", pdn], w=["Tst"])
            self.op("act", lambda e: e.copy(out=Tbf[:, :], in_=Tst[:, :]), r=["Tst"], w=["Tbf"])
        if nsteps - 1 >= NCTX:
            emit_y(nsteps - 1)
        self.end_phase()

    def phase_S5(self):
        nc = self.nc
        idf, idb = self.idf, self.idb
        s5p_d = self.inp("s5p", [128, 5, 512])
        s5s_d = self.inp("s5s", [128, 3, 64])
        ctn_d = self.inp("ctn", [128, 64, 16])
        s5D_d = self.inp("s5D", [128, 4])
        jsw_d = self.inp("jsw", [128, 128])
        gmask_d = self.inp("gmask", [128, 8])
        self.YAs = YAs = self.scratch("YAs", [W, NX], F32)
        self.begin_phase()
        TWO_PI = 2.0 * math.pi
        jsw = self.sb("jsws", [128, 128], F32)
        gmask = self.sb("gmasks", [128, 8], F32)
        s5D = self.sb("s5Ds", [128, 4], F32)
        self.dma(jsw[:], jsw_d[:, :], w=["jsws"])
        self.dma(gmask[:], gmask_d[:, :], w=["gmasks"])
        self.dma(s5D[:], s5D_d[:, :], w=["s5Ds"])
        pb = self.sb("s5pb", [128, 5, 512], F32)
        psm = self.sb("s5ps", [128, 3, 64], F32)
        self.dma(pb[:], s5p_d[:, :, :], w=["s5pb"])
        self.dma(psm[:], s5s_d[:, :, :], w=["s5ps"])
        tmp = [self.sb("s5t%d" % i, [128, 512], F32) for i in range(8)]
        tmi = self.sb("s5ti", [128, 512], mybir.dt.int32)

        def dv(fn, r, w):
            self.op("dve", fn, r=r, w=w)

        def frac_sin(dst, dn, t_ap, tn, n, tA, tAn, tB, tBn, add):
            dv(lambda e: e.tensor_scalar(out=tA[:, 0:n], in0=t_ap, scalar1=add, scalar2=None, op0=ALU.add), [tn], [tAn])
            dv(lambda e: e.tensor_copy(out=tmi[:, 0:n], in_=tA[:, 0:n]), [tAn], ["s5ti"])
            dv(lambda e: e.tensor_copy(out=tB[:, 0:n], in_=tmi[:, 0:n]), ["s5ti"], [tBn])
            dv(lambda e: e.tensor_tensor(out=tA[:, 0:n], in0=tA[:, 0:n], in1=tB[:, 0:n], op=ALU.subtract), [tAn, tBn], [tAn])
            dv(lambda e: e.tensor_scalar(out=tB[:, 0:n], in0=tA[:, 0:n], scalar1=0.5, scalar2=None, op0=ALU.is_gt), [tAn], [tBn])
            dv(lambda e: e.tensor_tensor(out=tA[:, 0:n], in0=tA[:, 0:n], in1=tB[:, 0:n], op=ALU.subtract), [tAn, tBn], [tAn])
            dv(lambda e: e.tensor_scalar(out=tB[:, 0:n], in0=tA[:, 0:n], scalar1=-0.5, scalar2=None, op0=ALU.is_lt), [tAn], [tBn])
            dv(lambda e: e.tensor_tensor(out=tA[:, 0:n], in0=tA[:, 0:n], in1=tB[:, 0:n], op=ALU.add), [tAn, tBn], [tAn])
            self.op("act", lambda e: e.activation(out=dst, in_=tA[:, 0:n], func=AF.Sin, scale=TWO_PI), r=[tAn], w=[dn])

        def abar(lre, lim, ldt, srcn, n, ar, arn, ai, ain):
            t0_, t1_, t2_, t3_ = tmp[0], tmp[1], tmp[2], tmp[3]
            self.op("act", lambda e: e.activation(out=t0_[:, 0:n], in_=ldt, func=AF.Exp), r=[srcn], w=["s5t0"])
            dv(lambda e: e.tensor_tensor(out=t1_[:, 0:n], in0=t0_[:, 0:n], in1=lre, op=ALU.mult), ["s5t0", srcn], ["s5t1"])
            self.op("act", lambda e: e.activation(out=t1_[:, 0:n], in_=t1_[:, 0:n], func=AF.Exp), r=["s5t1"], w=["s5t1"])
            dv(lambda e: e.scalar_tensor_tensor(out=t0_[:, 0:n], in0=t0_[:, 0:n], scalar=1.0 / TWO_PI, in1=lim,
                                                op0=ALU.mult, op1=ALU.mult), ["s5t0", srcn], ["s5t0"])
            frac_sin(ai, ain, t0_[:, 0:n], "s5t0", n, t2_, "s5t2", t3_, "s5t3", 0.0)
            frac_sin(ar, arn, t0_[:, 0:n], "s5t0", n, t2_, "s5t2", t3_, "s5t3", 0.25)
            dv(lambda e: e.tensor_tensor(out=ai, in0=ai, in1=t1_[:, 0:n], op=ALU.mult), [ain, "s5t1"], [ain])
            dv(lambda e: e.tensor_tensor(out=ar, in0=ar, in1=t1_[:, 0:n], op=ALU.mult), [arn, "s5t1"], [arn])

        arc = self.sb("arc", [128, 64], F32)
        ais = self.sb("ais", [128, 64], F32)
        abar(psm[:, 0, :], psm[:, 1, :], psm[:, 2, :], "s5ps", 64, arc[:, :], "arc", ais[:, :], "ais")
        dv(lambda e: e.tensor_scalar(out=ais[64:128, :], in0=ais[64:128, :], scalar1=-1.0, scalar2=None, op0=ALU.mult), ["ais"], ["ais"])
        arB = self.sb("arB", [128, 512], F32)
        aiB = self.sb("aiB", [128, 512], F32)
        abar(pb[:, 0, :], pb[:, 1, :], pb[:, 2, :], "s5pb", 512, arB[:, :], "arB", aiB[:, :], "aiB")
        lre, lim, bre, bim = pb[:, 0, :], pb[:, 1, :], pb[:, 3, :], pb[:, 4, :]
        den, zre, zim, t6 = tmp[4], tmp[5], tmp[6], tmp[7]
        dv(lambda e: e.tensor_tensor(out=den[:], in0=lre, in1=lre, op=ALU.mult), ["s5pb"], ["s5t4"])
        dv(lambda e: e.tensor_tensor(out=t6[:], in0=lim, in1=lim, op=ALU.mult), ["s5pb"], ["s5t7"])
        dv(lambda e: e.tensor_tensor(out=den[:], in0=den[:], in1=t6[:], op=ALU.add), ["s5t4", "s5t7"], ["s5t4"])
        dv(lambda e: e.reciprocal(out=den[:], in_=den[:]), ["s5t4"], ["s5t4"])
        dv(lambda e: e.tensor_scalar(out=arB[:], in0=arB[:], scalar1=-1.0, scalar2=None, op0=ALU.add), ["arB"], ["arB"])
        dv(lambda e: e.tensor_tensor(out=zre[:], in0=arB[:], in1=lre, op=ALU.mult), ["arB", "s5pb"], ["s5t5"])
        dv(lambda e: e.tensor_tensor(out=t6[:], in0=aiB[:], in1=lim, op=ALU.mult), ["aiB", "s5pb"], ["s5t7"])
        dv(lambda e: e.tensor_tensor(out=zre[:], in0=zre[:], in1=t6[:], op=ALU.add), ["s5t5", "s5t7"], ["s5t5"])
        dv(lambda e: e.tensor_tensor(out=zre[:], in0=zre[:], in1=den[:], op=ALU.mult), ["s5t5", "s5t4"], ["s5t5"])
        dv(lambda e: e.tensor_tensor(out=zim[:], in0=aiB[:], in1=lre, op=ALU.mult), ["aiB", "s5pb"], ["s5t6"])
        dv(lambda e: e.tensor_tensor(out=t6[:], in0=arB[:], in1=lim, op=ALU.mult), ["arB", "s5pb"], ["s5t7"])
        dv(lambda e: e.tensor_tensor(out=zim[:], in0=zim[:], in1=t6[:], op=ALU.subtract), ["s5t6", "s5t7"], ["s5t6"])
        dv(lambda e: e.tensor_tensor(out=zim[:], in0=zim[:], in1=den[:], op=ALU.mult), ["s5t6", "s5t4"], ["s5t6"])
        BT = self.sb("BTcat", [128, 8, 128], F32)
        z3 = lambda t: t[:].rearrange("p (a b) -> p a b", a=8)
        p3 = lambda ap_: ap_.rearrange("p (a b) -> p a b", a=8)
        t0_, t1_ = tmp[0], tmp[1]
        dv(lambda e: e.tensor_tensor(out=t0_[:], in0=zre[:], in1=bre, op=ALU.mult), ["s5t5", "s5pb"], ["s5t0"])
        dv(lambda e: e.tensor_tensor(out=t1_[:], in0=zim[:], in1=bim, op=ALU.mult), ["s5t6", "s5pb"], ["s5t1"])
        dv(lambda e: e.tensor_tensor(out=BT[:, :, 0:64], in0=z3(t0_), in1=z3(t1_), op=ALU.subtract), ["s5t0", "s5t1"], ["BTcat"])
        dv(lambda e: e.tensor_tensor(out=t0_[:], in0=zre[:], in1=bim, op=ALU.mult), ["s5t5", "s5pb", "BTcat"], ["s5t0"])
        dv(lambda e: e.tensor_tensor(out=t1_[:], in0=zim[:], in1=bre, op=ALU.mult), ["s5t6", "s5pb", "BTcat"], ["s5t1"])
        dv(lambda e: e.tensor_tensor(out=BT[:, :, 64:128], in0=z3(t0_), in1=z3(t1_), op=ALU.add), ["s5t0", "s5t1"], ["BTcat"])
        CT = self.sb("CTs", [128, 64, 16], F32)
        self.dma(CT[:], ctn_d[:, :, :], w=["CTs"])
        dv(lambda e: e.tensor_scalar(out=CT[64:128, :, :], in0=CT[64:128, :, :], scalar1=-1.0, scalar2=None, op0=ALU.mult), ["CTs"], ["CTs"])
        CTpad = [self.sb("CTpad%d" % i, [128, 128], BF16) for i in range(8)]
        for i in range(8):
            self.op("pool", lambda e: e.memset(CTpad[i][:], 0.0), w=["CTpad%d" % i])
        if self.stage == 31:
            self.end_phase()
            return
        uT = self.sb("uT", [128, 4, NT], BF16)
        for gt in range(4):
            self.dma(uT[:, gt, :], self.projT[gt * 128:(gt + 1) * 128, :], r=["projT"], w=["uT"])
        Sp = [self.sb("S5S%d" % i, [128, NT], BF16) for i in range(2)]
        yacc = self.sb("yacc", [128, NX], F32)
        lhsB = self.sb("lhsB", [128, 128], BF16)
        Qf = [self.sb("Qf%d" % i, [128, 128], F32) for i in range(2)]
        Pf = [self.sb("Pf%d" % i, [128, 128], F32) for i in range(2)]
        Qb = self.sb("Qb", [128, 13, 128], BF16)
        import os
        NLEV = 13
        LEVRUN = int(os.environ.get('S5_LEV', '13'))
        MAXIT = int(os.environ.get('S5_MAXIT', '64'))
        SKIPSQ = int(os.environ.get('S5_SKIPSQ', '0'))
        itc = [0]
        blocks = [(c0, min(c0 + 512, NT)) for c0 in range(0, NT, 512)]
        evi = [0]

        def evac(dst_ap, dn, src_ap, sn):
            evi[0] ^= 1
            if evi[0]:
                self.op("act", lambda e: e.copy(out=dst_ap, in_=src_ap), r=[sn], w=[dn])
            else:
                self.op("dve", lambda e: e.tensor_copy(out=dst_ap, in_=src_ap), r=[sn], w=[dn])

        for gt in range(4):
            dv(lambda e: e.tensor_scalar(out=yacc[:, :], in0=uT[:, gt, NCTX:NT], scalar1=s5D[:, gt:gt + 1], scalar2=None, op0=ALU.mult),
               ["uT", "s5Ds"], ["yacc"])
            for d in range(2):
                for gl in range(8):
                    g = gt * 8 + gl
                    dg = d * 32 + g
                    itc[0] += 1
                    if itc[0] > MAXIT:
                        continue
                    dgt = d * 4 + gt
                    dv(lambda e: e.tensor_scalar(out=lhsB[:, :], in0=BT[:, dgt, :], scalar1=gmask[:, gl:gl + 1], scalar2=None, op0=ALU.mult),
                       ["BTcat", "gmasks"], ["lhsB"])
                    self.op("pool", lambda e: e.tensor_copy(out=CTpad[gl][:, gl * 16:(gl + 1) * 16], in_=CT[:, dg, :]),
                            r=["CTs"], w=["CTpad%d" % gl])
                    tj = tmp[2]
                    dv(lambda e: e.tensor_scalar(out=tj[:, 0:128], in0=jsw[:, :], scalar1=ais[:, dg:dg + 1], scalar2=None, op0=ALU.mult),
                       ["jsws", "ais"], ["s5t2"])
                    dv(lambda e: e.scalar_tensor_tensor(out=Qf[0][:, :], in0=idf[:, :], scalar=arc[:, dg:dg + 1], in1=tj[:, 0:128],
                                                        op0=ALU.mult, op1=ALU.add), ["idf", "arc", "s5t2"], ["Qf0"])
                    dv(lambda e: e.scalar_tensor_tensor(out=Pf[0][:, :], in0=idf[:, :], scalar=arc[:, dg:dg + 1], in1=tj[:, 0:128],
                                                        op0=ALU.mult, op1=ALU.subtract), ["idf", "arc", "s5t2"], ["Pf0"])
                    self.op("act", lambda e: e.copy(out=Qb[:, 0, :], in_=Qf[0][:, :]), r=["Qf0"], w=["Qb"])
                    for m in range(1, NLEV if not SKIPSQ else 1):
                        a_, b_ = (m - 1) % 2, m % 2
                        pq, pqn = self.pbank()
                        pq2, pqn2 = self.pbank()
                        self.op("pe", lambda e: e.matmul(pq[:, 0:128], lhsT=Pf[a_][:, :], rhs=Qf[a_][:, :], start=True, stop=True),
                                r=["Pf%d" % a_, "Qf%d" % a_], w=[pqn])
                        self.op("pe", lambda e: e.matmul(pq2[:, 0:128], lhsT=Qf[a_][:, :], rhs=Pf[a_][:, :], start=True, stop=True),
                                r=["Pf%d" % a_, "Qf%d" % a_], w=[pqn2])
                        self.op("act", lambda e: e.copy(out=Qf[b_][:, :], in_=pq[:, 0:128]), r=[pqn], w=["Qf%d" % b_])
                        self.op("dve", lambda e: e.tensor_copy(out=Pf[b_][:, :], in_=pq2[:, 0:128]), r=[pqn2], w=["Pf%d" % b_])
                        self.op("dve", lambda e: e.tensor_copy(out=Qb[:, m, :], in_=pq[:, 0:128]), r=[pqn], w=["Qb"])
                    cur = 0
                    for (c0, c1) in [(0, NCTX)] + [(NCTX + i * 512, NCTX + (i + 1) * 512) for i in range(8)]:
                        n = c1 - c0
                        if d == 0:
                            o0 = c0
                        else:
                            o0 = NX + c0 if c0 < NCTX else c0 - NCTX
                        px, pxn = self.pbank()
                        self.op("pe", lambda e: e.matmul(px[:, 0:n], lhsT=lhsB[:, :], rhs=uT[:, gt, c0:c1], start=True, stop=True),
                                r=["lhsB", "uT"], w=[pxn])
                        evac(Sp[cur][:, o0:o0 + n], "S5S%d" % cur, px[:, 0:n], pxn)
                    for m in range(LEVRUN):
                        sh = 1 << m
                        nxt = 1 - cur
                        for (c0, c1) in blocks:
                            n = c1 - c0
                            pl, pln = self.pbank()
                            if d == 0:
                                lo = max(c0, sh)
                                has = lo < c1
                                self.op("pe", lambda e: e.matmul(pl[:, 0:n], lhsT=idb[:, :], rhs=Sp[cur][:, c0:c1], start=True, stop=not has),
                                        r=["idb", "S5S%d" % cur], w=[pln])
                                if has:
                                    self.op("pe", lambda e: e.matmul(pl[:, lo - c0:n], lhsT=Qb[:, m, :], rhs=Sp[cur][:, lo - sh:c1 - sh],
                                                                     start=False, stop=True), r=["Qb", "S5S%d" % cur], w=[pln])
                            else:
                                hi = min(c1, NT - sh)
                                has = hi > c0
                                self.op("pe", lambda e: e.matmul(pl[:, 0:n], lhsT=idb[:, :], rhs=Sp[cur][:, c0:c1], start=True, stop=not has),
                                        r=["idb", "S5S%d" % cur], w=[pln])
                                if has:
                                    self.op("pe", lambda e: e.matmul(pl[:, 0:hi - c0], lhsT=Qb[:, m, :], rhs=Sp[cur][:, c0 + sh:hi + sh],
                                                                     start=False, stop=True), r=["Qb", "S5S%d" % cur], w=[pln])
                            evac(Sp[nxt][:, c0:c1], "S5S%d" % nxt, pl[:, 0:n], pln)
                        cur = nxt
                    for tb in range(8):
                        c0 = (NCTX if d == 0 else 0) + tb * 512
                        py, pyn = self.pbank()
                        self.op("pe", lambda e: e.matmul(py[:, :], lhsT=CTpad[gl][:, :], rhs=Sp[cur][:, c0:c0 + 512], start=True, stop=True),
                                r=["CTpad%d" % gl, "S5S%d" % cur], w=[pyn])
                        dv(lambda e: e.tensor_tensor(out=yacc[:, tb * 512:(tb + 1) * 512], in0=py[:, :], in1=yacc[:, tb * 512:(tb + 1) * 512],
                                                     op=ALU.add), [pyn, "yacc"], ["yacc"])
            self.dma(YAs[gt * 128:(gt + 1) * 128, :], yacc[:, :], r=["yacc"], w=["YAs"])
        self.end_phase()

    def phase_C1(self):
        nc = self.nc
        idb, bct = self.idb, self.bct
        wglu_d = self.inp("s5_w_glu", [W, W])
        wproj_d = self.inp("s5_w_proj", [W, D])
        wo_d = self.inp("rwkv_w_o", [W, D])
        wout_d = self.inp("w_out", [D, D])
        g2_d = self.inp("rwkv_g2", [128, W])
        ln_d = self.inp("lnrows", [2, D])
        jrev_d = self.inp("jrev", [128, 128])
        self.x2s = x2s = self.scratch("x2s", [NX, D], F32)
        self.bcast_dram_row(0, ln_d[0:1, :], "ln_in")
        self.bcast_dram_row(1, ln_d[1:2, :], "ln_in")
        self.begin_phase()
        norm_T, hT, hTn = self.norm_tiles("c")
        NG = IN_COLS - 2304
        wgate = self.sb("wgate", [128, KT, NG], BF16)
        wv = self.w_in.rearrange("(k p) n -> p k n", p=128)
        for k in range(KT):
            for c0 in range(0, NG, NG // 2):
                self.wload(wgate[:, k, c0:c0 + NG // 2], wv[:, k, 2304 + c0:2304 + c0 + NG // 2], "wgate")
        wglu = self.sb("wglu", [128, 4, W], BF16)
        self.wload(wglu[:], wglu_d.rearrange("(k p) n -> p k n", p=128), "wglu", shape3=(4, W))
        wproj = self.sb("wproj", [128, 4, D], BF16)
        wo = self.sb("wo", [128, 4, D], BF16)
        for k in range(0, 4, 2):
            self.wload(wproj[:, k:k + 2, :], wproj_d.rearrange("(k p) n -> p k n", p=128)[:, k:k + 2, :], "wproj", shape3=(2, D))
            self.wload(wo[:, k:k + 2, :], wo_d.rearrange("(k p) n -> p k n", p=128)[:, k:k + 2, :], "wo", shape3=(2, D))
        wout = self.sb("wout", [128, KT, D], BF16)
        for k in range(0, KT, 2):
            self.wload(wout[:, k:k + 2, :], wout_d.rearrange("(k p) n -> p k n", p=128)[:, k:k + 2, :], "wout", shape3=(2, D))
        g2b = self.sb("g2b", [128, W], BF16)
        self.wload(g2b[:], g2_d[:, :], "g2b")
        jrf = self.sb("jrf", [128, 128], F32)
        jrb = self.sb("jrb", [128, 128], BF16)
        self.dma(jrf[:], jrev_d[:, :], w=["jrf"])
        self.op("dve", lambda e: e.tensor_copy(out=jrb[:], in_=jrf[:]), r=["jrf"], w=["jrb"])
        epsln = self.sb("epsln", [128, 1], F32)
        self.op("dve", lambda e: e.memset(epsln[:], 64e-5), w=["epsln"])
        xs1 = [self.sb("xs1_%d" % i, [128, D], F32) for i in range(2)]
        sgate = self.sb("sgate", [128, 2048], F32)
        sgdT = self.sb("sgdT", [128, 128], BF16)
        gsb = self.sb("gsb", [128, W], F32)
        ya4 = self.sb("ya4", [128, W], F32)
        gt1 = self.sb("gt1", [128, W], F32)
        guh = self.sb("guh", [128, W], F32)
        yag = self.sb("yag", [128, W], BF16)
        sig = self.sb("sigz", [128, W], F32)
        ya2T = self.sb("ya2T", [128, W], BF16)
        ma = self.sb("ma", [128, D], F32)
        yf = self.sb("yf", [128, W], BF16)
        yb = self.sb("yb", [128, W], BF16)
        bon = self.sb("bon", [128, W], BF16)
        ysum = self.sb("ysum", [128, W], F32)
        cen = self.sb("cen", [128, W], F32)
        sq = self.sb("sqc", [128, W], F32)
        st8 = self.sb("st8", [128, 4, 8], F32)
        ynb = self.sb("ynb", [128, W], BF16)
        ybT = self.sb("ybT", [128, W], BF16)
        tmpm = self.sb("tmpm", [128, W], F32)
        mg = self.sb("mg", [128, D], BF16)
        mT = self.sb("mT", [128, D], BF16)
        x2t = self.sb("x2t", [128, D], F32)
        GC = math.sqrt(2.0 / math.pi)

        def bc_last(ap2, n):
            return bass.AP(ap2.tensor, ap2.offset, [list(x_) for x_ in ap2.ap] + [[0, n]])

        v3 = lambda t: t[:, :].rearrange("p (h j) -> p h j", j=64)
        k3 = lambda t: t[:, :].rearrange("p (k c) -> p k c", c=128)
        YAv = self.YAs.rearrange("(k p) t -> p k t", p=128)
        for tt in range(NX // 128):
            xt0 = tt * 128
            t0 = NCTX + xt0
            xi = tt % 2
            xsn = "xs1_%d" % xi
            xs_ = xs1[xi]
            self.dma(xs_[:, :], self.x1s[t0:t0 + 128, :], r=["x1s"], w=[xsn])
            norm_T(xs_[:, :], xsn, 3, 4, 0)
            pt, pn = self.pbank()
            for k in range(KT):
                self.op("pe", lambda e: e.matmul(pt[:, 0:128], lhsT=wgate[:, k, 0:128], rhs=hT[:, k, 0:128],
                                                 start=(k == 0), stop=(k == KT - 1)), r=["wgate", hTn], w=[pn])
            self.op("act", lambda e: e.activation(out=sgdT[:, :], in_=pt[:, 0:128], func=AF.Sigmoid), r=[pn], w=["sgdT"])
            pt, pn = self.pbank()
            self.op("pe", lambda e: e.matmul(pt[:, :], lhsT=sgdT[:, :], rhs=g2b[:, :], start=True, stop=True), r=["sgdT", "g2b"], w=[pn])
            self.op("act", lambda e: e.copy(out=gsb[:, :], in_=pt[:, :]), r=[pn], w=["gsb"])
            for cb in range(4):
                pt, pn = self.pbank()
                for k in range(KT):
                    self.op("pe", lambda e: e.matmul(pt[:, :], lhsT=hT[:, k, 0:128], rhs=wgate[:, k, 128 + cb * 512:128 + (cb + 1) * 512],
                                                     start=(k == 0), stop=(k == KT - 1)), r=["wgate", hTn], w=[pn])
                self.op("act", lambda e: e.activation(out=sgate[:, cb * 512:(cb + 1) * 512], in_=pt[:, :], func=AF.Sigmoid),
                        r=[pn], w=["sgate"])
            self.dma(k3(ya4), YAv[:, :, xt0:xt0 + 128], r=["YAs"], w=["ya4"])
            self.op("pool", lambda e: e.tensor_tensor(out=gt1[:, :], in0=ya4[:, :], in1=ya4[:, :], op=ALU.mult), r=["ya4"], w=["gt1"])
            self.op("dve", lambda e: e.tensor_scalar(out=gt1[:, :], in0=gt1[:, :], scalar1=0.044715, scalar2=1.0, op0=ALU.mult, op1=ALU.add),
                    r=["gt1"], w=["gt1"])
            self.op("dve", lambda e: e.tensor_tensor(out=gt1[:, :], in0=gt1[:, :], in1=ya4[:, :], op=ALU.mult), r=["gt1", "ya4"], w=["gt1"])
            self.op("act", lambda e: e.activation(out=gt1[:, :], in_=gt1[:, :], func=AF.Tanh, scale=GC), r=["gt1"], w=["gt1"])
            self.op("pool", lambda e: e.tensor_scalar(out=guh[:, :], in0=ya4[:, :], scalar1=0.5, scalar2=None, op0=ALU.mult), r=["ya4"], w=["guh"])
            self.op("dve", lambda e: e.scalar_tensor_tensor(out=yag[:, :], in0=gt1[:, :], scalar=1.0, in1=guh[:, :], op0=ALU.add, op1=ALU.mult),
                    r=["gt1", "guh"], w=["yag"])
            for ct in range(4):
                pt, pn = self.pbank()
                for k in range(4):
                    self.op("pe", lambda e: e.matmul(pt[:, 0:128], lhsT=wglu[:, k, ct * 128:(ct + 1) * 128], rhs=k3(yag)[:, k, :],
                                                     start=(k == 0), stop=(k == 3)), r=["wglu", "yag"], w=[pn])
                self.op("act", lambda e: e.activation(out=sig[:, ct * 128:(ct + 1) * 128], in_=pt[:, 0:128], func=AF.Sigmoid), r=[pn], w=["sigz"])
            self.op("dve", lambda e: e.tensor_tensor(out=ya2T[:, :], in0=yag[:, :], in1=sig[:, :], op=ALU.mult), r=["yag", "sigz"], w=["ya2T"])
            for hlf in range(2):
                pt, pn = self.pbank()
                for k in range(4):
                    self.op("pe", lambda e: e.matmul(pt[:, :], lhsT=k3(ya2T)[:, k, :], rhs=wproj[:, k, hlf * 512:(hlf + 1) * 512],
                                                     start=(k == 0), stop=(k == 3)), r=["ya2T", "wproj"], w=[pn])
                self.op("dve", lambda e: e.tensor_tensor(out=ma[:, hlf * 512:(hlf + 1) * 512], in0=pt[:, :],
                                                         in1=sgate[:, hlf * 512:(hlf + 1) * 512], op=ALU.mult), r=[pn, "sgate"], w=["ma"])
            sA = NT + 128 - t0
            self.dma(yf[:, :], self.YTs[0, t0:t0 + 128, :], r=["YTs"], w=["yf"])
            self.dma(yb[:, :], self.YTs[1, sA:sA + 128, :], r=["YTs"], w=["yb"])
            self.dma(bon[:, :], self.BONs[t0:t0 + 128, :], r=["BONs"], w=["bon"])
            pt, pn = self.pbank()
            self.op("pe", lambda e: e.matmul(pt[:, :], lhsT=jrb[:, :], rhs=yb[:, :], start=True, stop=True), r=["jrb", "yb"], w=[pn])
            self.op("dve", lambda e: e.tensor_tensor(out=ysum[:, :], in0=pt[:, :], in1=yf[:, :], op=ALU.add), r=[pn, "yf"], w=["ysum"])
            self.op("dve", lambda e: e.tensor_reduce(out=st8[:, 0, :], in_=v3(ysum), axis=AX.X, op=ALU.add), r=["ysum"], w=["st8"])
            self.op("dve", lambda e: e.tensor_scalar(out=st8[:, 1, :], in0=st8[:, 0, :], scalar1=1.0 / 64, scalar2=None, op0=ALU.mult),
                    r=["st8"], w=["st8"])
            self.op("dve", lambda e: e.tensor_tensor(out=v3(cen), in0=v3(ysum), in1=bc_last(st8[:, 1, :], 64), op=ALU.subtract),
                    r=["ysum", "st8"], w=["cen"])
            self.op("pool", lambda e: e.tensor_tensor(out=sq[:, :], in0=cen[:, :], in1=cen[:, :], op=ALU.mult), r=["cen"], w=["sqc"])
            self.op("dve", lambda e: e.tensor_reduce(out=st8[:, 2, :], in_=v3(sq), axis=AX.X, op=ALU.add), r=["sqc", "st8"], w=["st8"])
            self.op("act", lambda e: e.activation(out=st8[:, 3, :], in_=st8[:, 2, :], func=AF.Sqrt, scale=1.0 / 64, bias=epsln[:, 0:1]),
                    r=["st8", "epsln"], w=["st8"])
            self.op("dve", lambda e: e.reciprocal(out=st8[:, 3, :], in_=st8[:, 3, :]), r=["st8"], w=["st8"])
            self.op("dve", lambda e: e.tensor_tensor(out=v3(cen), in0=v3(cen), in1=bc_last(st8[:, 3, :], 64), op=ALU.mult),
                    r=["cen", "st8"], w=["cen"])
            self.op("pool", lambda e: e.tensor_tensor(out=cen[:, :], in0=cen[:, :], in1=bct[0][:, 0:W], op=ALU.mult), r=["cen", "bc0"], w=["cen"])
            self.op("pool", lambda e: e.tensor_tensor(out=cen[:, :], in0=cen[:, :], in1=bct[1][:, 0:W], op=ALU.add), r=["cen", "bc1"], w=["cen"])
            self.op("dve", lambda e: e.tensor_tensor(out=cen[:, :], in0=cen[:, :], in1=bon[:, :], op=ALU.add), r=["cen", "bon"], w=["cen"])
            self.op("dve", lambda e: e.tensor_tensor(out=ynb[:, :], in0=cen[:, :], in1=gsb[:, :], op=ALU.mult), r=["cen", "gsb"], w=["ynb"])
            pt, pn = self.pbank()
            ptb = pt[:].bitcast(BF16)
            for k in range(4):
                self.op("pe", lambda e: e.transpose(out=ptb[:, k * 128:(k + 1) * 128], in_=ynb[:, k * 128:(k + 1) * 128], identity=idb[:]),
                        r=["ynb", "idb"], w=[pn])
            self.op("act", lambda e: e.copy(out=ybT[:, :], in_=ptb[:, 0:W]), r=[pn], w=["ybT"])
            for hlf in range(2):
                pt, pn = self.pbank()
                for k in range(4):
                    self.op("pe", lambda e: e.matmul(pt[:, :], lhsT=k3(ybT)[:, k, :], rhs=wo[:, k, hlf * 512:(hlf + 1) * 512],
                                                     start=(k == 0), stop=(k == 3)), r=["ybT", "wo"], w=[pn])
                self.op("dve", lambda e: e.tensor_tensor(out=tmpm[:, :], in0=pt[:, :], in1=sgate[:, 1024 + hlf * 512:1024 + (hlf + 1) * 512],
                                                         op=ALU.mult), r=[pn, "sgate"], w=["tmpm"])
                self.op("pool", lambda e: e.tensor_tensor(out=mg[:, hlf * 512:(hlf + 1) * 512], in0=tmpm[:, :],
                                                          in1=ma[:, hlf * 512:(hlf + 1) * 512], op=ALU.add), r=["tmpm", "ma"], w=["mg"])
            pt, pn = self.pbank()
            ptb = pt[:].bitcast(BF16)
            for k in range(KT):
                self.op("pe", lambda e: e.transpose(out=ptb[:, k * 128:(k + 1) * 128], in_=mg[:, k * 128:(k + 1) * 128], identity=idb[:]),
                        r=["mg", "idb"], w=[pn])
            self.op("act", lambda e: e.copy(out=mT[:, :], in_=ptb[:, :]), r=[pn], w=["mT"])
            for hlf in range(2):
                pt, pn = self.pbank()
                for k in range(KT):
                    self.op("pe", lambda e: e.matmul(pt[:, :], lhsT=k3(mT)[:, k, :], rhs=wout[:, k, hlf * 512:(hlf + 1) * 512],
                                                     start=(k == 0), stop=(k == KT - 1)), r=["mT", "wout"], w=[pn])
                self.op("dve", lambda e: e.tensor_tensor(out=tmpm[:, :], in0=pt[:, :], in1=bct[5][:, hlf * 512:(hlf + 1) * 512], op=ALU.mult),
                        r=[pn, "bc5"], w=["tmpm"])
                self.op("pool", lambda e: e.tensor_tensor(out=x2t[:, hlf * 512:(hlf + 1) * 512], in0=tmpm[:, :],
                                                          in1=xs_[:, hlf * 512:(hlf + 1) * 512], op=ALU.add), r=["tmpm", xsn], w=["x2t"])
            self.dma(x2s[xt0:xt0 + 128, :], x2t[:, :], r=["x2t"], w=["x2s"])
        self.end_phase()

    def phase_C2(self):
        out, x2s, bct = self.out, self.x2s, self.bct
        self.make_mod_tiles(0, 2, 2, 0, 0.5)
        self.bcast_dram_row(6, self.din["final_g"][:, :], "fg_in")
        supers = [(i * 512, 4) for i in range(NX // 512)]
        st = {}

        def epi(xs, si, t0, tt, norm_T):
            if "ot" not in st:
                st["ot"] = [self.sb("ot%d" % i, [128, D], F32) for i in range(2)]
            oi = self.rot.get("ot", 0)
            self.rot["ot"] = 1 - oi
            ot = st["ot"][oi]
            ssq = norm_T(xs[:, tt, :], "xs", None, None, 0)
            self.op("dve", lambda e: e.scalar_tensor_tensor(out=ot[:, :], in0=xs[:, tt, :], scalar=ssq[:, 2:3], in1=bct[6][:, :],
                                                            op0=ALU.mult, op1=ALU.mult), r=["xs", "ssqd", "bc6"], w=["ot%d" % oi])
            self.dma(out[t0 + tt * 128:t0 + (tt + 1) * 128, :], ot[:, :], r=["ot%d" % oi], w=["out"])

        self.ffn_phase(1, lambda t0, n: (x2s[t0:t0 + n, :], "x2s"), supers, lambda si: (0, 1, 2), epi, "d")

    def finish(self):
        self.S.drain_all("sp")
        print("ninst", self.S.ninst, "nwait", self.S.nwait)
        if self.scope is not None:
            self.scope.close()
            self.scope = None
        self.es.close()


def host_inputs(inputs, b):
    f = lambda a: np.ascontiguousarray(np.asarray(a, dtype=np.float32))
    m = {
        "x": f(inputs["x"][b]),
        "ctx": f(inputs["ctx"][b]),
        "cvec": f(np.stack([np.asarray(inputs["c"])[b], np.asarray(inputs["c_ctx"])], 0)),
        "w_mod": f(inputs["w_mod"][0]),
        "b_mod": f(np.asarray(inputs["b_mod"])[0][None, :]),
        "norm_g": f(inputs["norm_g"][0]),
        "final_g": f(np.asarray(inputs["final_g"])[None, :]),
        "ffn_w_gate": f(inputs["ffn_w_gate"][0]),
        "ffn_w_up": f(inputs["ffn_w_up"][0]),
        "ffn_w_down": f(inputs["ffn_w_down"][0]),
        "w_in": f(inputs["w_in"][0]),
        "ident": np.eye(128, dtype=np.float32),
    }
    cv = np.asarray(inputs["rwkv_conv"], np.float32)[0].reshape(9, 3, 4, 128)
    m["convw"] = f(cv.transpose(3, 1, 2, 0).reshape(128, 12, 9))
    m["w0T"] = f(np.asarray(inputs["rwkv_w0"], np.float32)[0].reshape(2, 4, 128).transpose(2, 0, 1).reshape(128, 8))
    m["a0T"] = f(np.asarray(inputs["rwkv_a0"], np.float32)[0].reshape(2, 4, 128).transpose(2, 0, 1).reshape(128, 8))
    v4 = np.stack([np.asarray(inputs[k], np.float32)[0].reshape(-1) for k in
                   ("rwkv_k_k", "rwkv_k_a", "rwkv_r_k", "rwkv_ln_g", "rwkv_ln_b")], 0)
    m["vec4"] = f(v4.reshape(5, 4, 128).transpose(2, 0, 1))
    m["w2"] = f(np.asarray(inputs["rwkv_w2"], np.float32)[0].reshape(128, 512))
    m["a2"] = f(np.asarray(inputs["rwkv_a2"], np.float32)[0].reshape(128, 512))
    blk = np.zeros((128, 128), np.float32); blk[:64, :64] = 1; blk[64:, 64:] = 1
    m["blk1"] = blk
    mk = np.zeros((16, 512), np.float32)
    for dh in range(16):
        h_ = dh % 8
        mk[dh, h_ * 64:(h_ + 1) * 64] = 1
    m["mask16"] = mk
    def big(a):
        a = np.asarray(a, np.float32)[0]
        if a.ndim == 3:
            a = np.broadcast_to(a[..., None], a.shape + (16,))
        a = a.reshape(2, 4, 8, 64, 16)
        return a.transpose(2, 4, 0, 1, 3).reshape(128, 512)
    ldt = np.broadcast_to(np.asarray(inputs["s5_log_dt"], np.float32)[:, :, :, None], (1, 2, 32, 64))
    m["s5p"] = f(np.stack([big(inputs["s5_A_re"]), big(inputs["s5_A_im"]), big(ldt), big(inputs["s5_B_re"]), big(inputs["s5_B_im"])], 1))
    def small(a):
        a = np.asarray(a, np.float32)[0].reshape(64, 64).T
        return np.concatenate([a, a], 0)
    m["s5s"] = f(np.stack([small(inputs["s5_A_re"]), small(inputs["s5_A_im"]), small(ldt)], 1))
    cre = np.asarray(inputs["s5_C_re"], np.float32)[0].reshape(64, 16, 64).transpose(2, 0, 1)
    cim = np.asarray(inputs["s5_C_im"], np.float32)[0].reshape(64, 16, 64).transpose(2, 0, 1)
    m["ctn"] = f(np.concatenate([cre, cim], 0))
    m["s5D"] = f(np.asarray(inputs["s5_D"], np.float32)[0].reshape(4, 128).T)
    js = np.zeros((128, 128), np.float32)
    for p_ in range(64):
        js[p_, 64 + p_] = 1; js[64 + p_, p_] = 1
    m["jsw"] = js
    gm = np.zeros((128, 8), np.float32)
    for p_ in range(128):
        gm[p_, p_ // 16] = 1
    m["gmask"] = gm
    m["s5_w_glu"] = f(inputs["s5_w_glu"][0])
    m["s5_w_proj"] = f(inputs["s5_w_proj"][0])
    m["rwkv_w_o"] = f(inputs["rwkv_w_o"][0])
    m["w_out"] = f(inputs["w_out"][0])
    m["rwkv_g2"] = f(inputs["rwkv_g2"][0])
    ln = np.zeros((2, 1024), np.float32)
    ln[0, :512] = np.asarray(inputs["rwkv_ln_g"], np.float32)[0]
    ln[1, :512] = np.asarray(inputs["rwkv_ln_b"], np.float32)[0]
    m["lnrows"] = ln
    m["jrev"] = np.ascontiguousarray(np.eye(128, dtype=np.float32)[::-1])
    return m


def run(inputs, stage=99, cores=8):
    kb = K(stage)
    kb.build()
    in_maps = []
    for b in range(cores):
        m = host_inputs(inputs, b)
        in_maps.append({k: v for k, v in m.items() if k in kb.din})
    res = run_bass_kernel_spmd(kb.nc, in_maps, core_ids=list(range(cores)))
    return res


def kernel(**inputs):
    res = run(inputs)
    outs = [np.asarray(r["out"], dtype=np.float32) for r in res.results]
    return np.stack(outs, 0)
```

```python
import math
import numpy as np
from contextlib import ExitStack
import concourse.bass as bass
import concourse.mybir as mybir
from concourse.bass_utils import run_bass_kernel_spmd

F32 = mybir.dt.float32
F32R = mybir.dt.float32r
BF16 = mybir.dt.bfloat16
AF = mybir.ActivationFunctionType
ALU = mybir.AluOpType
AX = mybir.AxisListType

D = 1024
FF = 2816
NFT = FF // 128
KT = D // 128
NX = 4096
NCTX = 256
NT = NX + NCTX
NTT = NT // 128
W = 512
H = 8
HD = 64
G = 32
PS = 64
EPS = 1e-6
IN_COLS = 4480


class Buf:
    __slots__ = ("name", "w", "r")

    def __init__(self, name=""):
        self.name = name
        self.w = None
        self.r = {}


class Sched:
    def __init__(self, nc, es, n_lanes=12):
        self.nc = nc
        self.eng = {"pe": nc.tensor, "act": nc.scalar, "dve": nc.vector, "pool": nc.gpsimd, "sp": nc.sync}
        self.sem = {}
        self.cnt = {}
        self.seen = {}
        for k in list(self.eng):
            self.sem[k] = es.enter_context(nc.semaphore("s_" + k))
            self.cnt[k] = 0
        self.lanes = []
        for i in range(n_lanes):
            k = "L%d" % i
            self.sem[k] = es.enter_context(nc.semaphore("s_" + k))
            self.cnt[k] = 0
            self.lanes.append(k)
        self.lane_rr = 0
        for k in self.eng:
            self.seen[k] = {}
        self.nwait = 0
        self.ninst = 0

    def _need(self, e, deps):
        best = {}
        for (src, c) in deps:
            if src == "pe" and e == "pe":
                continue
            if c > best.get(src, 0):
                best[src] = c
        for src, c in best.items():
            if self.seen[e].get(src, 0) >= c:
                continue
            self.eng[e].wait_ge(self.sem[src], c)
            self.seen[e][src] = c
            self.nwait += 1

    def _deps(self, e, reads, writes):
        deps = []
        for b in reads:
            if b.w is not None:
                deps.append(b.w)
        for b in writes:
            if b.w is not None and b.w[0] != e:
                deps.append(b.w)
            for src, c in b.r.items():
                if src != e:
                    deps.append((src, c))
        return deps

    def op(self, e, fn, reads=(), writes=()):
        self._need(e, self._deps(e, reads, writes))
        ins = fn(self.eng[e])
        self.cnt[e] += 1
        c = self.cnt[e]
        ins.then_inc(self.sem[e], 1)
        self.ninst += 1
        for b in reads:
            b.r[e] = c
        for b in writes:
            b.w = (e, c)
            b.r = {}
        return ins

    def dma(self, q, out, in_, reads=(), writes=(), slow=False):
        lane = self.lanes[self.lane_rr]
        self.lane_rr = (self.lane_rr + 1) % len(self.lanes)
        deps = self._deps(lane, reads, writes)
        if self.cnt[lane] > 0:
            deps.append((lane, self.cnt[lane]))
        self._need(q, deps)
        if slow:
            ins = self.eng[q].dma_start(out=out, in_=in_, allow_slow_non_contiguous=True)
        else:
            ins = self.eng[q].dma_start(out=out, in_=in_)
        self.cnt[lane] += 16
        c = self.cnt[lane]
        ins.then_inc(self.sem[lane], 16)
        self.ninst += 1
        for b in reads:
            b.r[lane] = c
        for b in writes:
            b.w = (lane, c)
            b.r = {}
        return ins

    def barrier(self):
        for e in self.eng:
            deps = [(k, self.cnt[k]) for k in self.cnt if k != e and self.cnt[k] > 0]
            if e != "pe" and self.cnt[e] > 0:
                deps.append((e, self.cnt[e]))
            self._need(e, deps)

    def drain_all(self, e="sp"):
        for ln in self.lanes:
            if self.cnt[ln]:
                self.eng[e].wait_ge(self.sem[ln], self.cnt[ln])
        for k in self.eng:
            if k != e and self.cnt[k]:
                self.eng[e].wait_ge(self.sem[k], self.cnt[k])


def rev(ap_):
    a = [list(x) for x in ap_.ap]
    st, n = a[-1]
    off = ap_.offset + st * (n - 1)
    a[-1] = [-st, n]
    return bass.AP(ap_.tensor, off, a)


class K:
    def __init__(self, stage=99):
        self.stage = stage
        self.nc = nc = bass.Bass("TRN2", target_bir_lowering=False)
        self.es = ExitStack()
        self.S = Sched(nc, self.es)
        self.din = {}
        self.bufs = {}
        self.rot = {}
        self.scope = None

    def inp(self, name, shape, dt=F32):
        t = self.nc.dram_tensor(name, list(shape), dt, kind="ExternalInput").ap()
        self.din[name] = t
        return t

    def scratch(self, name, shape, dt):
        t = self.nc.dram_tensor(name, list(shape), dt, kind="Internal").ap()
        self.bufs[name] = Buf(name)
        return t

    def sb(self, name, shape, dt=F32):
        es = self.scope if self.scope is not None else self.es
        self.uid = getattr(self, "uid", 0) + 1
        t = es.enter_context(self.nc.sbuf_tensor("%s_u%d" % (name, self.uid), list(shape), dt))
        self.bufs[name] = Buf(name)
        return t

    def begin_phase(self):
        assert self.scope is None
        self.scope = ExitStack()

    def end_phase(self):
        self.S.barrier()
        self.scope.close()
        self.scope = None

    def B(self, name):
        if name not in self.bufs:
            self.bufs[name] = Buf(name)
        return self.bufs[name]

    def op(self, e, fn, r=(), w=()):
        return self.S.op(e, fn, [self.B(x) if isinstance(x, str) else x for x in r],
                         [self.B(x) if isinstance(x, str) else x for x in w])

    def dma(self, out, in_, r=(), w=(), q="sp", slow=False):
        return self.S.dma(q, out, in_, [self.B(x) if isinstance(x, str) else x for x in r],
                          [self.B(x) if isinstance(x, str) else x for x in w], slow=slow)

    def pbank(self):
        i = self.rot.get("ps", 0)
        self.rot["ps"] = (i + 1) % 8
        return self.ps[i], "ps%d" % i

    def build(self):
        nc = self.nc
        x = self.inp("x", [NX, D])
        ctx = self.inp("ctx", [NCTX, D])
        cvec = self.inp("cvec", [2, D])
        w_mod = self.inp("w_mod", [D, 9 * D])
        b_mod = self.inp("b_mod", [1, 9 * D])
        norm_g = self.inp("norm_g", [3, D])
        final_g = self.inp("final_g", [1, D])
        wg = self.inp("ffn_w_gate", [2, D, FF])
        wu = self.inp("ffn_w_up", [2, D, FF])
        wd = self.inp("ffn_w_down", [2, FF, D])
        w_in = self.inp("w_in", [D, IN_COLS])
        ident = self.inp("ident", [128, 128])
        out = nc.dram_tensor("out", [NX, D], F32, kind="ExternalOutput").ap()
        self.dbg = {}

        x1s = self.scratch("x1s", [NT, D], F32)

        self.ps = [self.es.enter_context(nc.psum_tensor("ps%d" % i, [128, 512], F32)) for i in range(8)]

        idf = self.sb("idf", [128, 128], F32)
        idb = self.sb("idb", [128, 128], BF16)
        ones1 = self.sb("ones1", [1, 128], F32)
        self.dma(idf[:], ident[:, :], w=["idf"])
        self.op("dve", lambda e: e.tensor_copy(out=idb[:], in_=idf[:]), r=["idf"], w=["idb"])
        self.op("dve", lambda e: e.memset(ones1[:], 1.0), w=["ones1"])
        epsc = self.sb("epsc", [128, 1], F32)
        self.op("dve", lambda e: e.memset(epsc[:], EPS), w=["epsc"])

        NSTG = 3
        stg = [self.sb("stg%d" % i, [128, 2048], F32) for i in range(NSTG)]

        def wload(dst_ap, src_ap, dstbuf, shape3=None):
            i = self.rot.get("stg", 0)
            self.rot["stg"] = (i + 1) % NSTG
            n = 1
            for s_ in dst_ap.shape[1:]:
                n *= s_
            sv = stg[i][:, 0:n]
            if shape3 is not None:
                sv = sv.rearrange("p (a b) -> p a b", a=shape3[0])
            self.dma(sv, src_ap, w=["stg%d" % i])
            self.op("pool", lambda e: e.tensor_copy(out=dst_ap, in_=sv), r=["stg%d" % i], w=[dstbuf])

        modscr = self.scratch("modscr", [2, 9 * D], F32)
        self.begin_phase()
        cT = self.sb("cT", [128, 2, KT, 1], F32)
        for r_ in range(2):
            src = bass.AP(cvec.tensor, cvec.offset + r_ * D, [[1, 128], [128, KT], [1, 1]])
            self.dma(cT[:, r_, :, :], src, w=["cT"], slow=True)
        scT = self.sb("scT", [128, 2, KT], F32)
        self.op("act", lambda e: e.activation(out=scT[:], in_=cT[:, :, :, 0], func=AF.Silu), r=["cT"], w=["scT"])
        mrow = [self.sb("mrow%d" % i, [1, 512], F32) for i in range(4)]
        bmod = self.sb("bmod", [1, 9 * D], F32)
        self.dma(bmod[:], b_mod[:, :], w=["bmod"])
        wm_st = [self.sb("wm_st%d" % i, [128, KT, 512], F32) for i in range(2)]
        wm_v = w_mod.rearrange("(k p) n -> p k n", p=128)
        for cb in range(18):
            sl = slice(cb * 512, (cb + 1) * 512)
            st_ = wm_st[cb % 2]
            sn = "wm_st%d" % (cb % 2)
            self.dma(st_[:], wm_v[:, :, sl], w=[sn])
            for r_ in range(2):
                pt, pn = self.pbank()
                for k in range(KT):
                    self.op("pe", lambda e: e.matmul(pt[0:1, :], lhsT=scT[:, r_, k:k + 1], rhs=st_[:, k, :],
                                                     start=(k == 0), stop=(k == KT - 1)), r=["scT", sn], w=[pn])
                mi = (cb * 2 + r_) % 4
                self.op("dve", lambda e: e.tensor_tensor(out=mrow[mi][:, :], in0=pt[0:1, :], in1=bmod[:, sl], op=ALU.add),
                        r=[pn, "bmod"], w=["mrow%d" % mi])
                self.dma(modscr[r_:r_ + 1, sl], mrow[mi][:, :], r=["mrow%d" % mi], w=["modscr"])
        self.end_phase()

        NBC = 7
        bct = [self.sb("bc%d" % i, [128, D], F32) for i in range(NBC)]
        rowA = self.sb("rowA", [1, D], F32)
        rowB = self.sb("rowB", [1, D], F32)

        def bcast_row(dst_i, row_ap, rbufs):
            for hlf in range(2):
                pt, pn = self.pbank()
                self.op("pe", lambda e: e.matmul(pt[:, :], lhsT=ones1[:, :], rhs=row_ap[:, hlf * 512:(hlf + 1) * 512],
                                                 start=True, stop=True), r=["ones1"] + rbufs, w=[pn])
                self.op("act", lambda e: e.copy(out=bct[dst_i][:, hlf * 512:(hlf + 1) * 512], in_=pt[:, :]),
                        r=[pn], w=["bc%d" % dst_i])

        def bcast_dram_row(dst_i, dram_row_ap, rb):
            self.dma(rowA[:, :], dram_row_ap, r=[rb], w=["rowA"])
            bcast_row(dst_i, rowA, ["rowA"])

        def make_mod_tiles(r_, j, gi, base, mscale):
            self.dma(rowA[:, :], modscr[r_:r_ + 1, (3 * j + 1) * D:(3 * j + 2) * D], r=["modscr"], w=["rowA"])
            self.dma(rowB[:, :], norm_g[gi:gi + 1, :], w=["rowB"])
            self.op("dve", lambda e: e.scalar_tensor_tensor(out=rowA[:, :], in0=rowA[:, :], scalar=1.0, in1=rowB[:, :],
                                                            op0=ALU.add, op1=ALU.mult), r=["rowA", "rowB"], w=["rowA"])
            bcast_row(base + 0, rowA, ["rowA"])
            self.dma(rowB[:, :], modscr[r_:r_ + 1, (3 * j) * D:(3 * j + 1) * D], r=["modscr"], w=["rowB"])
            bcast_row(base + 1, rowB, ["rowB"])
            self.dma(rowA[:, :], modscr[r_:r_ + 1, (3 * j + 2) * D:(3 * j + 3) * D], r=["modscr"], w=["rowA"])
            self.op("dve", lambda e: e.tensor_scalar(out=rowA[:, :], in0=rowA[:, :], scalar1=mscale, scalar2=None, op0=ALU.mult),
                    r=["rowA"], w=["rowA"])
            bcast_row(base + 2, rowA, ["rowA"])

        self.modscr = modscr
        self.bct = bct
        self.make_mod_tiles = make_mod_tiles
        self.bcast_dram_row = bcast_dram_row
        self.idf, self.idb, self.ones1, self.epsc = idf, idb, ones1, epsc
        self.wload = wload
        self.x, self.ctx, self.out = x, ctx, out
        self.x1s = x1s
        self.wg, self.wu, self.wd, self.w_in = wg, wu, wd, w_in

        supers = [(0, 2)] + [(NCTX + i * 512, 4) for i in range(NX // 512)]
        self.supers = supers

        def src_A1(t0, n):
            if t0 < NCTX:
                return ctx[t0:t0 + n, :], None
            return x[t0 - NCTX:t0 - NCTX + n, :], None

        make_mod_tiles(1, 0, 0, 0, 0.5)
        make_mod_tiles(0, 0, 0, 3, 0.5)

        def epi_A1(xs, si, t0, tt, norm_T):
            self.dma(x1s[t0 + tt * 128:t0 + (tt + 1) * 128, :], xs[:, tt, :], r=["xs"], w=["x1s"])

        self.ffn_phase(0, src_A1, supers, lambda si: (0, 1, 2) if si == 0 else (3, 4, 5), epi_A1, "a")

        if self.stage == 1:
            self.begin_phase()
            xs = self.sb("xsd", [128, D], F32)
            for t in range(NX // 128):
                self.dma(xs[:, :], x1s[NCTX + t * 128:NCTX + (t + 1) * 128, :], r=["x1s"], w=["xsd"])
                self.dma(out[t * 128:(t + 1) * 128, :], xs[:, :], r=["xsd"], w=["out"])
            self.finish()
            return

        self.phase_A2()
        if self.stage == 21:
            self.finish()
            return
        self.phase_B1()
        if self.stage == 22:
            self.finish()
            return
        if self.stage in (3, 31):
            self.phase_S5()
            if self.stage == 31:
                self.finish()
                return
            self.dbg_copy("YAs", self.YAs, [W, NX], F32)
            self.finish()
            return
        self.phase_B2()
        if self.stage in (24, 25, 26):
            self.finish()
            return
        if self.stage == 23:
            self.finish()
            return
        self.phase_S5()
        if self.stage == 2:
            self.dbg_copy("YTs", self.YTs.rearrange("d s c -> (d s) c"), [2 * NT, W], BF16)
            for d in range(2):
                self.dbg_copy("VTs%d" % d, self.VTs[d], [NT, W], BF16)
                self.dbg_copy("AFs%d" % d, self.AFs[d], [W, NT], F32)
                self.dbg_copy("RFs%d" % d, self.RFs[d], [W, NT], F32)
                self.dbg_copy("LWs%d" % d, self.LWs[d], [W, NT], F32)
                self.dbg_copy("BFs%d" % d, self.BFs[d], [W, NT], F32)
                self.dbg_copy("KFs%d" % d, self.KFs[d], [W, NT], F32)
            self.dbg_copy("BONs", self.BONs, [NT, W], BF16)
            self.dbg_copy("projT", self.projT, [2304, NT], BF16)
            self.finish()
            return
        self.phase_C1()
        if self.stage == 4:
            self.dbg_copy("x2s", self.x2s, [NX, D], F32)
            self.finish()
            return
        self.phase_C2()
        self.finish()

    def dbg_copy(self, name, src_ap, shape, dt):
        o = self.nc.dram_tensor("dbg_" + name, list(shape), dt, kind="ExternalOutput").ap()
        self.S.barrier()
        self.dma(o, src_ap, r=[name], w=["dbg_" + name])

    def norm_tiles(self, sfx):
        hb = [self.sb("hb%d%s" % (i, sfx), [128, D], BF16) for i in range(2)]
        htmp = self.sb("htmp" + sfx, [128, D], F32)
        junk = self.sb("junk" + sfx, [128, D], BF16)
        ssq = self.sb("ssq" + sfx, [128, 4], F32)
        hT = self.sb("hT" + sfx, [128, KT, 512], BF16)
        bct, idb, epsc = self.bct, self.idb, self.epsc

        def norm_T(xs_ap, xbuf, Gi, Si, col0):
            self.op("act", lambda e: e.activation(out=junk[:], in_=xs_ap, func=AF.Square, accum_out=ssq[:, 0:1]),
                    r=[xbuf], w=["junk" + sfx, "ssq" + sfx])
            self.op("act", lambda e: e.activation(out=ssq[:, 1:2], in_=ssq[:, 0:1], func=AF.Sqrt, scale=1.0 / D, bias=epsc[:, 0:1]),
                    r=["ssq" + sfx, "epsc"], w=["ssq" + sfx])
            self.op("dve", lambda e: e.reciprocal(out=ssq[:, 2:3], in_=ssq[:, 1:2]), r=["ssq" + sfx], w=["ssq" + sfx])
            i = self.rot.get("hb", 0)
            self.rot["hb"] = 1 - i
            hbn = "hb%d%s" % (i, sfx)
            if Gi is None:
                return ssq
            self.op("dve", lambda e: e.scalar_tensor_tensor(out=htmp[:], in0=xs_ap, scalar=ssq[:, 2:3], in1=bct[Gi][:],
                                                            op0=ALU.mult, op1=ALU.mult),
                    r=[xbuf, "ssq" + sfx, "bc%d" % Gi], w=["htmp" + sfx])
            self.op("pool", lambda e: e.tensor_tensor(out=hb[i][:], in0=htmp[:], in1=bct[Si][:], op=ALU.add),
                    r=["htmp" + sfx, "bc%d" % Si], w=[hbn])
            pt, pn = self.pbank()
            ptb = pt[:].bitcast(BF16)
            for k in range(KT):
                self.op("pe", lambda e: e.transpose(out=ptb[:, k * 128:(k + 1) * 128], in_=hb[i][:, k * 128:(k + 1) * 128],
                                                    identity=idb[:]), r=[hbn, "idb"], w=[pn])
            self.op("act", lambda e: e.copy(out=hT[:, :, col0:col0 + 128],
                                            in_=ptb.rearrange("p (k c) -> p k c", k=KT)), r=[pn], w=["hT" + sfx])
            return ssq

        return norm_T, hT, "hT" + sfx

    def ffn_phase(self, li, src_of, supers, tiles_of, epilogue, sfx):
        self.begin_phase()
        bct, wload = self.bct, self.wload
        wg, wu, wd = self.wg, self.wu, self.wd
        norm_T, hT, hTn = self.norm_tiles(sfx)
        xs = self.sb("xs", [128, 4, D], F32)
        actT = self.sb("actT", [128, NFT, 512], BF16)
        wd_bf = self.sb("wd_bf", [128, NFT, D], BF16)
        FB = 256
        NFB = FF // FB
        wg_bf = [self.sb("wg_bf%d" % i, [128, KT, FB], BF16) for i in range(2)]
        wu_bf = [self.sb("wu_bf%d" % i, [128, KT, FB], BF16) for i in range(2)]
        sg = [self.sb("sg%d" % i, [128, 512], F32) for i in range(2)]
        ytmp = self.sb("ytmp", [128, 512], F32)
        wdv = wd[li].rearrange("(f p) n -> p f n", p=128)
        for f in range(0, NFT, 2):
            wload(wd_bf[:, f:f + 2, :], wdv[:, f:f + 2, :], "wd_bf", shape3=(2, D))
        wgv = wg[li].rearrange("(k p) f -> p k f", p=128)
        wuv = wu[li].rearrange("(k p) f -> p k f", p=128)
        for si, (t0, ntt) in enumerate(supers):
            ntok = ntt * 128
            Gi, Si, Mi = tiles_of(si)
            src, srcbuf = src_of(t0, ntok)
            self.dma(xs[:, 0:ntt, :], src.rearrange("(t p) d -> p t d", p=128), r=[srcbuf] if srcbuf else [], w=["xs"])
            for tt in range(ntt):
                norm_T(xs[:, tt, :], "xs", Gi, Si, tt * 128)
            for fb in range(NFB):
                bi = self.rot.get("wgu", 0)
                self.rot["wgu"] = 1 - bi
                wload(wg_bf[bi][:], wgv[:, :, fb * FB:(fb + 1) * FB], "wg_bf%d" % bi, shape3=(KT, FB))
                wload(wu_bf[bi][:], wuv[:, :, fb * FB:(fb + 1) * FB], "wu_bf%d" % bi, shape3=(KT, FB))
                for fl in range(FB // 128):
                    f = fb * (FB // 128) + fl
                    pg, pgn = self.pbank()
                    pu, pun = self.pbank()
                    for k in range(KT):
                        self.op("pe", lambda e: e.matmul(pg[:, 0:ntok], lhsT=wg_bf[bi][:, k, fl * 128:(fl + 1) * 128],
                                                         rhs=hT[:, k, 0:ntok], start=(k == 0), stop=(k == KT - 1)),
                                r=["wg_bf%d" % bi, hTn], w=[pgn])
                    for k in range(KT):
                        self.op("pe", lambda e: e.matmul(pu[:, 0:ntok], lhsT=wu_bf[bi][:, k, fl * 128:(fl + 1) * 128],
                                                         rhs=hT[:, k, 0:ntok], start=(k == 0), stop=(k == KT - 1)),
                                r=["wu_bf%d" % bi, hTn], w=[pun])
                    gi_ = self.rot.get("sg", 0)
                    self.rot["sg"] = 1 - gi_
                    self.op("act", lambda e: e.activation(out=sg[gi_][:, 0:ntok], in_=pg[:, 0:ntok], func=AF.Silu),
                            r=[pgn], w=["sg%d" % gi_])
                    self.op("dve", lambda e: e.tensor_tensor(out=actT[:, f, 0:ntok], in0=sg[gi_][:, 0:ntok],
                                                             in1=pu[:, 0:ntok], op=ALU.mult),
                            r=["sg%d" % gi_, pun], w=["actT"])
            for tt in range(ntt):
                for hlf in range(2):
                    py, pyn = self.pbank()
                    for f in range(NFT):
                        self.op("pe", lambda e: e.matmul(py[:, :], lhsT=actT[:, f, tt * 128:(tt + 1) * 128],
                                                         rhs=wd_bf[:, f, hlf * 512:(hlf + 1) * 512],
                                                         start=(f == 0), stop=(f == NFT - 1)),
                                r=["actT", "wd_bf"], w=[pyn])
                    self.op("dve", lambda e: e.tensor_tensor(out=ytmp[:], in0=py[:, :], in1=bct[Mi][:, hlf * 512:(hlf + 1) * 512],
                                                             op=ALU.mult), r=[pyn, "bc%d" % Mi], w=["ytmp"])
                    self.op("pool", lambda e: e.tensor_tensor(out=xs[:, tt, hlf * 512:(hlf + 1) * 512], in0=ytmp[:],
                                                              in1=xs[:, tt, hlf * 512:(hlf + 1) * 512], op=ALU.add),
                            r=["ytmp", "xs"], w=["xs"])
                epilogue(xs, si, t0, tt, norm_T)
        self.end_phase()

    def phase_A2(self):
        nc = self.nc
        NPC = 2304
        self.projT = projT = self.scratch("projT", [NPC, NT], BF16)
        self.make_mod_tiles(1, 1, 1, 0, 1.0)
        self.make_mod_tiles(0, 1, 1, 3, 1.0)
        self.begin_phase()
        norm_T, hT, hTn = self.norm_tiles("b")
        win_bf = self.sb("win_bf", [128, KT, NPC], BF16)
        wv = self.w_in.rearrange("(k p) n -> p k n", p=128)
        for k in range(KT):
            for c0 in range(0, NPC, 1152):
                self.wload(win_bf[:, k, c0:c0 + 1152], wv[:, k, c0:c0 + 1152], "win_bf")
        xs = self.sb("xs2", [128, 4, D], F32)
        pst = [self.sb("pst%d" % i, [128, 512], BF16) for i in range(3)]
        for si, (t0, ntt) in enumerate(self.supers):
            ntok = ntt * 128
            Gi, Si = (0, 1) if si == 0 else (3, 4)
            self.dma(xs[:, 0:ntt, :], self.x1s[t0:t0 + ntok, :].rearrange("(t p) d -> p t d", p=128), r=["x1s"], w=["xs2"])
            for tt in range(ntt):
                norm_T(xs[:, tt, :], "xs2", Gi, Si, tt * 128)
            for ct in range(NPC // 128):
                pt, pn = self.pbank()
                for k in range(KT):
                    self.op("pe", lambda e: e.matmul(pt[:, 0:ntok], lhsT=win_bf[:, k, ct * 128:(ct + 1) * 128],
                                                     rhs=hT[:, k, 0:ntok], start=(k == 0), stop=(k == KT - 1)),
                            r=["win_bf", hTn], w=[pn])
                pi = self.rot.get("pst", 0)
                self.rot["pst"] = (pi + 1) % 3
                eng = "act" if ct % 2 == 0 else "dve"
                if eng == "act":
                    self.op("act", lambda e: e.copy(out=pst[pi][:, 0:ntok], in_=pt[:, 0:ntok]), r=[pn], w=["pst%d" % pi])
                else:
                    self.op("dve", lambda e: e.tensor_copy(out=pst[pi][:, 0:ntok], in_=pt[:, 0:ntok]), r=[pn], w=["pst%d" % pi])
                self.dma(projT[ct * 128:(ct + 1) * 128, t0:t0 + ntok], pst[pi][:, 0:ntok], r=["pst%d" % pi], w=["projT"])
        self.end_phase()

    def phase_B1(self):
        nc = self.nc
        projT, idb = self.projT, self.idb
        convw = self.inp("convw", [128, 12, 9])
        w0T_d = self.inp("w0T", [128, 8])
        a0T_d = self.inp("a0T", [128, 8])
        vec4_d = self.inp("vec4", [128, 5, 4])
        w2_d = self.inp("w2", [128, W])
        a2_d = self.inp("a2", [128, W])
        blk1_d = self.inp("blk1", [128, 128])
        self.AFs = [self.scratch("AFs%d" % d, [W, NT], F32) for d in range(2)]
        self.RFs = [self.scratch("RFs%d" % d, [W, NT], F32) for d in range(2)]
        self.LWs = [self.scratch("LWs%d" % d, [W, NT], F32) for d in range(2)]
        self.BFs = [self.scratch("BFs%d" % d, [W, NT], F32) for d in range(2)]
        self.KFs = [self.scratch("KFs%d" % d, [W, NT], F32) for d in range(2)]
        self.VTs = [self.scratch("VTs%d" % d, [NT, W], BF16) for d in range(2)]
        self.BONs = self.scratch("BONs", [NT, W], BF16)
        self.begin_phase()
        cw = self.sb("cw", [128, 12, 9], F32)
        w0T = self.sb("w0Ts", [128, 8], F32)
        a0T = self.sb("a0Ts", [128, 8], F32)
        vec4 = self.sb("vec4s", [128, 5, 4], F32)
        blk1 = self.sb("blk1s", [128, 128], F32)
        w2b = self.sb("w2b", [128, W], BF16)
        a2b = self.sb("a2b", [128, W], BF16)
        self.dma(cw[:], convw[:, :, :], w=["cw"])
        self.dma(w0T[:], w0T_d[:, :], w=["w0Ts"])
        self.dma(a0T[:], a0T_d[:, :], w=["a0Ts"])
        self.dma(vec4[:], vec4_d[:, :, :], w=["vec4s"])
        self.dma(blk1[:], blk1_d[:, :], w=["blk1s"])
        self.wload(w2b[:], w2_d[:, :], "w2b")
        self.wload(a2b[:], a2_d[:, :], "a2b")
        eps12 = self.sb("eps12", [128, 1], F32)
        self.op("dve", lambda e: e.memset(eps12[:], 1e-12), w=["eps12"])
        twd = self.sb("twd", [128, NT], BF16)
        adb = self.sb("adb", [128, NT], BF16)
        self.dma(twd[:], projT[2048:2176, :], r=["projT"], w=["twd"])
        self.dma(adb[:], projT[2176:2304, :], r=["projT"], w=["adb"])
        self.op("act", lambda e: e.activation(out=twd[:], in_=twd[:], func=AF.Tanh), r=["twd"], w=["twd"])
        raw = [self.sb("raw%d" % a, [128, NT], BF16) for a in range(3)]
        cv = [self.sb("cv%d" % a, [128, NT], F32) for a in range(3)]
        NB = 256
        tnames = ["sig", "dec0", "dec1", "icl0", "icl1", "kx", "sq", "rn", "kk", "t1", "t2", "kt0", "kt1", "rvt"]
        tf = {n: self.sb("t_" + n, [128, NB], F32) for n in tnames}
        bnames = ["ab", "b0", "b1", "k0", "k1", "vb", "rvb", "bon"]
        tb_ = {n: self.sb("tb_" + n, [128, NB], BF16) for n in bnames}
        tmo = [self.sb("tmo%d" % i, [128, 2, 128], BF16) for i in range(2)]

        def s0_of(d, t0):
            if d == 0 or t0 < NCTX:
                return t0
            return NT - t0

        def emit_fm(scr, sname, d, src_ap, srcbuf, ct, t0):
            if d == 0:
                self.dma(scr[ct * 128:(ct + 1) * 128, t0:t0 + NB], src_ap, r=[srcbuf], w=[sname])
            else:
                self.op("pool", lambda e: e.tensor_copy(out=tf["rvt"][:], in_=rev(src_ap)), r=[srcbuf], w=["t_rvt"])
                s0 = s0_of(1, t0)
                self.dma(scr[ct * 128:(ct + 1) * 128, s0:s0 + NB], tf["rvt"][:], r=["t_rvt"], w=[sname])

        def emit_tm(scr, sname, d, src_t, srcbuf, ct, t0):
            src = src_t
            sb_ = srcbuf
            if d == 1:
                self.op("pool", lambda e: e.tensor_copy(out=tb_["rvb"][:], in_=rev(src_t[:, :])), r=[srcbuf], w=["tb_rvb"])
                src = tb_["rvb"]
                sb_ = "tb_rvb"
            pt, pn = self.pbank()
            ptb = pt[:].bitcast(BF16)
            for j in range(2):
                self.op("pe", lambda e: e.transpose(out=ptb[:, j * 128:(j + 1) * 128], in_=src[:, j * 128:(j + 1) * 128],
                                                    identity=idb[:]), r=[sb_, "idb"], w=[pn])
            oi = self.rot.get("tmo", 0)
            self.rot["tmo"] = 1 - oi
            self.op("act", lambda e: e.copy(out=tmo[oi][:], in_=ptb[:, 0:256].rearrange("p (j c) -> p j c", j=2)),
                    r=[pn], w=["tmo%d" % oi])
            s0 = s0_of(d, t0)
            self.dma(scr[s0:s0 + NB, ct * 128:(ct + 1) * 128].rearrange("(j p) c -> p j c", p=128), tmo[oi][:],
                     r=["tmo%d" % oi], w=[sname])

        def conv(dst, dn, src, sn, cwi):
            self.op("dve", lambda e: e.tensor_scalar(out=dst[:, :], in0=src[:, :], scalar1=cw[:, cwi, 4:5], scalar2=None,
                                                     op0=ALU.mult), r=[sn, "cw"], w=[dn])
            for tap, sh in ((3, -1), (5, 1)):
                if sh == -1:
                    o, i_ = dst[:, 1:NCTX], src[:, 0:NCTX - 1]
                else:
                    o, i_ = dst[:, 0:NCTX - 1], src[:, 1:NCTX]
                self.op("dve", lambda e: e.scalar_tensor_tensor(out=o, in0=i_, scalar=cw[:, cwi, tap:tap + 1], in1=o,
                                                                op0=ALU.mult, op1=ALU.add), r=[sn, dn, "cw"], w=[dn])
            gd = dst[:, NCTX:NT].rearrange("p (r c) -> p r c", c=64)
            gs = src[:, NCTX:NT].rearrange("p (r c) -> p r c", c=64)
            for dy in range(3):
                for dx in range(3):
                    if dy == 1 and dx == 1:
                        continue
                    oy, ox = dy - 1, dx - 1
                    r0, r1 = max(0, -oy), 64 - max(0, oy)
                    c0, c1 = max(0, -ox), 64 - max(0, ox)
                    o = gd[:, r0:r1, c0:c1]
                    i_ = gs[:, r0 + oy:r1 + oy, c0 + ox:c1 + ox]
                    tap = dy * 3 + dx
                    self.op("dve", lambda e: e.scalar_tensor_tensor(out=o, in0=i_, scalar=cw[:, cwi, tap:tap + 1], in1=o,
                                                                    op0=ALU.mult, op1=ALU.add), r=[sn, dn, "cw"], w=[dn])

        def mm_lora(wb, wbn, src, srcn, d, ct, sl):
            pt, pn = self.pbank()
            self.op("pe", lambda e: e.matmul(pt[:, 0:NB], lhsT=wb[d * 64:(d + 1) * 64, ct * 128:(ct + 1) * 128],
                                             rhs=src[d * 64:(d + 1) * 64, sl], start=True, stop=True), r=[wbn, srcn], w=[pn])
            return pt, pn

        def headsum(src_t, srcn):
            pt, pn = self.pbank()
            self.op("pe", lambda e: e.matmul(pt[:, 0:NB], lhsT=blk1[:, :], rhs=src_t[:, :], start=True, stop=True),
                    r=["blk1s", srcn], w=[pn])
            return pt, pn

        T = lambda n: tf[n]
        for ct in range(4):
            for a in range(3):
                self.dma(raw[a][:], projT[512 + a * 512 + ct * 128:512 + a * 512 + (ct + 1) * 128, :], r=["projT"], w=["raw%d" % a])
                conv(cv[a], "cv%d" % a, raw[a], "raw%d" % a, a * 4 + ct)
            rc, kc, vc = cv
            for bi in range(NT // NB):
                t0 = bi * NB
                sl = slice(t0, t0 + NB)
                for d in range(2):
                    pt, pn = mm_lora(w2b, "w2b", twd, "twd", d, ct, sl)
                    self.op("act", lambda e: e.activation(out=T("sig")[:], in_=pt[:, 0:NB], func=AF.Sigmoid,
                                                          bias=w0T[:, d * 4 + ct:d * 4 + ct + 1]), r=[pn, "w0Ts"], w=["t_sig"])
                    dn = "dec%d" % d
                    self.op("pool", lambda e: e.tensor_scalar(out=T(dn)[:], in0=T("sig")[:], scalar1=-math.exp(-0.5), scalar2=None,
                                                              op0=ALU.mult), r=["t_sig"], w=["t_" + dn])
                    emit_fm(self.LWs[d], "LWs%d" % d, d, T(dn)[:], "t_" + dn, ct, t0)
                    pt, pn = mm_lora(a2b, "a2b", adb, "adb", d, ct, sl)
                    inm = "icl%d" % d
                    self.op("act", lambda e: e.activation(out=T(inm)[:], in_=pt[:, 0:NB], func=AF.Sigmoid,
                                                          bias=a0T[:, d * 4 + ct:d * 4 + ct + 1]), r=[pn, "a0Ts"], w=["t_" + inm])
                self.op("dve", lambda e: e.tensor_scalar(out=T("kx")[:], in0=kc[:, sl], scalar1=vec4[:, 0, ct:ct + 1], scalar2=None,
                                                         op0=ALU.mult), r=["cv1", "vec4s"], w=["t_kx"])
                self.op("pool", lambda e: e.tensor_tensor(out=T("sq")[:], in0=T("kx")[:], in1=T("kx")[:], op=ALU.mult),
                        r=["t_kx"], w=["t_sq"])
                pt, pn = headsum(T("sq"), "t_sq")
                self.op("act", lambda e: e.activation(out=T("rn")[:], in_=pt[:, 0:NB], func=AF.Sqrt, bias=eps12[:, 0:1]),
                        r=[pn, "eps12"], w=["t_rn"])
                self.op("dve", lambda e: e.reciprocal(out=T("rn")[:], in_=T("rn")[:]), r=["t_rn"], w=["t_rn"])
                self.op("dve", lambda e: e.tensor_tensor(out=T("kk")[:], in0=T("kx")[:], in1=T("rn")[:], op=ALU.mult),
                        r=["t_kx", "t_rn"], w=["t_kk"])
                self.op("pool", lambda e: e.tensor_scalar(out=T("t1")[:], in0=T("kk")[:], scalar1=-1.0, scalar2=None, op0=ALU.mult),
                        r=["t_kk"], w=["t_t1"])
                for d in range(2):
                    emit_fm(self.AFs[d], "AFs%d" % d, d, T("t1")[:], "t_t1", ct, t0)
                    emit_fm(self.RFs[d], "RFs%d" % d, d, rc[:, sl], "cv0", ct, t0)
                self.op("act", lambda e: e.copy(out=tb_["vb"][:], in_=vc[:, sl]), r=["cv2"], w=["tb_vb"])
                for d in range(2):
                    emit_tm(self.VTs[d], "VTs%d" % d, d, tb_["vb"], "tb_vb", ct, t0)
                for d in range(2):
                    bn, kn, ktn, inm = "b%d" % d, "k%d" % d, "kt%d" % d, "icl%d" % d
                    self.op("dve", lambda e: e.tensor_tensor(out=T("sq")[:], in0=T("kk")[:], in1=T(inm)[:], op=ALU.mult),
                            r=["t_kk", "t_" + inm], w=["t_sq"])
                    emit_fm(self.BFs[d], "BFs%d" % d, d, T("sq")[:], "t_sq", ct, t0)
                    self.op("dve", lambda e: e.tensor_scalar(out=T("t2")[:], in0=T(inm)[:], scalar1=-1.0,
                                                             scalar2=vec4[:, 1, ct:ct + 1], op0=ALU.add, op1=ALU.mult),
                            r=["t_" + inm, "vec4s"], w=["t_t2"])
                    self.op("dve", lambda e: e.scalar_tensor_tensor(out=T(ktn)[:], in0=T("t2")[:], scalar=1.0, in1=kc[:, sl],
                                                                    op0=ALU.add, op1=ALU.mult), r=["t_t2", "cv1"], w=["t_" + ktn])
                    emit_fm(self.KFs[d], "KFs%d" % d, d, T(ktn)[:], "t_" + ktn, ct, t0)
                self.op("pool", lambda e: e.tensor_tensor(out=T("t2")[:], in0=T("kt0")[:], in1=T("kt1")[:], op=ALU.add),
                        r=["t_kt0", "t_kt1"], w=["t_t2"])
                self.op("dve", lambda e: e.scalar_tensor_tensor(out=T("sq")[:], in0=rc[:, sl], scalar=vec4[:, 2, ct:ct + 1],
                                                                in1=T("t2")[:], op0=ALU.mult, op1=ALU.mult),
                        r=["cv0", "vec4s", "t_t2"], w=["t_sq"])
                pt, pn = headsum(T("sq"), "t_sq")
                self.op("dve", lambda e: e.tensor_tensor(out=tb_["bon"][:], in0=pt[:, 0:NB], in1=vc[:, sl], op=ALU.mult),
                        r=[pn, "cv2"], w=["tb_bon"])
                emit_tm(self.BONs, "BONs", 0, tb_["bon"], "tb_bon", ct, t0)
        self.end_phase()

    def phase_B2(self):
        nc = self.nc
        idf, idb = self.idf, self.idb
        maskR_d = self.inp("maskR", [128, W])
        mlow_d = self.inp("mlow", [128, 128])
        mup_d = self.inp("mup", [128, 128])
        mupi_d = self.inp("mupi", [128, 128])
        selv_d = self.inp("selv", [16, 128])
        self.YTs = YTs = self.scratch("YTs", [2, NT, W], BF16)
        self.begin_phase()
        L = 8
        SBK = 128
        NCH = SBK // L
        import os
        NQ = NT // L if self.stage != 23 else 48
        if self.stage == 26:
            NQ = int(os.environ.get('NQDBG', '1'))
        NBLK = (NQ * L + SBK - 1) // SBK
        QY = NCTX // L
        maskR = self.sb("maskR", [128, W], F32)
        mlow = self.sb("mlow", [128, 128], F32)
        mup = self.sb("mup", [128, 128], F32)
        mupi = self.sb("mupi", [128, 128], F32)
        selvf = self.sb("selvf", [16, 128], F32)
        selv = self.sb("selv", [16, 128], BF16)
        self.dma(maskR[:], maskR_d[:, :], w=["maskR"])
        self.dma(mlow[:], mlow_d[:, :], w=["mlow"])
        self.dma(mup[:], mup_d[:, :], w=["mup"])
        self.dma(mupi[:], mupi_d[:, :], w=["mupi"])
        self.dma(selvf[:], selv_d[:, :], w=["selvf"])
        self.op("dve", lambda e: e.tensor_copy(out=selv[:], in_=selvf[:]), r=["selvf"], w=["selv"])
        segm = self.sb("segm", [128, H * SBK], F32)
        self.op("pool", lambda e: e.memset(segm[:], 1.0), w=["segm"])
        self.op("pool", lambda e: e.memset(segm[:, :].rearrange("p (n t) -> p n t", t=L)[:, :, 0:1], 0.0), w=["segm"])
        Tst = self.sb("Tst", [128, W], F32)
        Tbf = self.sb("Tbf", [128, W], BF16)
        Tw = self.sb("Tw", [128, W], F32)
        self.op("dve", lambda e: e.memset(Tst[:], 0.0), w=["Tst"])
        self.op("pool", lambda e: e.memset(Tbf[:], 0.0), w=["Tbf"])
        stn = ("a", "r", "b", "k", "lw", "cl", "e", "dd")
        stg_ = {n: self.sb("bb_" + n, [128, H, SBK], F32) for n in stn}
        bnames = ("at", "rt", "bt", "kt", "bh", "kh")
        blk = [{n: self.sb("blk%d_%s" % (i, n), [128, NCH, 128], BF16) for n in bnames} for i in range(2)]
        for i in range(2):
            for n in bnames:
                self.op("pool", lambda e: e.memset(blk[i][n][:], 0.0), w=["blk%d_%s" % (i, n)])
        PLp = [self.sb("PLp%d" % i, [128, H, NCH], F32) for i in range(2)]

        flat = lambda t: t[:, :, :].rearrange("p h s -> p (h s)")
        perm = lambda t, dd: t[dd * 64:(dd + 1) * 64, :, :].rearrange("p h (c t) -> p c h t", t=L)

        def bview(bi, n, dd):
            return blk[bi][n][dd * 64:(dd + 1) * 64, :, :].rearrange("p c (h x) -> p c h x", x=16)[:, :, :, dd * L:(dd + 1) * L]

        def prep(b):
            bi = b % 2
            s0 = b * SBK
            srcs = (("a", self.AFs, "AFs"), ("r", self.RFs, "RFs"), ("b", self.BFs, "BFs"), ("k", self.KFs, "KFs"), ("lw", self.LWs, "LWs"))
            for d in range(2):
                for (n, scr, sn) in srcs:
                    self.dma(stg_[n][d * 64:(d + 1) * 64, :, :], scr[d].rearrange("(h j) s -> j h s", j=64)[:, :, s0:s0 + SBK],
                             r=["%s%d" % (sn, d)], w=["bb_" + n])
            cl, lw, e_, dd_ = stg_["cl"], stg_["lw"], stg_["e"], stg_["dd"]
            self.op("dve", lambda e: e.tensor_tensor_scan(out=flat(cl), data0=segm[:, :], data1=flat(lw), initial=0.0,
                                                          op0=ALU.mult, op1=ALU.add), r=["segm", "bb_lw"], w=["bb_cl"])
            eng = ["dve", "pool"]
            cnt = [0]

            def prod(dst, src):
                for dd in range(2):
                    en = eng[cnt[0] % 2]
                    cnt[0] += 1
                    self.op(en, lambda e: e.tensor_tensor(out=bview(bi, dst, dd), in0=perm(stg_[src], dd), in1=perm(e_, dd), op=ALU.mult),
                            r=["bb_" + src, "bb_e"], w=["blk%d_%s" % (bi, dst)])

            self.op("pool", lambda e: e.tensor_tensor(out=flat(dd_), in0=flat(cl), in1=flat(lw), op=ALU.subtract),
                    r=["bb_cl", "bb_lw"], w=["bb_dd"])
            self.op("act", lambda e: e.activation(out=flat(e_), in_=flat(dd_), func=AF.Exp), r=["bb_dd"], w=["bb_e"])
            prod("at", "a")
            self.op("act", lambda e: e.activation(out=flat(e_), in_=flat(cl), func=AF.Exp), r=["bb_cl"], w=["bb_e"])
            prod("rt", "r")
            self.op("act", lambda e: e.activation(out=flat(e_), in_=flat(cl), func=AF.Exp, scale=-1.0), r=["bb_cl"], w=["bb_e"])
            prod("bt", "b")
            prod("kt", "k")
            cl4 = cl[:, :, :].rearrange("p h (c t) -> p h c t", t=L)
            cll = cl4[:, :, :, L - 1:L]
            cll_bc = bass.AP(cll.tensor, cll.offset, [list(x_) for x_ in cll.ap[:-1]] + [[0, L]])
            self.op("dve", lambda e: e.tensor_tensor(out=dd_[:, :, :].rearrange("p h (c t) -> p h c t", t=L), in0=cll_bc, in1=cl4,
                                                     op=ALU.subtract), r=["bb_cl"], w=["bb_dd"])
            self.op("act", lambda e: e.activation(out=flat(e_), in_=flat(dd_), func=AF.Exp), r=["bb_dd"], w=["bb_e"])
            prod("bh", "b")
            prod("kh", "k")
            self.op("act", lambda e: e.activation(out=PLp[bi][:, :, :], in_=cl4[:, :, :, L - 1], func=AF.Exp), r=["bb_cl"], w=["PLp%d" % bi])

        D4, D2 = 4, 2
        V8 = [self.sb("V8_%d" % i, [16, W], BF16) for i in range(D4)]
        Vm = [self.sb("Vm%d" % i, [128, W], BF16) for i in range(D4)]
        Makb = [self.sb("Makb%d" % i, [128, 128], BF16) for i in range(D4)]
        ArbT = [self.sb("ArbT%d" % i, [128, 128], BF16) for i in range(D4)]
        ArkT = [self.sb("ArkT%d" % i, [128, 128], BF16) for i in range(D4)]
        Nl = [self.sb("Nl%d" % i, [128, 128], F32) for i in range(D2)]
        Nu = [self.sb("Nu%d" % i, [128, 128], F32) for i in range(D2)]
        Nu2 = [self.sb("Nu2_%d" % i, [128, 128], F32) for i in range(D2)]
        N2 = [self.sb("N2_%d" % i, [128, 128], F32) for i in range(D2)]
        N4 = [self.sb("N4_%d" % i, [128, 128], F32) for i in range(D2)]
        S1 = [self.sb("S1_%d" % i, [128, 128], F32) for i in range(D2)]
        S2 = [self.sb("S2_%d" % i, [128, 128], F32) for i in range(D2)]
        MinvT = [self.sb("MinvT%d" % i, [128, 128], BF16) for i in range(D2)]
        CT = [self.sb("CT%d" % i, [128, 128], BF16) for i in range(D2)]
        BKT = [self.sb("BKT%d" % i, [128, 256], BF16) for i in range(D2)]
        Um0 = self.sb("Um0", [128, W], BF16)
        Um = self.sb("Um", [128, W], BF16)
        Yf = [self.sb("Yf%d" % i, [128, W], BF16) for i in range(2)]
        sbk = [0]

        def sbank():
            i = 4 + sbk[0]
            sbk[0] = (sbk[0] + 1) % 4
            return self.ps[i], "ps%d" % i

        def bl(q, n):
            return blk[(q // NCH) % 2][n][:, q % NCH, :], "blk%d_%s" % ((q // NCH) % 2, n)

        SLOTS = ("pe_a", "dve_a", "act_a", "pe_b", "dve_b", "pe_c", "act_b")

        def st1(q, sl):
            i4, i2 = q % D4, q % D2
            s0 = q * L
            A_, An = bl(q, "at")
            R_, Rn = bl(q, "rt")
            B_, Bn = bl(q, "bt")
            K_, Kn = bl(q, "kt")
            needy = q >= QY

            def pe_part():
                for d in range(2):
                    self.dma(V8[i4][d * L:(d + 1) * L, :], self.VTs[d][s0:s0 + L, :], r=["VTs%d" % d], w=["V8_%d" % i4])
                res = {}
                pv, pvn = sbank()
                self.op("pe", lambda e: e.matmul(pv[:, :], lhsT=selv[0:16, :], rhs=V8[i4][0:16, :], start=True, stop=True),
                        r=["selv", "V8_%d" % i4], w=[pvn])
                res["v"] = (pv, pvn)
                specs = [("n", A_, An, B_, Bn), ("nu", B_, Bn, A_, An), ("mk", A_, An, K_, Kn)]
                if needy:
                    specs += [("rb", B_, Bn, R_, Rn), ("rk", K_, Kn, R_, Rn)]
                st1.pending = (res, specs)

            def dve_part():
                res, specs = st1.pending
                pv, pvn = res["v"]
                self.op("dve", lambda e: e.tensor_tensor(out=Vm[i4][:, :], in0=pv[:, :], in1=maskR[:, :], op=ALU.mult),
                        r=[pvn, "maskR"], w=["Vm%d" % i4])
                outs = {"n": (Nl[i2], "Nl%d" % i2, mlow, "mlow"), "nu": (Nu[i2], "Nu%d" % i2, mup, "mup"),
                        "mk": (Makb[i4], "Makb%d" % i4, mlow, "mlow"), "rb": (ArbT[i4], "ArbT%d" % i4, mupi, "mupi"),
                        "rk": (ArkT[i4], "ArkT%d" % i4, mupi, "mupi")}
                for (nm, l_, ln, r_, rn) in specs:
                    pp, ppn = sbank()
                    self.op("pe", lambda e: e.matmul(pp[:, 0:128], lhsT=l_, rhs=r_, start=True, stop=True), r=[ln, rn], w=[ppn])
                    o_, on, m_, mn = outs[nm]
                    self.op("dve", lambda e: e.tensor_tensor(out=o_[:, :], in0=pp[:, 0:128], in1=m_[:, :], op=ALU.mult),
                            r=[ppn, mn], w=[on])
            sl["pe_a"].append(pe_part)
            sl["dve_a"].append(dve_part)

        def st2(q, sl):
            i2 = q % D2

            def pe_part():
                pa, pan = sbank()
                self.op("pe", lambda e: e.matmul(pa[:, 0:128], lhsT=Nl[i2][:, :], rhs=Nu[i2][:, :], start=True, stop=True),
                        r=["Nl%d" % i2, "Nu%d" % i2], w=[pan])
                self.op("act", lambda e: e.copy(out=Nu2[i2][:, :], in_=pa[:, 0:128]), r=[pan], w=["Nu2_%d" % i2])
                pb_, pbn = sbank()
                self.op("pe", lambda e: e.matmul(pb_[:, 0:128], lhsT=Nu[i2][:, :], rhs=Nl[i2][:, :], start=True, stop=True),
                        r=["Nl%d" % i2, "Nu%d" % i2], w=[pbn])
                self.op("act", lambda e: e.copy(out=N2[i2][:, :], in_=pb_[:, 0:128]), r=[pbn], w=["N2_%d" % i2])
                self.op("dve", lambda e: e.tensor_tensor(out=S1[i2][:, :], in0=Nu[i2][:, :], in1=idf[:, :], op=ALU.add),
                        r=["Nu%d" % i2, "idf"], w=["S1_%d" % i2])

            def pe_b():
                pc, pcn = sbank()
                self.op("pe", lambda e: e.matmul(pc[:, 0:128], lhsT=Nu2[i2][:, :], rhs=N2[i2][:, :], start=True, stop=True),
                        r=["Nu2_%d" % i2, "N2_%d" % i2], w=[pcn])
                self.op("act", lambda e: e.copy(out=N4[i2][:, :], in_=pc[:, 0:128]), r=[pcn], w=["N4_%d" % i2])
            sl["act_a"].append(pe_part)
            sl["pe_b"].append(pe_b)

        def st3(q, sl):
            i4, i2 = q % D4, q % D2
            Bh_, Bhn = bl(q, "bh")
            Kh_, Khn = bl(q, "kh")

            def part_a():
                pd_, pdn = sbank()
                self.op("pe", lambda e: e.matmul(pd_[:, 0:128], lhsT=N2[i2][:, :], rhs=S1[i2][:, :], start=True, stop=True),
                        r=["N2_%d" % i2, "S1_%d" % i2], w=[pdn])
                self.op("dve", lambda e: e.tensor_tensor(out=S2[i2][:, :], in0=pd_[:, 0:128], in1=S1[i2][:, :], op=ALU.add),
                        r=[pdn, "S1_%d" % i2], w=["S2_%d" % i2])

            def part_b():
                pe_, pen = sbank()
                self.op("pe", lambda e: e.matmul(pe_[:, 0:128], lhsT=N4[i2][:, :], rhs=S2[i2][:, :], start=True, stop=True),
                        r=["N4_%d" % i2, "S2_%d" % i2], w=[pen])
                self.op("dve", lambda e: e.tensor_tensor(out=MinvT[i2][:, :], in0=pe_[:, 0:128], in1=S2[i2][:, :], op=ALU.add),
                        r=[pen, "S2_%d" % i2], w=["MinvT%d" % i2])

            def part_c():
                pf, pfn = sbank()
                self.op("pe", lambda e: e.matmul(pf[:, 0:128], lhsT=Makb[i4][:, :], rhs=MinvT[i2][:, :], start=True, stop=True),
                        r=["Makb%d" % i4, "MinvT%d" % i2], w=[pfn])
                self.op("act", lambda e: e.copy(out=CT[i2][:, :], in_=pf[:, 0:128]), r=[pfn], w=["CT%d" % i2])
                pt, ptn = sbank()
                ptb = pt[:].bitcast(BF16)
                self.op("pe", lambda e: e.transpose(out=ptb[:, 0:128], in_=Bh_, identity=idb[:]), r=[Bhn, "idb"], w=[ptn])
                self.op("pe", lambda e: e.transpose(out=ptb[:, 128:256], in_=Kh_, identity=idb[:]), r=[Khn, "idb"], w=[ptn])
                self.op("act", lambda e: e.copy(out=BKT[i2][:, :], in_=ptb[:, 0:256]), r=[ptn], w=["BKT%d" % i2])
            sl["dve_a"].insert(0, part_a)
            sl["dve_b"].append(part_b)
            sl["pe_c"].append(part_c)

        def flush(sl, names):
            for n in names:
                for f in sl[n]:
                    f()
                sl[n] = []

        def new_slots():
            return {n: [] for n in SLOTS}

        prep(0)
        if self.stage == 24:
            for n in ("cl", "lw", "e", "a"):
                o = self.nc.dram_tensor("dbg_" + n, [128, H * SBK], F32, kind="ExternalOutput").ap()
                self.dma(o, flat(stg_[n]), r=["bb_" + n], w=["dbg_" + n])
            for n in ("at", "bt", "bh", "rt"):
                o = self.nc.dram_tensor("dbg_blk_" + n, [128, NCH * 128], BF16, kind="ExternalOutput").ap()
                self.dma(o, blk[0][n][:, :, :].rearrange("p c r -> p (c r)"), r=["blk0_" + n], w=["dbg_blk_" + n])
            o = self.nc.dram_tensor("dbg_PL", [128, H * NCH], F32, kind="ExternalOutput").ap()
            self.dma(o, PLp[0][:, :, :].rearrange("p h c -> p (h c)"), r=["PLp0"], w=["dbg_PL"])
            o = self.nc.dram_tensor("dbg_segm", [128, H * SBK], F32, kind="ExternalOutput").ap()
            self.dma(o, segm[:, :], r=["segm"], w=["dbg_segm"])
            self.end_phase()
            return
        if NBLK > 1:
            prep(1)
        for (fn, q_) in ((st1, 0), (st2, 0), (st3, 0), (st1, 1), (st2, 1), (st1, 2)):
            if q_ < NQ:
                sl = new_slots()
                fn(q_, sl)
                flush(sl, SLOTS)
        if self.stage == 25:
            for (n, t_, dt_) in (("Nl", Nl[0], F32), ("Nu", Nu[0], F32), ("Nu2", Nu2[0], F32), ("N2", N2[0], F32), ("N4", N4[0], F32),
                                 ("S1", S1[0], F32), ("S2", S2[0], F32), ("MinvT", MinvT[0], BF16), ("CT", CT[0], BF16),
                                 ("Makb", Makb[0], BF16), ("BKT", BKT[0], BF16), ("Vm", Vm[0], BF16)):
                shp = [128, t_[:, :].shape[1]]
                o = self.nc.dram_tensor("dbg_" + n, shp, dt_, kind="ExternalOutput").ap()
                bn_ = {"Nl": "Nl0", "Nu": "Nu0", "Nu2": "Nu2_0", "N2": "N2_0", "N4": "N4_0", "S1": "S1_0", "S2": "S2_0", "MinvT": "MinvT0",
                       "CT": "CT0", "Makb": "Makb0", "BKT": "BKT0", "Vm": "Vm0"}[n]
                self.dma(o, t_[:, :], r=[bn_], w=["dbg_" + n])
            for n in ("at", "bt", "kt", "bh", "kh", "rt"):
                o = self.nc.dram_tensor("dbg_blk_" + n, [128, NCH * 128], BF16, kind="ExternalOutput").ap()
                self.dma(o, blk[0][n][:, :, :].rearrange("p c r -> p (c r)"), r=["blk0_" + n], w=["dbg_blk_" + n])
            self.end_phase()
            return
        pu, pun = self.ps[0], "ps0"
        pU, pUn = self.ps[1], "ps1"
        pd, pdn = self.ps[2], "ps2"
        py, pyn = self.ps[3], "ps3"
        for q in range(NQ):
            b_, c_ = q // NCH, q % NCH
            if c_ == NCH - 6 and b_ + 2 < NBLK + 0 and (b_ + 2) * NCH < NQ + NCH:
                pass
            if c_ == 4 and b_ >= 1 and b_ + 1 < NBLK:
                prep(b_ + 1)
            sl = new_slots()
            if q + 3 < NQ:
                st1(q + 3, sl)
            if q + 2 < NQ:
                st2(q + 2, sl)
            if q + 1 < NQ:
                st3(q + 1, sl)
            i4, i2, yi = q % D4, q % D2, q % 2
            bi = b_ % 2
            needy = q >= QY
            A_, An = bl(q, "at")
            R_, Rn = bl(q, "rt")
            pl = PLp[bi][:, :, c_:c_ + 1]
            pl_bc = bass.AP(pl.tensor, pl.offset, [list(pl.ap[0]), list(pl.ap[1]), [0, 64]])
            self.op("pool", lambda e: e.tensor_tensor(out=Tw[:, :].rearrange("p (h j) -> p h j", j=64),
                                                      in0=Tst[:, :].rearrange("p (h j) -> p h j", j=64), in1=pl_bc, op=ALU.mult),
                    r=["Tst", "PLp%d" % bi], w=["Tw"])
            self.op("pe", lambda e: e.matmul(pu[:, :], lhsT=A_, rhs=Tbf[:, :], start=True, stop=True), r=[An, "Tbf"], w=[pun])
            if needy:
                self.op("pe", lambda e: e.matmul(py[:, :], lhsT=R_, rhs=Tbf[:, :], start=True, stop=False), r=[Rn, "Tbf"], w=[pyn])
            self.op("pe", lambda e: e.matmul(pU[:, :], lhsT=CT[i2][:, :], rhs=Vm[i4][:, :], start=True, stop=False),
                    r=["CT%d" % i2, "Vm%d" % i4], w=[pUn])
            self.op("pe", lambda e: e.matmul(pd[:, :], lhsT=BKT[i2][:, 128:256], rhs=Vm[i4][:, :], start=True, stop=False),
                    r=["BKT%d" % i2, "Vm%d" % i4], w=[pdn])
            flush(sl, ["pe_a"])
            self.op("dve", lambda e: e.tensor_tensor(out=Um0[:, :], in0=pu[:, :], in1=maskR[:, :], op=ALU.mult),
                    r=[pun, "maskR"], w=["Um0"])
            flush(sl, ["dve_a", "act_a"])
            self.op("pe", lambda e: e.matmul(pU[:, :], lhsT=MinvT[i2][:, :], rhs=Um0[:, :], start=False, stop=True),
                    r=["MinvT%d" % i2, "Um0"], w=[pUn])
            flush(sl, ["pe_b"])
            self.op("act", lambda e: e.copy(out=Um[:, :], in_=pU[:, :]), r=[pUn], w=["Um"])
            self.op("pe", lambda e: e.matmul(pd[:, :], lhsT=BKT[i2][:, 0:128], rhs=Um[:, :], start=False, stop=True),
                    r=["BKT%d" % i2, "Um"], w=[pdn])
            if needy:
                self.op("pe", lambda e: e.matmul(py[:, :], lhsT=ArbT[i4][:, :], rhs=Um[:, :], start=False, stop=False),
                        r=["ArbT%d" % i4, "Um"], w=[pyn])
                self.op("pe", lambda e: e.matmul(py[:, :], lhsT=ArkT[i4][:, :], rhs=Vm[i4][:, :], start=False, stop=True),
                        r=["ArkT%d" % i4, "Vm%d" % i4], w=[pyn])
            flush(sl, ["dve_b"])
            self.op("dve", lambda e: e.tensor_tensor(out=Tst[:, :], in0=Tw[:, :], in1=pd[:, :], op=ALU.add), r=["Tw", pdn], w=["Tst"])
            flush(sl, ["pe_c"])
            self.op("act", lambda e: e.copy(out=Tbf[:, :], in_=Tst[:, :]), r=["Tst"], w=["Tbf"])
            flush(sl, ["act_b"])
            if needy:
                self.op("act", lambda e: e.copy(out=Yf[yi][:, :], in_=py[:, :]), r=[pyn], w=["Yf%d" % yi])
                s0 = q * L
                for h in range(H):
                    for d in range(2):
                        p0 = h * 16 + d * L
                        self.dma(YTs[d, s0:s0 + L, h * 64:(h + 1) * 64], Yf[yi][p0:p0 + L, h * 64:(h + 1) * 64],
                                 r=["Yf%d" % yi], w=["YTs"])
        if self.stage == 26:
            for (n, t_, bn_, dt_) in (("Tst", Tst, "Tst", F32), ("Um", Um, "Um", BF16), ("Um0", Um0, "Um0", BF16), ("Tw", Tw, "Tw", F32)):
                o = self.nc.dram_tensor("dbg_" + n, [128, W], dt_, kind="ExternalOutput").ap()
                self.dma(o, t_[:, :], r=[bn_], w=["dbg_" + n])
        self.end_phase()

    def phase_S5(self):
        nc = self.nc
        idf, idb = self.idf, self.idb
        s5p_d = self.inp("s5p", [128, 5, 512])
        s5s_d = self.inp("s5s", [128, 3, 64])
        ctn_d = self.inp("ctn", [128, 64, 16])
        s5D_d = self.inp("s5D", [128, 4])
        jsw_d = self.inp("jsw", [128, 128])
        gmask_d = self.inp("gmask", [128, 8])
        self.YAs = YAs = self.scratch("YAs", [W, NX], F32)
        self.begin_phase()
        TWO_PI = 2.0 * math.pi
        jsw = self.sb("jsws", [128, 128], F32)
        gmask = self.sb("gmasks", [128, 8], F32)
        s5D = self.sb("s5Ds", [128, 4], F32)
        self.dma(jsw[:], jsw_d[:, :], w=["jsws"])
        self.dma(gmask[:], gmask_d[:, :], w=["gmasks"])
        self.dma(s5D[:], s5D_d[:, :], w=["s5Ds"])
        pb = self.sb("s5pb", [128, 5, 512], F32)
        psm = self.sb("s5ps", [128, 3, 64], F32)
        self.dma(pb[:], s5p_d[:, :, :], w=["s5pb"])
        self.dma(psm[:], s5s_d[:, :, :], w=["s5ps"])
        tmp = [self.sb("s5t%d" % i, [128, 512], F32) for i in range(8)]
        tmi = self.sb("s5ti", [128, 512], mybir.dt.int32)

        def dv(fn, r, w):
            self.op("dve", fn, r=r, w=w)

        def frac_sin(dst, dn, t_ap, tn, n, tA, tAn, tB, tBn, add):
            dv(lambda e: e.tensor_scalar(out=tA[:, 0:n], in0=t_ap, scalar1=add, scalar2=None, op0=ALU.add), [tn], [tAn])
            dv(lambda e: e.tensor_copy(out=tmi[:, 0:n], in_=tA[:, 0:n]), [tAn], ["s5ti"])
            dv(lambda e: e.tensor_copy(out=tB[:, 0:n], in_=tmi[:, 0:n]), ["s5ti"], [tBn])
            dv(lambda e: e.tensor_tensor(out=tA[:, 0:n], in0=tA[:, 0:n], in1=tB[:, 0:n], op=ALU.subtract), [tAn, tBn], [tAn])
            dv(lambda e: e.tensor_scalar(out=tB[:, 0:n], in0=tA[:, 0:n], scalar1=0.5, scalar2=None, op0=ALU.is_gt), [tAn], [tBn])
            dv(lambda e: e.tensor_tensor(out=tA[:, 0:n], in0=tA[:, 0:n], in1=tB[:, 0:n], op=ALU.subtract), [tAn, tBn], [tAn])
            dv(lambda e: e.tensor_scalar(out=tB[:, 0:n], in0=tA[:, 0:n], scalar1=-0.5, scalar2=None, op0=ALU.is_lt), [tAn], [tBn])
            dv(lambda e: e.tensor_tensor(out=tA[:, 0:n], in0=tA[:, 0:n], in1=tB[:, 0:n], op=ALU.add), [tAn, tBn], [tAn])
            self.op("act", lambda e: e.activation(out=dst, in_=tA[:, 0:n], func=AF.Sin, scale=TWO_PI), r=[tAn], w=[dn])

        def abar(lre, lim, ldt, srcn, n, ar, arn, ai, ain):
            t0_, t1_, t2_, t3_ = tmp[0], tmp[1], tmp[2], tmp[3]
            self.op("act", lambda e: e.activation(out=t0_[:, 0:n], in_=ldt, func=AF.Exp), r=[srcn], w=["s5t0"])
            dv(lambda e: e.tensor_tensor(out=t1_[:, 0:n], in0=t0_[:, 0:n], in1=lre, op=ALU.mult), ["s5t0", srcn], ["s5t1"])
            self.op("act", lambda e: e.activation(out=t1_[:, 0:n], in_=t1_[:, 0:n], func=AF.Exp), r=["s5t1"], w=["s5t1"])
            dv(lambda e: e.scalar_tensor_tensor(out=t0_[:, 0:n], in0=t0_[:, 0:n], scalar=1.0 / TWO_PI, in1=lim,
                                                op0=ALU.mult, op1=ALU.mult), ["s5t0", srcn], ["s5t0"])
            frac_sin(ai, ain, t0_[:, 0:n], "s5t0", n, t2_, "s5t2", t3_, "s5t3", 0.0)
            frac_sin(ar, arn, t0_[:, 0:n], "s5t0", n, t2_, "s5t2", t3_, "s5t3", 0.25)
            dv(lambda e: e.tensor_tensor(out=ai, in0=ai, in1=t1_[:, 0:n], op=ALU.mult), [ain, "s5t1"], [ain])
            dv(lambda e: e.tensor_tensor(out=ar, in0=ar, in1=t1_[:, 0:n], op=ALU.mult), [arn, "s5t1"], [arn])

        arc = self.sb("arc", [128, 64], F32)
        ais = self.sb("ais", [128, 64], F32)
        abar(psm[:, 0, :], psm[:, 1, :], psm[:, 2, :], "s5ps", 64, arc[:, :], "arc", ais[:, :], "ais")
        dv(lambda e: e.tensor_scalar(out=ais[64:128, :], in0=ais[64:128, :], scalar1=-1.0, scalar2=None, op0=ALU.mult), ["ais"], ["ais"])
        arB = self.sb("arB", [128, 512], F32)
        aiB = self.sb("aiB", [128, 512], F32)
        abar(pb[:, 0, :], pb[:, 1, :], pb[:, 2, :], "s5pb", 512, arB[:, :], "arB", aiB[:, :], "aiB")
        lre, lim, bre, bim = pb[:, 0, :], pb[:, 1, :], pb[:, 3, :], pb[:, 4, :]
        den, zre, zim, t6 = tmp[4], tmp[5], tmp[6], tmp[7]
        dv(lambda e: e.tensor_tensor(out=den[:], in0=lre, in1=lre, op=ALU.mult), ["s5pb"], ["s5t4"])
        dv(lambda e: e.tensor_tensor(out=t6[:], in0=lim, in1=lim, op=ALU.mult), ["s5pb"], ["s5t7"])
        dv(lambda e: e.tensor_tensor(out=den[:], in0=den[:], in1=t6[:], op=ALU.add), ["s5t4", "s5t7"], ["s5t4"])
        dv(lambda e: e.reciprocal(out=den[:], in_=den[:]), ["s5t4"], ["s5t4"])
        dv(lambda e: e.tensor_scalar(out=arB[:], in0=arB[:], scalar1=-1.0, scalar2=None, op0=ALU.add), ["arB"], ["arB"])
        dv(lambda e: e.tensor_tensor(out=zre[:], in0=arB[:], in1=lre, op=ALU.mult), ["arB", "s5pb"], ["s5t5"])
        dv(lambda e: e.tensor_tensor(out=t6[:], in0=aiB[:], in1=lim, op=ALU.mult), ["aiB", "s5pb"], ["s5t7"])
        dv(lambda e: e.tensor_tensor(out=zre[:], in0=zre[:], in1=t6[:], op=ALU.add), ["s5t5", "s5t7"], ["s5t5"])
        dv(lambda e: e.tensor_tensor(out=zre[:], in0=zre[:], in1=den[:], op=ALU.mult), ["s5t5", "s5t4"], ["s5t5"])
        dv(lambda e: e.tensor_tensor(out=zim[:], in0=aiB[:], in1=lre, op=ALU.mult), ["aiB", "s5pb"], ["s5t6"])
        dv(lambda e: e.tensor_tensor(out=t6[:], in0=arB[:], in1=lim, op=ALU.mult), ["arB", "s5pb"], ["s5t7"])
        dv(lambda e: e.tensor_tensor(out=zim[:], in0=zim[:], in1=t6[:], op=ALU.subtract), ["s5t6", "s5t7"], ["s5t6"])
        dv(lambda e: e.tensor_tensor(out=zim[:], in0=zim[:], in1=den[:], op=ALU.mult), ["s5t6", "s5t4"], ["s5t6"])
        BT = self.sb("BTcat", [128, 8, 128], F32)
        z3 = lambda t: t[:].rearrange("p (a b) -> p a b", a=8)
        p3 = lambda ap_: ap_.rearrange("p (a b) -> p a b", a=8)
        t0_, t1_ = tmp[0], tmp[1]
        dv(lambda e: e.tensor_tensor(out=t0_[:], in0=zre[:], in1=bre, op=ALU.mult), ["s5t5", "s5pb"], ["s5t0"])
        dv(lambda e: e.tensor_tensor(out=t1_[:], in0=zim[:], in1=bim, op=ALU.mult), ["s5t6", "s5pb"], ["s5t1"])
        dv(lambda e: e.tensor_tensor(out=BT[:, :, 0:64], in0=z3(t0_), in1=z3(t1_), op=ALU.subtract), ["s5t0", "s5t1"], ["BTcat"])
        dv(lambda e: e.tensor_tensor(out=t0_[:], in0=zre[:], in1=bim, op=ALU.mult), ["s5t5", "s5pb", "BTcat"], ["s5t0"])
        dv(lambda e: e.tensor_tensor(out=t1_[:], in0=zim[:], in1=bre, op=ALU.mult), ["s5t6", "s5pb", "BTcat"], ["s5t1"])
        dv(lambda e: e.tensor_tensor(out=BT[:, :, 64:128], in0=z3(t0_), in1=z3(t1_), op=ALU.add), ["s5t0", "s5t1"], ["BTcat"])
        CT = self.sb("CTs", [128, 64, 16], F32)
        self.dma(CT[:], ctn_d[:, :, :], w=["CTs"])
        dv(lambda e: e.tensor_scalar(out=CT[64:128, :, :], in0=CT[64:128, :, :], scalar1=-1.0, scalar2=None, op0=ALU.mult), ["CTs"], ["CTs"])
        CTpad = [self.sb("CTpad%d" % i, [128, 128], BF16) for i in range(8)]
        for i in range(8):
            self.op("pool", lambda e: e.memset(CTpad[i][:], 0.0), w=["CTpad%d" % i])
        if self.stage == 31:
            self.end_phase()
            return
        uT = self.sb("uT", [128, 4, NT], BF16)
        for gt in range(4):
            self.dma(uT[:, gt, :], self.projT[gt * 128:(gt + 1) * 128, :], r=["projT"], w=["uT"])
        Sp = [self.sb("S5S%d" % i, [128, NT], BF16) for i in range(2)]
        yacc = self.sb("yacc", [128, NX], F32)
        lhsB = self.sb("lhsB", [128, 128], BF16)
        Qf = [self.sb("Qf%d" % i, [128, 128], F32) for i in range(2)]
        Pf = [self.sb("Pf%d" % i, [128, 128], F32) for i in range(2)]
        Qb = self.sb("Qb", [128, 13, 128], BF16)
        import os
        NLEV = 13
        LEVRUN = int(os.environ.get('S5_LEV', '13'))
        MAXIT = int(os.environ.get('S5_MAXIT', '64'))
        SKIPSQ = int(os.environ.get('S5_SKIPSQ', '0'))
        itc = [0]
        blocks = [(c0, min(c0 + 512, NT)) for c0 in range(0, NT, 512)]
        evi = [0]

        def evac(dst_ap, dn, src_ap, sn):
            evi[0] ^= 1
            if evi[0]:
                self.op("act", lambda e: e.copy(out=dst_ap, in_=src_ap), r=[sn], w=[dn])
            else:
                self.op("dve", lambda e: e.tensor_copy(out=dst_ap, in_=src_ap), r=[sn], w=[dn])

        for gt in range(4):
            dv(lambda e: e.tensor_scalar(out=yacc[:, :], in0=uT[:, gt, NCTX:NT], scalar1=s5D[:, gt:gt + 1], scalar2=None, op0=ALU.mult),
               ["uT", "s5Ds"], ["yacc"])
            for d in range(2):
                for gl in range(8):
                    g = gt * 8 + gl
                    dg = d * 32 + g
                    itc[0] += 1
                    if itc[0] > MAXIT:
                        continue
                    dgt = d * 4 + gt
                    dv(lambda e: e.tensor_scalar(out=lhsB[:, :], in0=BT[:, dgt, :], scalar1=gmask[:, gl:gl + 1], scalar2=None, op0=ALU.mult),
                       ["BTcat", "gmasks"], ["lhsB"])
                    self.op("pool", lambda e: e.tensor_copy(out=CTpad[gl][:, gl * 16:(gl + 1) * 16], in_=CT[:, dg, :]),
                            r=["CTs"], w=["CTpad%d" % gl])
                    tj = tmp[2]
                    dv(lambda e: e.tensor_scalar(out=tj[:, 0:128], in0=jsw[:, :], scalar1=ais[:, dg:dg + 1], scalar2=None, op0=ALU.mult),
                       ["jsws", "ais"], ["s5t2"])
                    dv(lambda e: e.scalar_tensor_tensor(out=Qf[0][:, :], in0=idf[:, :], scalar=arc[:, dg:dg + 1], in1=tj[:, 0:128],
                                                        op0=ALU.mult, op1=ALU.add), ["idf", "arc", "s5t2"], ["Qf0"])
                    dv(lambda e: e.scalar_tensor_tensor(out=Pf[0][:, :], in0=idf[:, :], scalar=arc[:, dg:dg + 1], in1=tj[:, 0:128],
                                                        op0=ALU.mult, op1=ALU.subtract), ["idf", "arc", "s5t2"], ["Pf0"])
                    self.op("act", lambda e: e.copy(out=Qb[:, 0, :], in_=Qf[0][:, :]), r=["Qf0"], w=["Qb"])
                    for m in range(1, NLEV if not SKIPSQ else 1):
                        a_, b_ = (m - 1) % 2, m % 2
                        pq, pqn = self.pbank()
                        pq2, pqn2 = self.pbank()
                        self.op("pe", lambda e: e.matmul(pq[:, 0:128], lhsT=Pf[a_][:, :], rhs=Qf[a_][:, :], start=True, stop=True),
                                r=["Pf%d" % a_, "Qf%d" % a_], w=[pqn])
                        self.op("pe", lambda e: e.matmul(pq2[:, 0:128], lhsT=Qf[a_][:, :], rhs=Pf[a_][:, :], start=True, stop=True),
                                r=["Pf%d" % a_, "Qf%d" % a_], w=[pqn2])
                        self.op("act", lambda e: e.copy(out=Qf[b_][:, :], in_=pq[:, 0:128]), r=[pqn], w=["Qf%d" % b_])
                        self.op("dve", lambda e: e.tensor_copy(out=Pf[b_][:, :], in_=pq2[:, 0:128]), r=[pqn2], w=["Pf%d" % b_])
                        self.op("dve", lambda e: e.tensor_copy(out=Qb[:, m, :], in_=pq[:, 0:128]), r=[pqn], w=["Qb"])
                    cur = 0
                    for (c0, c1) in [(0, NCTX)] + [(NCTX + i * 512, NCTX + (i + 1) * 512) for i in range(8)]:
                        n = c1 - c0
                        if d == 0:
                            o0 = c0
                        else:
                            o0 = NX + c0 if c0 < NCTX else c0 - NCTX
                        px, pxn = self.pbank()
                        self.op("pe", lambda e: e.matmul(px[:, 0:n], lhsT=lhsB[:, :], rhs=uT[:, gt, c0:c1], start=True, stop=True),
                                r=["lhsB", "uT"], w=[pxn])
                        evac(Sp[cur][:, o0:o0 + n], "S5S%d" % cur, px[:, 0:n], pxn)
                    for m in range(LEVRUN):
                        sh = 1 << m
                        nxt = 1 - cur
                        for (c0, c1) in blocks:
                            n = c1 - c0
                            pl, pln = self.pbank()
                            if d == 0:
                                lo = max(c0, sh)
                                has = lo < c1
                                self.op("pe", lambda e: e.matmul(pl[:, 0:n], lhsT=idb[:, :], rhs=Sp[cur][:, c0:c1], start=True, stop=not has),
                                        r=["idb", "S5S%d" % cur], w=[pln])
                                if has:
                                    self.op("pe", lambda e: e.matmul(pl[:, lo - c0:n], lhsT=Qb[:, m, :], rhs=Sp[cur][:, lo - sh:c1 - sh],
                                                                     start=False, stop=True), r=["Qb", "S5S%d" % cur], w=[pln])
                            else:
                                hi = min(c1, NT - sh)
                                has = hi > c0
                                self.op("pe", lambda e: e.matmul(pl[:, 0:n], lhsT=idb[:, :], rhs=Sp[cur][:, c0:c1], start=True, stop=not has),
                                        r=["idb", "S5S%d" % cur], w=[pln])
                                if has:
                                    self.op("pe", lambda e: e.matmul(pl[:, 0:hi - c0], lhsT=Qb[:, m, :], rhs=Sp[cur][:, c0 + sh:hi + sh],
                                                                     start=False, stop=True), r=["Qb", "S5S%d" % cur], w=[pln])
                            evac(Sp[nxt][:, c0:c1], "S5S%d" % nxt, pl[:, 0:n], pln)
                        cur = nxt
                    for tb in range(8):
                        c0 = (NCTX if d == 0 else 0) + tb * 512
                        py, pyn = self.pbank()
                        self.op("pe", lambda e: e.matmul(py[:, :], lhsT=CTpad[gl][:, :], rhs=Sp[cur][:, c0:c0 + 512], start=True, stop=True),
                                r=["CTpad%d" % gl, "S5S%d" % cur], w=[pyn])
                        dv(lambda e: e.tensor_tensor(out=yacc[:, tb * 512:(tb + 1) * 512], in0=py[:, :], in1=yacc[:, tb * 512:(tb + 1) * 512],
                                                     op=ALU.add), [pyn, "yacc"], ["yacc"])
            self.dma(YAs[gt * 128:(gt + 1) * 128, :], yacc[:, :], r=["yacc"], w=["YAs"])
        self.end_phase()

    def phase_C1(self):
        nc = self.nc
        idb, bct = self.idb, self.bct
        wglu_d = self.inp("s5_w_glu", [W, W])
        wproj_d = self.inp("s5_w_proj", [W, D])
        wo_d = self.inp("rwkv_w_o", [W, D])
        wout_d = self.inp("w_out", [D, D])
        g2_d = self.inp("rwkv_g2", [128, W])
        ln_d = self.inp("lnrows", [2, D])
        jrev_d = self.inp("jrev", [128, 128])
        self.x2s = x2s = self.scratch("x2s", [NX, D], F32)
        self.bcast_dram_row(0, ln_d[0:1, :], "ln_in")
        self.bcast_dram_row(1, ln_d[1:2, :], "ln_in")
        self.begin_phase()
        norm_T, hT, hTn = self.norm_tiles("c")
        NG = IN_COLS - 2304
        wgate = self.sb("wgate", [128, KT, NG], BF16)
        wv = self.w_in.rearrange("(k p) n -> p k n", p=128)
        for k in range(KT):
            for c0 in range(0, NG, NG // 2):
                self.wload(wgate[:, k, c0:c0 + NG // 2], wv[:, k, 2304 + c0:2304 + c0 + NG // 2], "wgate")
        wglu = self.sb("wglu", [128, 4, W], BF16)
        self.wload(wglu[:], wglu_d.rearrange("(k p) n -> p k n", p=128), "wglu", shape3=(4, W))
        wproj = self.sb("wproj", [128, 4, D], BF16)
        wo = self.sb("wo", [128, 4, D], BF16)
        for k in range(0, 4, 2):
            self.wload(wproj[:, k:k + 2, :], wproj_d.rearrange("(k p) n -> p k n", p=128)[:, k:k + 2, :], "wproj", shape3=(2, D))
            self.wload(wo[:, k:k + 2, :], wo_d.rearrange("(k p) n -> p k n", p=128)[:, k:k + 2, :], "wo", shape3=(2, D))
        wout = self.sb("wout", [128, KT, D], BF16)
        for k in range(0, KT, 2):
            self.wload(wout[:, k:k + 2, :], wout_d.rearrange("(k p) n -> p k n", p=128)[:, k:k + 2, :], "wout", shape3=(2, D))
        g2b = self.sb("g2b", [128, W], BF16)
        self.wload(g2b[:], g2_d[:, :], "g2b")
        jrf = self.sb("jrf", [128, 128], F32)
        jrb = self.sb("jrb", [128, 128], BF16)
        self.dma(jrf[:], jrev_d[:, :], w=["jrf"])
        self.op("dve", lambda e: e.tensor_copy(out=jrb[:], in_=jrf[:]), r=["jrf"], w=["jrb"])
        epsln = self.sb("epsln", [128, 1], F32)
        self.op("dve", lambda e: e.memset(epsln[:], 64e-5), w=["epsln"])
        xs1 = [self.sb("xs1_%d" % i, [128, D], F32) for i in range(2)]
        sgate = self.sb("sgate", [128, 2048], F32)
        sgdT = self.sb("sgdT", [128, 128], BF16)
        gsb = self.sb("gsb", [128, W], F32)
        ya4 = self.sb("ya4", [128, W], F32)
        gt1 = self.sb("gt1", [128, W], F32)
        guh = self.sb("guh", [128, W], F32)
        yag = self.sb("yag", [128, W], BF16)
        sig = self.sb("sigz", [128, W], F32)
        ya2T = self.sb("ya2T", [128, W], BF16)
        ma = self.sb("ma", [128, D], F32)
        yf = self.sb("yf", [128, W], BF16)
        yb = self.sb("yb", [128, W], BF16)
        bon = self.sb("bon", [128, W], BF16)
        ysum = self.sb("ysum", [128, W], F32)
        cen = self.sb("cen", [128, W], F32)
        sq = self.sb("sqc", [128, W], F32)
        st8 = self.sb("st8", [128, 4, 8], F32)
        ynb = self.sb("ynb", [128, W], BF16)
        ybT = self.sb("ybT", [128, W], BF16)
        tmpm = self.sb("tmpm", [128, W], F32)
        mg = self.sb("mg", [128, D], BF16)
        mT = self.sb("mT", [128, D], BF16)
        x2t = self.sb("x2t", [128, D], F32)
        GC = math.sqrt(2.0 / math.pi)

        def bc_last(ap2, n):
            return bass.AP(ap2.tensor, ap2.offset, [list(x_) for x_ in ap2.ap] + [[0, n]])

        v3 = lambda t: t[:, :].rearrange("p (h j) -> p h j", j=64)
        k3 = lambda t: t[:, :].rearrange("p (k c) -> p k c", c=128)
        YAv = self.YAs.rearrange("(k p) t -> p k t", p=128)
        for tt in range(NX // 128):
            xt0 = tt * 128
            t0 = NCTX + xt0
            xi = tt % 2
            xsn = "xs1_%d" % xi
            xs_ = xs1[xi]
            self.dma(xs_[:, :], self.x1s[t0:t0 + 128, :], r=["x1s"], w=[xsn])
            norm_T(xs_[:, :], xsn, 3, 4, 0)
            pt, pn = self.pbank()
            for k in range(KT):
                self.op("pe", lambda e: e.matmul(pt[:, 0:128], lhsT=wgate[:, k, 0:128], rhs=hT[:, k, 0:128],
                                                 start=(k == 0), stop=(k == KT - 1)), r=["wgate", hTn], w=[pn])
            self.op("act", lambda e: e.activation(out=sgdT[:, :], in_=pt[:, 0:128], func=AF.Sigmoid), r=[pn], w=["sgdT"])
            pt, pn = self.pbank()
            self.op("pe", lambda e: e.matmul(pt[:, :], lhsT=sgdT[:, :], rhs=g2b[:, :], start=True, stop=True), r=["sgdT", "g2b"], w=[pn])
            self.op("act", lambda e: e.copy(out=gsb[:, :], in_=pt[:, :]), r=[pn], w=["gsb"])
            for cb in range(4):
                pt, pn = self.pbank()
                for k in range(KT):
                    self.op("pe", lambda e: e.matmul(pt[:, :], lhsT=hT[:, k, 0:128], rhs=wgate[:, k, 128 + cb * 512:128 + (cb + 1) * 512],
                                                     start=(k == 0), stop=(k == KT - 1)), r=["wgate", hTn], w=[pn])
                self.op("act", lambda e: e.activation(out=sgate[:, cb * 512:(cb + 1) * 512], in_=pt[:, :], func=AF.Sigmoid),
                        r=[pn], w=["sgate"])
            self.dma(k3(ya4), YAv[:, :, xt0:xt0 + 128], r=["YAs"], w=["ya4"])
            self.op("pool", lambda e: e.tensor_tensor(out=gt1[:, :], in0=ya4[:, :], in1=ya4[:, :], op=ALU.mult), r=["ya4"], w=["gt1"])
            self.op("dve", lambda e: e.tensor_scalar(out=gt1[:, :], in0=gt1[:, :], scalar1=0.044715, scalar2=1.0, op0=ALU.mult, op1=ALU.add),
                    r=["gt1"], w=["gt1"])
            self.op("dve", lambda e: e.tensor_tensor(out=gt1[:, :], in0=gt1[:, :], in1=ya4[:, :], op=ALU.mult), r=["gt1", "ya4"], w=["gt1"])
            self.op("act", lambda e: e.activation(out=gt1[:, :], in_=gt1[:, :], func=AF.Tanh, scale=GC), r=["gt1"], w=["gt1"])
            self.op("pool", lambda e: e.tensor_scalar(out=guh[:, :], in0=ya4[:, :], scalar1=0.5, scalar2=None, op0=ALU.mult), r=["ya4"], w=["guh"])
            self.op("dve", lambda e: e.scalar_tensor_tensor(out=yag[:, :], in0=gt1[:, :], scalar=1.0, in1=guh[:, :], op0=ALU.add, op1=ALU.mult),
                    r=["gt1", "guh"], w=["yag"])
            for ct in range(4):
                pt, pn = self.pbank()
                for k in range(4):
                    self.op("pe", lambda e: e.matmul(pt[:, 0:128], lhsT=wglu[:, k, ct * 128:(ct + 1) * 128], rhs=k3(yag)[:, k, :],
                                                     start=(k == 0), stop=(k == 3)), r=["wglu", "yag"], w=[pn])
                self.op("act", lambda e: e.activation(out=sig[:, ct * 128:(ct + 1) * 128], in_=pt[:, 0:128], func=AF.Sigmoid), r=[pn], w=["sigz"])
            self.op("dve", lambda e: e.tensor_tensor(out=ya2T[:, :], in0=yag[:, :], in1=sig[:, :], op=ALU.mult), r=["yag", "sigz"], w=["ya2T"])
            for hlf in range(2):
                pt, pn = self.pbank()
                for k in range(4):
                    self.op("pe", lambda e: e.matmul(pt[:, :], lhsT=k3(ya2T)[:, k, :], rhs=wproj[:, k, hlf * 512:(hlf + 1) * 512],
                                                     start=(k == 0), stop=(k == 3)), r=["ya2T", "wproj"], w=[pn])
                self.op("dve", lambda e: e.tensor_tensor(out=ma[:, hlf * 512:(hlf + 1) * 512], in0=pt[:, :],
                                                         in1=sgate[:, hlf * 512:(hlf + 1) * 512], op=ALU.mult), r=[pn, "sgate"], w=["ma"])
            sA = NT + 128 - t0
            self.dma(yf[:, :], self.YTs[0, t0:t0 + 128, :], r=["YTs"], w=["yf"])
            self.dma(yb[:, :], self.YTs[1, sA:sA + 128, :], r=["YTs"], w=["yb"])
            self.dma(bon[:, :], self.BONs[t0:t0 + 128, :], r=["BONs"], w=["bon"])
            pt, pn = self.pbank()
            self.op("pe", lambda e: e.matmul(pt[:, :], lhsT=jrb[:, :], rhs=yb[:, :], start=True, stop=True), r=["jrb", "yb"], w=[pn])
            self.op("dve", lambda e: e.tensor_tensor(out=ysum[:, :], in0=pt[:, :], in1=yf[:, :], op=ALU.add), r=[pn, "yf"], w=["ysum"])
            self.op("dve", lambda e: e.tensor_reduce(out=st8[:, 0, :], in_=v3(ysum), axis=AX.X, op=ALU.add), r=["ysum"], w=["st8"])
            self.op("dve", lambda e: e.tensor_scalar(out=st8[:, 1, :], in0=st8[:, 0, :], scalar1=1.0 / 64, scalar2=None, op0=ALU.mult),
                    r=["st8"], w=["st8"])
            self.op("dve", lambda e: e.tensor_tensor(out=v3(cen), in0=v3(ysum), in1=bc_last(st8[:, 1, :], 64), op=ALU.subtract),
                    r=["ysum", "st8"], w=["cen"])
            self.op("pool", lambda e: e.tensor_tensor(out=sq[:, :], in0=cen[:, :], in1=cen[:, :], op=ALU.mult), r=["cen"], w=["sqc"])
            self.op("dve", lambda e: e.tensor_reduce(out=st8[:, 2, :], in_=v3(sq), axis=AX.X, op=ALU.add), r=["sqc", "st8"], w=["st8"])
            self.op("act", lambda e: e.activation(out=st8[:, 3, :], in_=st8[:, 2, :], func=AF.Sqrt, scale=1.0 / 64, bias=epsln[:, 0:1]),
                    r=["st8", "epsln"], w=["st8"])
            self.op("dve", lambda e: e.reciprocal(out=st8[:, 3, :], in_=st8[:, 3, :]), r=["st8"], w=["st8"])
            self.op("dve", lambda e: e.tensor_tensor(out=v3(cen), in0=v3(cen), in1=bc_last(st8[:, 3, :], 64), op=ALU.mult),
                    r=["cen", "st8"], w=["cen"])
            self.op("pool", lambda e: e.tensor_tensor(out=cen[:, :], in0=cen[:, :], in1=bct[0][:, 0:W], op=ALU.mult), r=["cen", "bc0"], w=["cen"])
            self.op("pool", lambda e: e.tensor_tensor(out=cen[:, :], in0=cen[:, :], in1=bct[1][:, 0:W], op=ALU.add), r=["cen", "bc1"], w=["cen"])
            self.op("dve", lambda e: e.tensor_tensor(out=cen[:, :], in0=cen[:, :], in1=bon[:, :], op=ALU.add), r=["cen", "bon"], w=["cen"])
            self.op("dve", lambda e: e.tensor_tensor(out=ynb[:, :], in0=cen[:, :], in1=gsb[:, :], op=ALU.mult), r=["cen", "gsb"], w=["ynb"])
            pt, pn = self.pbank()
            ptb = pt[:].bitcast(BF16)
            for k in range(4):
                self.op("pe", lambda e: e.transpose(out=ptb[:, k * 128:(k + 1) * 128], in_=ynb[:, k * 128:(k + 1) * 128], identity=idb[:]),
                        r=["ynb", "idb"], w=[pn])
            self.op("act", lambda e: e.copy(out=ybT[:, :], in_=ptb[:, 0:W]), r=[pn], w=["ybT"])
            for hlf in range(2):
                pt, pn = self.pbank()
                for k in range(4):
                    self.op("pe", lambda e: e.matmul(pt[:, :], lhsT=k3(ybT)[:, k, :], rhs=wo[:, k, hlf * 512:(hlf + 1) * 512],
                                                     start=(k == 0), stop=(k == 3)), r=["ybT", "wo"], w=[pn])
                self.op("dve", lambda e: e.tensor_tensor(out=tmpm[:, :], in0=pt[:, :], in1=sgate[:, 1024 + hlf * 512:1024 + (hlf + 1) * 512],
                                                         op=ALU.mult), r=[pn, "sgate"], w=["tmpm"])
                self.op("pool", lambda e: e.tensor_tensor(out=mg[:, hlf * 512:(hlf + 1) * 512], in0=tmpm[:, :],
                                                          in1=ma[:, hlf * 512:(hlf + 1) * 512], op=ALU.add), r=["tmpm", "ma"], w=["mg"])
            pt, pn = self.pbank()
            ptb = pt[:].bitcast(BF16)
            for k in range(KT):
                self.op("pe", lambda e: e.transpose(out=ptb[:, k * 128:(k + 1) * 128], in_=mg[:, k * 128:(k + 1) * 128], identity=idb[:]),
                        r=["mg", "idb"], w=[pn])
            self.op("act", lambda e: e.copy(out=mT[:, :], in_=ptb[:, :]), r=[pn], w=["mT"])
            for hlf in range(2):
                pt, pn = self.pbank()
                for k in range(KT):
                    self.op("pe", lambda e: e.matmul(pt[:, :], lhsT=k3(mT)[:, k, :], rhs=wout[:, k, hlf * 512:(hlf + 1) * 512],
                                                     start=(k == 0), stop=(k == KT - 1)), r=["mT", "wout"], w=[pn])
                self.op("dve", lambda e: e.tensor_tensor(out=tmpm[:, :], in0=pt[:, :], in1=bct[5][:, hlf * 512:(hlf + 1) * 512], op=ALU.mult),
                        r=[pn, "bc5"], w=["tmpm"])
                self.op("pool", lambda e: e.tensor_tensor(out=x2t[:, hlf * 512:(hlf + 1) * 512], in0=tmpm[:, :],
                                                          in1=xs_[:, hlf * 512:(hlf + 1) * 512], op=ALU.add), r=["tmpm", xsn], w=["x2t"])
            self.dma(x2s[xt0:xt0 + 128, :], x2t[:, :], r=["x2t"], w=["x2s"])
        self.end_phase()

    def phase_C2(self):
        out, x2s, bct = self.out, self.x2s, self.bct
        self.make_mod_tiles(0, 2, 2, 0, 0.5)
        self.bcast_dram_row(6, self.din["final_g"][:, :], "fg_in")
        supers = [(i * 512, 4) for i in range(NX // 512)]
        st = {}

        def epi(xs, si, t0, tt, norm_T):
            if "ot" not in st:
                st["ot"] = [self.sb("ot%d" % i, [128, D], F32) for i in range(2)]
            oi = self.rot.get("ot", 0)
            self.rot["ot"] = 1 - oi
            ot = st["ot"][oi]
            ssq = norm_T(xs[:, tt, :], "xs", None, None, 0)
            self.op("dve", lambda e: e.scalar_tensor_tensor(out=ot[:, :], in0=xs[:, tt, :], scalar=ssq[:, 2:3], in1=bct[6][:, :],
                                                            op0=ALU.mult, op1=ALU.mult), r=["xs", "ssqd", "bc6"], w=["ot%d" % oi])
            self.dma(out[t0 + tt * 128:t0 + (tt + 1) * 128, :], ot[:, :], r=["ot%d" % oi], w=["out"])

        self.ffn_phase(1, lambda t0, n: (x2s[t0:t0 + n, :], "x2s"), supers, lambda si: (0, 1, 2), epi, "d")

    def finish(self):
        self.S.drain_all("sp")
        print("ninst", self.S.ninst, "nwait", self.S.nwait)
        if self.scope is not None:
            self.scope.close()
            self.scope = None
        self.es.close()


def host_inputs(inputs, b):
    f = lambda a: np.ascontiguousarray(np.asarray(a, dtype=np.float32))
    m = {
        "x": f(inputs["x"][b]),
        "ctx": f(inputs["ctx"][b]),
        "cvec": f(np.stack([np.asarray(inputs["c"])[b], np.asarray(inputs["c_ctx"])], 0)),
        "w_mod": f(inputs["w_mod"][0]),
        "b_mod": f(np.asarray(inputs["b_mod"])[0][None, :]),
        "norm_g": f(inputs["norm_g"][0]),
        "final_g": f(np.asarray(inputs["final_g"])[None, :]),
        "ffn_w_gate": f(inputs["ffn_w_gate"][0]),
        "ffn_w_up": f(inputs["ffn_w_up"][0]),
        "ffn_w_down": f(inputs["ffn_w_down"][0]),
        "w_in": f(inputs["w_in"][0]),
        "ident": np.eye(128, dtype=np.float32),
    }
    cv = np.asarray(inputs["rwkv_conv"], np.float32)[0].reshape(9, 3, 4, 128)
    m["convw"] = f(cv.transpose(3, 1, 2, 0).reshape(128, 12, 9))
    m["w0T"] = f(np.asarray(inputs["rwkv_w0"], np.float32)[0].reshape(2, 4, 128).transpose(2, 0, 1).reshape(128, 8))
    m["a0T"] = f(np.asarray(inputs["rwkv_a0"], np.float32)[0].reshape(2, 4, 128).transpose(2, 0, 1).reshape(128, 8))
    v4 = np.stack([np.asarray(inputs[k], np.float32)[0].reshape(-1) for k in
                   ("rwkv_k_k", "rwkv_k_a", "rwkv_r_k", "rwkv_ln_g", "rwkv_ln_b")], 0)
    m["vec4"] = f(v4.reshape(5, 4, 128).transpose(2, 0, 1))
    m["w2"] = f(np.asarray(inputs["rwkv_w2"], np.float32)[0].reshape(128, 512))
    m["a2"] = f(np.asarray(inputs["rwkv_a2"], np.float32)[0].reshape(128, 512))
    blk = np.zeros((128, 128), np.float32); blk[:64, :64] = 1; blk[64:, 64:] = 1
    m["blk1"] = blk
    mk = np.zeros((16, 512), np.float32)
    for dh in range(16):
        h_ = dh % 8
        mk[dh, h_ * 64:(h_ + 1) * 64] = 1
    m["mask16"] = mk
    rr = np.arange(128)
    hh, dd, tt_ = rr // 16, (rr // 8) % 2, rr % 8
    mR = np.zeros((128, 512), np.float32)
    for r_ in range(128):
        mR[r_, hh[r_] * 64:(hh[r_] + 1) * 64] = 1
    m["maskR"] = mR
    same = (rr[:, None] // 8) == (rr[None, :] // 8)
    m["mlow"] = (same & (tt_[:, None] > tt_[None, :])).astype(np.float32)
    m["mup"] = (same & (tt_[:, None] < tt_[None, :])).astype(np.float32)
    m["mupi"] = (same & (tt_[:, None] <= tt_[None, :])).astype(np.float32)
    sv = np.zeros((16, 128), np.float32)
    for r_ in range(128):
        sv[dd[r_] * 8 + tt_[r_], r_] = 1
    m["selv"] = sv
    def big(a):
        a = np.asarray(a, np.float32)[0]
        if a.ndim == 3:
            a = np.broadcast_to(a[..., None], a.shape + (16,))
        a = a.reshape(2, 4, 8, 64, 16)
        return a.transpose(2, 4, 0, 1, 3).reshape(128, 512)
    ldt = np.broadcast_to(np.asarray(inputs["s5_log_dt"], np.float32)[:, :, :, None], (1, 2, 32, 64))
    m["s5p"] = f(np.stack([big(inputs["s5_A_re"]), big(inputs["s5_A_im"]), big(ldt), big(inputs["s5_B_re"]), big(inputs["s5_B_im"])], 1))
    def small(a):
        a = np.asarray(a, np.float32)[0].reshape(64, 64).T
        return np.concatenate([a, a], 0)
    m["s5s"] = f(np.stack([small(inputs["s5_A_re"]), small(inputs["s5_A_im"]), small(ldt)], 1))
    cre = np.asarray(inputs["s5_C_re"], np.float32)[0].reshape(64, 16, 64).transpose(2, 0, 1)
    cim = np.asarray(inputs["s5_C_im"], np.float32)[0].reshape(64, 16, 64).transpose(2, 0, 1)
    m["ctn"] = f(np.concatenate([cre, cim], 0))
    m["s5D"] = f(np.asarray(inputs["s5_D"], np.float32)[0].reshape(4, 128).T)
    js = np.zeros((128, 128), np.float32)
    for p_ in range(64):
        js[p_, 64 + p_] = 1; js[64 + p_, p_] = 1
    m["jsw"] = js
    gm = np.zeros((128, 8), np.float32)
    for p_ in range(128):
        gm[p_, p_ // 16] = 1
    m["gmask"] = gm
    m["s5_w_glu"] = f(inputs["s5_w_glu"][0])
    m["s5_w_proj"] = f(inputs["s5_w_proj"][0])
    m["rwkv_w_o"] = f(inputs["rwkv_w_o"][0])
    m["w_out"] = f(inputs["w_out"][0])
    m["rwkv_g2"] = f(inputs["rwkv_g2"][0])
    ln = np.zeros((2, 1024), np.float32)
    ln[0, :512] = np.asarray(inputs["rwkv_ln_g"], np.float32)[0]
    ln[1, :512] = np.asarray(inputs["rwkv_ln_b"], np.float32)[0]
    m["lnrows"] = ln
    m["jrev"] = np.ascontiguousarray(np.eye(128, dtype=np.float32)[::-1])
    return m


def run(inputs, stage=99, cores=8):
    kb = K(stage)
    kb.build()
    in_maps = []
    for b in range(cores):
        m = host_inputs(inputs, b)
        in_maps.append({k: v for k, v in m.items() if k in kb.din})
    res = run_bass_kernel_spmd(kb.nc, in_maps, core_ids=list(range(cores)))
    return res


def kernel(**inputs):
    res = run(inputs)
    outs = [np.asarray(r["out"], dtype=np.float32) for r in res.results]
    return np.stack(outs, 0)
```

```python
import math
import numpy as np
from contextlib import ExitStack
import concourse.bass as bass
import concourse.mybir as mybir
from concourse.bass_utils import run_bass_kernel_spmd

F32 = mybir.dt.float32
F32R = mybir.dt.float32r
BF16 = mybir.dt.bfloat16
AF = mybir.ActivationFunctionType
ALU = mybir.AluOpType
AX = mybir.AxisListType

D = 1024
FF = 2816
NFT = FF // 128
KT = D // 128
NX = 4096
NCTX = 256
NT = NX + NCTX
NTT = NT // 128
W = 512
H = 8
HD = 64
G = 32
PS = 64
EPS = 1e-6
IN_COLS = 4480


class Buf:
    __slots__ = ("name", "w", "r")

    def __init__(self, name=""):
        self.name = name
        self.w = None
        self.r = {}


class Sched:
    def __init__(self, nc, es, n_lanes=12):
        self.nc = nc
        self.eng = {"pe": nc.tensor, "act": nc.scalar, "dve": nc.vector, "pool": nc.gpsimd, "sp": nc.sync}
        self.sem = {}
        self.cnt = {}
        self.seen = {}
        for k in list(self.eng):
            self.sem[k] = es.enter_context(nc.semaphore("s_" + k))
            self.cnt[k] = 0
        self.lanes = []
        for i in range(n_lanes):
            k = "L%d" % i
            self.sem[k] = es.enter_context(nc.semaphore("s_" + k))
            self.cnt[k] = 0
            self.lanes.append(k)
        self.lane_rr = 0
        for k in self.eng:
            self.seen[k] = {}
        self.nwait = 0
        self.ninst = 0

    def _need(self, e, deps):
        best = {}
        for (src, c) in deps:
            if src == "pe" and e == "pe":
                continue
            if c > best.get(src, 0):
                best[src] = c
        for src, c in best.items():
            if self.seen[e].get(src, 0) >= c:
                continue
            self.eng[e].wait_ge(self.sem[src], c)
            self.seen[e][src] = c
            self.nwait += 1

    def _deps(self, e, reads, writes):
        deps = []
        for b in reads:
            if b.w is not None:
                deps.append(b.w)
        for b in writes:
            if b.w is not None and b.w[0] != e:
                deps.append(b.w)
            for src, c in b.r.items():
                if src != e:
                    deps.append((src, c))
        return deps

    def op(self, e, fn, reads=(), writes=()):
        self._need(e, self._deps(e, reads, writes))
        ins = fn(self.eng[e])
        self.cnt[e] += 1
        c = self.cnt[e]
        ins.then_inc(self.sem[e], 1)
        self.ninst += 1
        for b in reads:
            b.r[e] = c
        for b in writes:
            b.w = (e, c)
            b.r = {}
        return ins

    def dma(self, q, out, in_, reads=(), writes=(), slow=False):
        lane = self.lanes[self.lane_rr]
        self.lane_rr = (self.lane_rr + 1) % len(self.lanes)
        deps = self._deps(lane, reads, writes)
        if self.cnt[lane] > 0:
            deps.append((lane, self.cnt[lane]))
        self._need(q, deps)
        if slow:
            ins = self.eng[q].dma_start(out=out, in_=in_, allow_slow_non_contiguous=True)
        else:
            ins = self.eng[q].dma_start(out=out, in_=in_)
        self.cnt[lane] += 16
        c = self.cnt[lane]
        ins.then_inc(self.sem[lane], 16)
        self.ninst += 1
        for b in reads:
            b.r[lane] = c
        for b in writes:
            b.w = (lane, c)
            b.r = {}
        return ins

    def barrier(self):
        for e in self.eng:
            deps = [(k, self.cnt[k]) for k in self.cnt if k != e and self.cnt[k] > 0]
            if e != "pe" and self.cnt[e] > 0:
                deps.append((e, self.cnt[e]))
            self._need(e, deps)

    def drain_all(self, e="sp"):
        for ln in self.lanes:
            if self.cnt[ln]:
                self.eng[e].wait_ge(self.sem[ln], self.cnt[ln])
        for k in self.eng:
            if k != e and self.cnt[k]:
                self.eng[e].wait_ge(self.sem[k], self.cnt[k])


def rev(ap_):
    a = [list(x) for x in ap_.ap]
    st, n = a[-1]
    off = ap_.offset + st * (n - 1)
    a[-1] = [-st, n]
    return bass.AP(ap_.tensor, off, a)


class K:
    def __init__(self, stage=99):
        self.stage = stage
        self.nc = nc = bass.Bass("TRN2", target_bir_lowering=False)
        self.es = ExitStack()
        self.S = Sched(nc, self.es)
        self.din = {}
        self.bufs = {}
        self.rot = {}
        self.scope = None

    def inp(self, name, shape, dt=F32):
        t = self.nc.dram_tensor(name, list(shape), dt, kind="ExternalInput").ap()
        self.din[name] = t
        return t

    def scratch(self, name, shape, dt):
        t = self.nc.dram_tensor(name, list(shape), dt, kind="Internal").ap()
        self.bufs[name] = Buf(name)
        return t

    def sb(self, name, shape, dt=F32):
        es = self.scope if self.scope is not None else self.es
        self.uid = getattr(self, "uid", 0) + 1
        t = es.enter_context(self.nc.sbuf_tensor("%s_u%d" % (name, self.uid), list(shape), dt))
        self.bufs[name] = Buf(name)
        return t

    def begin_phase(self):
        assert self.scope is None
        self.scope = ExitStack()

    def end_phase(self):
        self.S.barrier()
        self.scope.close()
        self.scope = None

    def B(self, name):
        if name not in self.bufs:
            self.bufs[name] = Buf(name)
        return self.bufs[name]

    def op(self, e, fn, r=(), w=()):
        return self.S.op(e, fn, [self.B(x) if isinstance(x, str) else x for x in r],
                         [self.B(x) if isinstance(x, str) else x for x in w])

    def dma(self, out, in_, r=(), w=(), q="sp", slow=False):
        return self.S.dma(q, out, in_, [self.B(x) if isinstance(x, str) else x for x in r],
                          [self.B(x) if isinstance(x, str) else x for x in w], slow=slow)

    def pbank(self):
        i = self.rot.get("ps", 0)
        self.rot["ps"] = (i + 1) % 8
        return self.ps[i], "ps%d" % i

    def build(self):
        nc = self.nc
        x = self.inp("x", [NX, D])
        ctx = self.inp("ctx", [NCTX, D])
        cvec = self.inp("cvec", [2, D])
        w_mod = self.inp("w_mod", [D, 9 * D])
        b_mod = self.inp("b_mod", [1, 9 * D])
        norm_g = self.inp("norm_g", [3, D])
        final_g = self.inp("final_g", [1, D])
        wg = self.inp("ffn_w_gate", [2, D, FF])
        wu = self.inp("ffn_w_up", [2, D, FF])
        wd = self.inp("ffn_w_down", [2, FF, D])
        w_in = self.inp("w_in", [D, IN_COLS])
        ident = self.inp("ident", [128, 128])
        out = nc.dram_tensor("out", [NX, D], F32, kind="ExternalOutput").ap()
        self.dbg = {}

        x1s = self.scratch("x1s", [NT, D], F32)

        self.ps = [self.es.enter_context(nc.psum_tensor("ps%d" % i, [128, 512], F32)) for i in range(8)]

        idf = self.sb("idf", [128, 128], F32)
        idb = self.sb("idb", [128, 128], BF16)
        ones1 = self.sb("ones1", [1, 128], F32)
        self.dma(idf[:], ident[:, :], w=["idf"])
        self.op("dve", lambda e: e.tensor_copy(out=idb[:], in_=idf[:]), r=["idf"], w=["idb"])
        self.op("dve", lambda e: e.memset(ones1[:], 1.0), w=["ones1"])
        epsc = self.sb("epsc", [128, 1], F32)
        self.op("dve", lambda e: e.memset(epsc[:], EPS), w=["epsc"])

        NSTG = 3
        stg = [self.sb("stg%d" % i, [128, 2048], F32) for i in range(NSTG)]

        def wload(dst_ap, src_ap, dstbuf, shape3=None):
            i = self.rot.get("stg", 0)
            self.rot["stg"] = (i + 1) % NSTG
            n = 1
            for s_ in dst_ap.shape[1:]:
                n *= s_
            sv = stg[i][:, 0:n]
            if shape3 is not None:
                sv = sv.rearrange("p (a b) -> p a b", a=shape3[0])
            self.dma(sv, src_ap, w=["stg%d" % i])
            self.op("pool", lambda e: e.tensor_copy(out=dst_ap, in_=sv), r=["stg%d" % i], w=[dstbuf])

        modscr = self.scratch("modscr", [2, 9 * D], F32)
        self.begin_phase()
        cT = self.sb("cT", [128, 2, KT, 1], F32)
        for r_ in range(2):
            src = bass.AP(cvec.tensor, cvec.offset + r_ * D, [[1, 128], [128, KT], [1, 1]])
            self.dma(cT[:, r_, :, :], src, w=["cT"], slow=True)
        scT = self.sb("scT", [128, 2, KT], F32)
        self.op("act", lambda e: e.activation(out=scT[:], in_=cT[:, :, :, 0], func=AF.Silu), r=["cT"], w=["scT"])
        mrow = [self.sb("mrow%d" % i, [1, 512], F32) for i in range(4)]
        bmod = self.sb("bmod", [1, 9 * D], F32)
        self.dma(bmod[:], b_mod[:, :], w=["bmod"])
        wm_st = [self.sb("wm_st%d" % i, [128, KT, 512], F32) for i in range(2)]
        wm_v = w_mod.rearrange("(k p) n -> p k n", p=128)
        for cb in range(18):
            sl = slice(cb * 512, (cb + 1) * 512)
            st_ = wm_st[cb % 2]
            sn = "wm_st%d" % (cb % 2)
            self.dma(st_[:], wm_v[:, :, sl], w=[sn])
            for r_ in range(2):
                pt, pn = self.pbank()
                for k in range(KT):
                    self.op("pe", lambda e: e.matmul(pt[0:1, :], lhsT=scT[:, r_, k:k + 1], rhs=st_[:, k, :],
                                                     start=(k == 0), stop=(k == KT - 1)), r=["scT", sn], w=[pn])
                mi = (cb * 2 + r_) % 4
                self.op("dve", lambda e: e.tensor_tensor(out=mrow[mi][:, :], in0=pt[0:1, :], in1=bmod[:, sl], op=ALU.add),
                        r=[pn, "bmod"], w=["mrow%d" % mi])
                self.dma(modscr[r_:r_ + 1, sl], mrow[mi][:, :], r=["mrow%d" % mi], w=["modscr"])
        self.end_phase()

        NBC = 7
        bct = [self.sb("bc%d" % i, [128, D], F32) for i in range(NBC)]
        rowA = self.sb("rowA", [1, D], F32)
        rowB = self.sb("rowB", [1, D], F32)

        def bcast_row(dst_i, row_ap, rbufs):
            for hlf in range(2):
                pt, pn = self.pbank()
                self.op("pe", lambda e: e.matmul(pt[:, :], lhsT=ones1[:, :], rhs=row_ap[:, hlf * 512:(hlf + 1) * 512],
                                                 start=True, stop=True), r=["ones1"] + rbufs, w=[pn])
                self.op("act", lambda e: e.copy(out=bct[dst_i][:, hlf * 512:(hlf + 1) * 512], in_=pt[:, :]),
                        r=[pn], w=["bc%d" % dst_i])

        def bcast_dram_row(dst_i, dram_row_ap, rb):
            self.dma(rowA[:, :], dram_row_ap, r=[rb], w=["rowA"])
            bcast_row(dst_i, rowA, ["rowA"])

        def make_mod_tiles(r_, j, gi, base, mscale):
            self.dma(rowA[:, :], modscr[r_:r_ + 1, (3 * j + 1) * D:(3 * j + 2) * D], r=["modscr"], w=["rowA"])
            self.dma(rowB[:, :], norm_g[gi:gi + 1, :], w=["rowB"])
            self.op("dve", lambda e: e.scalar_tensor_tensor(out=rowA[:, :], in0=rowA[:, :], scalar=1.0, in1=rowB[:, :],
                                                            op0=ALU.add, op1=ALU.mult), r=["rowA", "rowB"], w=["rowA"])
            bcast_row(base + 0, rowA, ["rowA"])
            self.dma(rowB[:, :], modscr[r_:r_ + 1, (3 * j) * D:(3 * j + 1) * D], r=["modscr"], w=["rowB"])
            bcast_row(base + 1, rowB, ["rowB"])
            self.dma(rowA[:, :], modscr[r_:r_ + 1, (3 * j + 2) * D:(3 * j + 3) * D], r=["modscr"], w=["rowA"])
            self.op("dve", lambda e: e.tensor_scalar(out=rowA[:, :], in0=rowA[:, :], scalar1=mscale, scalar2=None, op0=ALU.mult),
                    r=["rowA"], w=["rowA"])
            bcast_row(base + 2, rowA, ["rowA"])

        self.modscr = modscr
        self.bct = bct
        self.make_mod_tiles = make_mod_tiles
        self.bcast_dram_row = bcast_dram_row
        self.idf, self.idb, self.ones1, self.epsc = idf, idb, ones1, epsc
        self.wload = wload
        self.x, self.ctx, self.out = x, ctx, out
        self.x1s = x1s
        self.wg, self.wu, self.wd, self.w_in = wg, wu, wd, w_in

        supers = [(0, 2)] + [(NCTX + i * 512, 4) for i in range(NX // 512)]
        self.supers = supers

        def src_A1(t0, n):
            if t0 < NCTX:
                return ctx[t0:t0 + n, :], None
            return x[t0 - NCTX:t0 - NCTX + n, :], None

        make_mod_tiles(1, 0, 0, 0, 0.5)
        make_mod_tiles(0, 0, 0, 3, 0.5)

        def epi_A1(xs, si, t0, tt, norm_T):
            self.dma(x1s[t0 + tt * 128:t0 + (tt + 1) * 128, :], xs[:, tt, :], r=["xs"], w=["x1s"])

        self.ffn_phase(0, src_A1, supers, lambda si: (0, 1, 2) if si == 0 else (3, 4, 5), epi_A1, "a")

        if self.stage == 1:
            self.begin_phase()
            xs = self.sb("xsd", [128, D], F32)
            for t in range(NX // 128):
                self.dma(xs[:, :], x1s[NCTX + t * 128:NCTX + (t + 1) * 128, :], r=["x1s"], w=["xsd"])
                self.dma(out[t * 128:(t + 1) * 128, :], xs[:, :], r=["xsd"], w=["out"])
            self.finish()
            return

        self.phase_A2()
        if self.stage == 21:
            self.finish()
            return
        self.phase_B1()
        if self.stage == 22:
            self.finish()
            return
        if self.stage in (3, 31):
            self.phase_S5()
            if self.stage == 31:
                self.finish()
                return
            self.dbg_copy("YAs", self.YAs, [W, NX], F32)
            self.finish()
            return
        self.phase_B2()
        if self.stage in (24, 25, 26):
            self.finish()
            return
        if self.stage == 23:
            self.finish()
            return
        self.phase_S5()
        if self.stage == 2:
            self.dbg_copy("YTs", self.YTs.rearrange("d s c -> (d s) c"), [2 * NT, W], BF16)
            for d in range(2):
                self.dbg_copy("VTs%d" % d, self.VTs[d], [NT, W], BF16)
                self.dbg_copy("AFs%d" % d, self.AFs[d], [W, NT], F32)
                self.dbg_copy("RFs%d" % d, self.RFs[d], [W, NT], F32)
                self.dbg_copy("LWs%d" % d, self.LWs[d], [W, NT], F32)
                self.dbg_copy("BFs%d" % d, self.BFs[d], [W, NT], F32)
                self.dbg_copy("KFs%d" % d, self.KFs[d], [W, NT], F32)
            self.dbg_copy("BONs", self.BONs, [NT, W], BF16)
            self.dbg_copy("projT", self.projT, [2304, NT], BF16)
            self.finish()
            return
        self.phase_C1()
        if self.stage == 4:
            self.dbg_copy("x2s", self.x2s, [NX, D], F32)
            self.finish()
            return
        self.phase_C2()
        self.finish()

    def dbg_copy(self, name, src_ap, shape, dt):
        o = self.nc.dram_tensor("dbg_" + name, list(shape), dt, kind="ExternalOutput").ap()
        self.S.barrier()
        self.dma(o, src_ap, r=[name], w=["dbg_" + name])

    def norm_tiles(self, sfx):
        hb = [self.sb("hb%d%s" % (i, sfx), [128, D], BF16) for i in range(2)]
        htmp = self.sb("htmp" + sfx, [128, D], F32)
        junk = self.sb("junk" + sfx, [128, D], BF16)
        ssq = self.sb("ssq" + sfx, [128, 4], F32)
        hT = self.sb("hT" + sfx, [128, KT, 512], BF16)
        bct, idb, epsc = self.bct, self.idb, self.epsc

        def norm_T(xs_ap, xbuf, Gi, Si, col0):
            self.op("act", lambda e: e.activation(out=junk[:], in_=xs_ap, func=AF.Square, accum_out=ssq[:, 0:1]),
                    r=[xbuf], w=["junk" + sfx, "ssq" + sfx])
            self.op("act", lambda e: e.activation(out=ssq[:, 1:2], in_=ssq[:, 0:1], func=AF.Sqrt, scale=1.0 / D, bias=epsc[:, 0:1]),
                    r=["ssq" + sfx, "epsc"], w=["ssq" + sfx])
            self.op("dve", lambda e: e.reciprocal(out=ssq[:, 2:3], in_=ssq[:, 1:2]), r=["ssq" + sfx], w=["ssq" + sfx])
            i = self.rot.get("hb", 0)
            self.rot["hb"] = 1 - i
            hbn = "hb%d%s" % (i, sfx)
            if Gi is None:
                return ssq
            self.op("dve", lambda e: e.scalar_tensor_tensor(out=htmp[:], in0=xs_ap, scalar=ssq[:, 2:3], in1=bct[Gi][:],
                                                            op0=ALU.mult, op1=ALU.mult),
                    r=[xbuf, "ssq" + sfx, "bc%d" % Gi], w=["htmp" + sfx])
            self.op("pool", lambda e: e.tensor_tensor(out=hb[i][:], in0=htmp[:], in1=bct[Si][:], op=ALU.add),
                    r=["htmp" + sfx, "bc%d" % Si], w=[hbn])
            pt, pn = self.pbank()
            ptb = pt[:].bitcast(BF16)
            for k in range(KT):
                self.op("pe", lambda e: e.transpose(out=ptb[:, k * 128:(k + 1) * 128], in_=hb[i][:, k * 128:(k + 1) * 128],
                                                    identity=idb[:]), r=[hbn, "idb"], w=[pn])
            self.op("act", lambda e: e.copy(out=hT[:, :, col0:col0 + 128],
                                            in_=ptb.rearrange("p (k c) -> p k c", k=KT)), r=[pn], w=["hT" + sfx])
            return ssq

        return norm_T, hT, "hT" + sfx

    def ffn_phase(self, li, src_of, supers, tiles_of, epilogue, sfx):
        self.begin_phase()
        bct, wload = self.bct, self.wload
        wg, wu, wd = self.wg, self.wu, self.wd
        norm_T, hT, hTn = self.norm_tiles(sfx)
        xs = self.sb("xs", [128, 4, D], F32)
        actT = self.sb("actT", [128, NFT, 512], BF16)
        wd_bf = self.sb("wd_bf", [128, NFT, D], BF16)
        FB = 256
        NFB = FF // FB
        wg_bf = [self.sb("wg_bf%d" % i, [128, KT, FB], BF16) for i in range(2)]
        wu_bf = [self.sb("wu_bf%d" % i, [128, KT, FB], BF16) for i in range(2)]
        sg = [self.sb("sg%d" % i, [128, 512], F32) for i in range(2)]
        ytmp = self.sb("ytmp", [128, 512], F32)
        wdv = wd[li].rearrange("(f p) n -> p f n", p=128)
        for f in range(0, NFT, 2):
            wload(wd_bf[:, f:f + 2, :], wdv[:, f:f + 2, :], "wd_bf", shape3=(2, D))
        wgv = wg[li].rearrange("(k p) f -> p k f", p=128)
        wuv = wu[li].rearrange("(k p) f -> p k f", p=128)
        for si, (t0, ntt) in enumerate(supers):
            ntok = ntt * 128
            Gi, Si, Mi = tiles_of(si)
            src, srcbuf = src_of(t0, ntok)
            self.dma(xs[:, 0:ntt, :], src.rearrange("(t p) d -> p t d", p=128), r=[srcbuf] if srcbuf else [], w=["xs"])
            for tt in range(ntt):
                norm_T(xs[:, tt, :], "xs", Gi, Si, tt * 128)
            for fb in range(NFB):
                bi = self.rot.get("wgu", 0)
                self.rot["wgu"] = 1 - bi
                wload(wg_bf[bi][:], wgv[:, :, fb * FB:(fb + 1) * FB], "wg_bf%d" % bi, shape3=(KT, FB))
                wload(wu_bf[bi][:], wuv[:, :, fb * FB:(fb + 1) * FB], "wu_bf%d" % bi, shape3=(KT, FB))
                for fl in range(FB // 128):
                    f = fb * (FB // 128) + fl
                    pg, pgn = self.pbank()
                    pu, pun = self.pbank()
                    for k in range(KT):
                        self.op("pe", lambda e: e.matmul(pg[:, 0:ntok], lhsT=wg_bf[bi][:, k, fl * 128:(fl + 1) * 128],
                                                         rhs=hT[:, k, 0:ntok], start=(k == 0), stop=(k == KT - 1)),
                                r=["wg_bf%d" % bi, hTn], w=[pgn])
                    for k in range(KT):
                        self.op("pe", lambda e: e.matmul(pu[:, 0:ntok], lhsT=wu_bf[bi][:, k, fl * 128:(fl + 1) * 128],
                                                         rhs=hT[:, k, 0:ntok], start=(k == 0), stop=(k == KT - 1)),
                                r=["wu_bf%d" % bi, hTn], w=[pun])
                    gi_ = self.rot.get("sg", 0)
                    self.rot["sg"] = 1 - gi_
                    self.op("act", lambda e: e.activation(out=sg[gi_][:, 0:ntok], in_=pg[:, 0:ntok], func=AF.Silu),
                            r=[pgn], w=["sg%d" % gi_])
                    self.op("dve", lambda e: e.tensor_tensor(out=actT[:, f, 0:ntok], in0=sg[gi_][:, 0:ntok],
                                                             in1=pu[:, 0:ntok], op=ALU.mult),
                            r=["sg%d" % gi_, pun], w=["actT"])
            for tt in range(ntt):
                for hlf in range(2):
                    py, pyn = self.pbank()
                    for f in range(NFT):
                        self.op("pe", lambda e: e.matmul(py[:, :], lhsT=actT[:, f, tt * 128:(tt + 1) * 128],
                                                         rhs=wd_bf[:, f, hlf * 512:(hlf + 1) * 512],
                                                         start=(f == 0), stop=(f == NFT - 1)),
                                r=["actT", "wd_bf"], w=[pyn])
                    self.op("dve", lambda e: e.tensor_tensor(out=ytmp[:], in0=py[:, :], in1=bct[Mi][:, hlf * 512:(hlf + 1) * 512],
                                                             op=ALU.mult), r=[pyn, "bc%d" % Mi], w=["ytmp"])
                    self.op("pool", lambda e: e.tensor_tensor(out=xs[:, tt, hlf * 512:(hlf + 1) * 512], in0=ytmp[:],
                                                              in1=xs[:, tt, hlf * 512:(hlf + 1) * 512], op=ALU.add),
                            r=["ytmp", "xs"], w=["xs"])
                epilogue(xs, si, t0, tt, norm_T)
        self.end_phase()

    def phase_A2(self):
        nc = self.nc
        NPC = 2304
        self.projT = projT = self.scratch("projT", [NPC, NT], BF16)
        self.make_mod_tiles(1, 1, 1, 0, 1.0)
        self.make_mod_tiles(0, 1, 1, 3, 1.0)
        self.begin_phase()
        norm_T, hT, hTn = self.norm_tiles("b")
        win_bf = self.sb("win_bf", [128, KT, NPC], BF16)
        wv = self.w_in.rearrange("(k p) n -> p k n", p=128)
        for k in range(KT):
            for c0 in range(0, NPC, 1152):
                self.wload(win_bf[:, k, c0:c0 + 1152], wv[:, k, c0:c0 + 1152], "win_bf")
        xs = self.sb("xs2", [128, 4, D], F32)
        pst = [self.sb("pst%d" % i, [128, 512], BF16) for i in range(3)]
        for si, (t0, ntt) in enumerate(self.supers):
            ntok = ntt * 128
            Gi, Si = (0, 1) if si == 0 else (3, 4)
            self.dma(xs[:, 0:ntt, :], self.x1s[t0:t0 + ntok, :].rearrange("(t p) d -> p t d", p=128), r=["x1s"], w=["xs2"])
            for tt in range(ntt):
                norm_T(xs[:, tt, :], "xs2", Gi, Si, tt * 128)
            for ct in range(NPC // 128):
                pt, pn = self.pbank()
                for k in range(KT):
                    self.op("pe", lambda e: e.matmul(pt[:, 0:ntok], lhsT=win_bf[:, k, ct * 128:(ct + 1) * 128],
                                                     rhs=hT[:, k, 0:ntok], start=(k == 0), stop=(k == KT - 1)),
                            r=["win_bf", hTn], w=[pn])
                pi = self.rot.get("pst", 0)
                self.rot["pst"] = (pi + 1) % 3
                eng = "act" if ct % 2 == 0 else "dve"
                if eng == "act":
                    self.op("act", lambda e: e.copy(out=pst[pi][:, 0:ntok], in_=pt[:, 0:ntok]), r=[pn], w=["pst%d" % pi])
                else:
                    self.op("dve", lambda e: e.tensor_copy(out=pst[pi][:, 0:ntok], in_=pt[:, 0:ntok]), r=[pn], w=["pst%d" % pi])
                self.dma(projT[ct * 128:(ct + 1) * 128, t0:t0 + ntok], pst[pi][:, 0:ntok], r=["pst%d" % pi], w=["projT"])
        self.end_phase()

    def phase_B1(self):
        nc = self.nc
        projT, idb = self.projT, self.idb
        convw = self.inp("convw", [128, 12, 9])
        w0T_d = self.inp("w0T", [128, 8])
        a0T_d = self.inp("a0T", [128, 8])
        vec4_d = self.inp("vec4", [128, 5, 4])
        w2_d = self.inp("w2", [128, W])
        a2_d = self.inp("a2", [128, W])
        blk1_d = self.inp("blk1", [128, 128])
        self.AFs = [self.scratch("AFs%d" % d, [W, NT], F32) for d in range(2)]
        self.RFs = [self.scratch("RFs%d" % d, [W, NT], F32) for d in range(2)]
        self.LWs = [self.scratch("LWs%d" % d, [W, NT], F32) for d in range(2)]
        self.BFs = [self.scratch("BFs%d" % d, [W, NT], F32) for d in range(2)]
        self.KFs = [self.scratch("KFs%d" % d, [W, NT], F32) for d in range(2)]
        self.VTs = [self.scratch("VTs%d" % d, [NT, W], BF16) for d in range(2)]
        self.BONs = self.scratch("BONs", [NT, W], BF16)
        self.begin_phase()
        cw = self.sb("cw", [128, 12, 9], F32)
        w0T = self.sb("w0Ts", [128, 8], F32)
        a0T = self.sb("a0Ts", [128, 8], F32)
        vec4 = self.sb("vec4s", [128, 5, 4], F32)
        blk1 = self.sb("blk1s", [128, 128], F32)
        w2b = self.sb("w2b", [128, W], BF16)
        a2b = self.sb("a2b", [128, W], BF16)
        self.dma(cw[:], convw[:, :, :], w=["cw"])
        self.dma(w0T[:], w0T_d[:, :], w=["w0Ts"])
        self.dma(a0T[:], a0T_d[:, :], w=["a0Ts"])
        self.dma(vec4[:], vec4_d[:, :, :], w=["vec4s"])
        self.dma(blk1[:], blk1_d[:, :], w=["blk1s"])
        self.wload(w2b[:], w2_d[:, :], "w2b")
        self.wload(a2b[:], a2_d[:, :], "a2b")
        eps12 = self.sb("eps12", [128, 1], F32)
        self.op("dve", lambda e: e.memset(eps12[:], 1e-12), w=["eps12"])
        twd = self.sb("twd", [128, NT], BF16)
        adb = self.sb("adb", [128, NT], BF16)
        self.dma(twd[:], projT[2048:2176, :], r=["projT"], w=["twd"])
        self.dma(adb[:], projT[2176:2304, :], r=["projT"], w=["adb"])
        self.op("act", lambda e: e.activation(out=twd[:], in_=twd[:], func=AF.Tanh), r=["twd"], w=["twd"])
        raw = [self.sb("raw%d" % a, [128, NT], BF16) for a in range(3)]
        cv = [self.sb("cv%d" % a, [128, NT], F32) for a in range(3)]
        NB = 256
        tnames = ["sig", "dec0", "dec1", "icl0", "icl1", "kx", "sq", "rn", "kk", "t1", "t2", "kt0", "kt1", "rvt"]
        tf = {n: self.sb("t_" + n, [128, NB], F32) for n in tnames}
        bnames = ["ab", "b0", "b1", "k0", "k1", "vb", "rvb", "bon"]
        tb_ = {n: self.sb("tb_" + n, [128, NB], BF16) for n in bnames}
        tmo = [self.sb("tmo%d" % i, [128, 2, 128], BF16) for i in range(2)]

        def s0_of(d, t0):
            if d == 0 or t0 < NCTX:
                return t0
            return NT - t0

        def emit_fm(scr, sname, d, src_ap, srcbuf, ct, t0):
            if d == 0:
                self.dma(scr[ct * 128:(ct + 1) * 128, t0:t0 + NB], src_ap, r=[srcbuf], w=[sname])
            else:
                self.op("pool", lambda e: e.tensor_copy(out=tf["rvt"][:], in_=rev(src_ap)), r=[srcbuf], w=["t_rvt"])
                s0 = s0_of(1, t0)
                self.dma(scr[ct * 128:(ct + 1) * 128, s0:s0 + NB], tf["rvt"][:], r=["t_rvt"], w=[sname])

        def emit_tm(scr, sname, d, src_t, srcbuf, ct, t0):
            src = src_t
            sb_ = srcbuf
            if d == 1:
                self.op("pool", lambda e: e.tensor_copy(out=tb_["rvb"][:], in_=rev(src_t[:, :])), r=[srcbuf], w=["tb_rvb"])
                src = tb_["rvb"]
                sb_ = "tb_rvb"
            pt, pn = self.pbank()
            ptb = pt[:].bitcast(BF16)
            for j in range(2):
                self.op("pe", lambda e: e.transpose(out=ptb[:, j * 128:(j + 1) * 128], in_=src[:, j * 128:(j + 1) * 128],
                                                    identity=idb[:]), r=[sb_, "idb"], w=[pn])
            oi = self.rot.get("tmo", 0)
            self.rot["tmo"] = 1 - oi
            self.op("act", lambda e: e.copy(out=tmo[oi][:], in_=ptb[:, 0:256].rearrange("p (j c) -> p j c", j=2)),
                    r=[pn], w=["tmo%d" % oi])
            s0 = s0_of(d, t0)
            self.dma(scr[s0:s0 + NB, ct * 128:(ct + 1) * 128].rearrange("(j p) c -> p j c", p=128), tmo[oi][:],
                     r=["tmo%d" % oi], w=[sname])

        def conv(dst, dn, src, sn, cwi):
            self.op("dve", lambda e: e.tensor_scalar(out=dst[:, :], in0=src[:, :], scalar1=cw[:, cwi, 4:5], scalar2=None,
                                                     op0=ALU.mult), r=[sn, "cw"], w=[dn])
            for tap, sh in ((3, -1), (5, 1)):
                if sh == -1:
                    o, i_ = dst[:, 1:NCTX], src[:, 0:NCTX - 1]
                else:
                    o, i_ = dst[:, 0:NCTX - 1], src[:, 1:NCTX]
                self.op("dve", lambda e: e.scalar_tensor_tensor(out=o, in0=i_, scalar=cw[:, cwi, tap:tap + 1], in1=o,
                                                                op0=ALU.mult, op1=ALU.add), r=[sn, dn, "cw"], w=[dn])
            gd = dst[:, NCTX:NT].rearrange("p (r c) -> p r c", c=64)
            gs = src[:, NCTX:NT].rearrange("p (r c) -> p r c", c=64)
            for dy in range(3):
                for dx in range(3):
                    if dy == 1 and dx == 1:
                        continue
                    oy, ox = dy - 1, dx - 1
                    r0, r1 = max(0, -oy), 64 - max(0, oy)
                    c0, c1 = max(0, -ox), 64 - max(0, ox)
                    o = gd[:, r0:r1, c0:c1]
                    i_ = gs[:, r0 + oy:r1 + oy, c0 + ox:c1 + ox]
                    tap = dy * 3 + dx
                    self.op("dve", lambda e: e.scalar_tensor_tensor(out=o, in0=i_, scalar=cw[:, cwi, tap:tap + 1], in1=o,
                                                                    op0=ALU.mult, op1=ALU.add), r=[sn, dn, "cw"], w=[dn])

        def mm_lora(wb, wbn, src, srcn, d, ct, sl):
            pt, pn = self.pbank()
            self.op("pe", lambda e: e.matmul(pt[:, 0:NB], lhsT=wb[d * 64:(d + 1) * 64, ct * 128:(ct + 1) * 128],
                                             rhs=src[d * 64:(d + 1) * 64, sl], start=True, stop=True), r=[wbn, srcn], w=[pn])
            return pt, pn

        def headsum(src_t, srcn):
            pt, pn = self.pbank()
            self.op("pe", lambda e: e.matmul(pt[:, 0:NB], lhsT=blk1[:, :], rhs=src_t[:, :], start=True, stop=True),
                    r=["blk1s", srcn], w=[pn])
            return pt, pn

        T = lambda n: tf[n]
        for ct in range(4):
            for a in range(3):
                self.dma(raw[a][:], projT[512 + a * 512 + ct * 128:512 + a * 512 + (ct + 1) * 128, :], r=["projT"], w=["raw%d" % a])
                conv(cv[a], "cv%d" % a, raw[a], "raw%d" % a, a * 4 + ct)
            rc, kc, vc = cv
            for bi in range(NT // NB):
                t0 = bi * NB
                sl = slice(t0, t0 + NB)
                for d in range(2):
                    pt, pn = mm_lora(w2b, "w2b", twd, "twd", d, ct, sl)
                    self.op("act", lambda e: e.activation(out=T("sig")[:], in_=pt[:, 0:NB], func=AF.Sigmoid,
                                                          bias=w0T[:, d * 4 + ct:d * 4 + ct + 1]), r=[pn, "w0Ts"], w=["t_sig"])
                    dn = "dec%d" % d
                    self.op("pool", lambda e: e.tensor_scalar(out=T(dn)[:], in0=T("sig")[:], scalar1=-math.exp(-0.5), scalar2=None,
                                                              op0=ALU.mult), r=["t_sig"], w=["t_" + dn])
                    emit_fm(self.LWs[d], "LWs%d" % d, d, T(dn)[:], "t_" + dn, ct, t0)
                    pt, pn = mm_lora(a2b, "a2b", adb, "adb", d, ct, sl)
                    inm = "icl%d" % d
                    self.op("act", lambda e: e.activation(out=T(inm)[:], in_=pt[:, 0:NB], func=AF.Sigmoid,
                                                          bias=a0T[:, d * 4 + ct:d * 4 + ct + 1]), r=[pn, "a0Ts"], w=["t_" + inm])
                self.op("dve", lambda e: e.tensor_scalar(out=T("kx")[:], in0=kc[:, sl], scalar1=vec4[:, 0, ct:ct + 1], scalar2=None,
                                                         op0=ALU.mult), r=["cv1", "vec4s"], w=["t_kx"])
                self.op("pool", lambda e: e.tensor_tensor(out=T("sq")[:], in0=T("kx")[:], in1=T("kx")[:], op=ALU.mult),
                        r=["t_kx"], w=["t_sq"])
                pt, pn = headsum(T("sq"), "t_sq")
                self.op("act", lambda e: e.activation(out=T("rn")[:], in_=pt[:, 0:NB], func=AF.Sqrt, bias=eps12[:, 0:1]),
                        r=[pn, "eps12"], w=["t_rn"])
                self.op("dve", lambda e: e.reciprocal(out=T("rn")[:], in_=T("rn")[:]), r=["t_rn"], w=["t_rn"])
                self.op("dve", lambda e: e.tensor_tensor(out=T("kk")[:], in0=T("kx")[:], in1=T("rn")[:], op=ALU.mult),
                        r=["t_kx", "t_rn"], w=["t_kk"])
                self.op("pool", lambda e: e.tensor_scalar(out=T("t1")[:], in0=T("kk")[:], scalar1=-1.0, scalar2=None, op0=ALU.mult),
                        r=["t_kk"], w=["t_t1"])
                for d in range(2):
                    emit_fm(self.AFs[d], "AFs%d" % d, d, T("t1")[:], "t_t1", ct, t0)
                    emit_fm(self.RFs[d], "RFs%d" % d, d, rc[:, sl], "cv0", ct, t0)
                self.op("act", lambda e: e.copy(out=tb_["vb"][:], in_=vc[:, sl]), r=["cv2"], w=["tb_vb"])
                for d in range(2):
                    emit_tm(self.VTs[d], "VTs%d" % d, d, tb_["vb"], "tb_vb", ct, t0)
                for d in range(2):
                    bn, kn, ktn, inm = "b%d" % d, "k%d" % d, "kt%d" % d, "icl%d" % d
                    self.op("dve", lambda e: e.tensor_tensor(out=T("sq")[:], in0=T("kk")[:], in1=T(inm)[:], op=ALU.mult),
                            r=["t_kk", "t_" + inm], w=["t_sq"])
                    emit_fm(self.BFs[d], "BFs%d" % d, d, T("sq")[:], "t_sq", ct, t0)
                    self.op("dve", lambda e: e.tensor_scalar(out=T("t2")[:], in0=T(inm)[:], scalar1=-1.0,
                                                             scalar2=vec4[:, 1, ct:ct + 1], op0=ALU.add, op1=ALU.mult),
                            r=["t_" + inm, "vec4s"], w=["t_t2"])
                    self.op("dve", lambda e: e.scalar_tensor_tensor(out=T(ktn)[:], in0=T("t2")[:], scalar=1.0, in1=kc[:, sl],
                                                                    op0=ALU.add, op1=ALU.mult), r=["t_t2", "cv1"], w=["t_" + ktn])
                    emit_fm(self.KFs[d], "KFs%d" % d, d, T(ktn)[:], "t_" + ktn, ct, t0)
                self.op("pool", lambda e: e.tensor_tensor(out=T("t2")[:], in0=T("kt0")[:], in1=T("kt1")[:], op=ALU.add),
                        r=["t_kt0", "t_kt1"], w=["t_t2"])
                self.op("dve", lambda e: e.scalar_tensor_tensor(out=T("sq")[:], in0=rc[:, sl], scalar=vec4[:, 2, ct:ct + 1],
                                                                in1=T("t2")[:], op0=ALU.mult, op1=ALU.mult),
                        r=["cv0", "vec4s", "t_t2"], w=["t_sq"])
                pt, pn = headsum(T("sq"), "t_sq")
                self.op("dve", lambda e: e.tensor_tensor(out=tb_["bon"][:], in0=pt[:, 0:NB], in1=vc[:, sl], op=ALU.mult),
                        r=[pn, "cv2"], w=["tb_bon"])
                emit_tm(self.BONs, "BONs", 0, tb_["bon"], "tb_bon", ct, t0)
        self.end_phase()

    def phase_B2(self):
        nc = self.nc
        idf, idb = self.idf, self.idb
        maskR_d = self.inp("maskR", [128, W])
        mlow_d = self.inp("mlow", [128, 128])
        mup_d = self.inp("mup", [128, 128])
        mupi_d = self.inp("mupi", [128, 128])
        selv_d = self.inp("selv", [16, 128])
        self.YTs = YTs = self.scratch("YTs", [2, NT, W], BF16)
        self.begin_phase()
        L = 8
        SBK = 128
        NCH = SBK // L
        import os
        NQ = NT // L if self.stage != 23 else 48
        if self.stage == 26:
            NQ = int(os.environ.get('NQDBG', '1'))
        NBLK = (NQ * L + SBK - 1) // SBK
        QY = NCTX // L
        maskR = self.sb("maskR", [128, W], F32)
        mlow = self.sb("mlow", [128, 128], F32)
        mup = self.sb("mup", [128, 128], F32)
        mupi = self.sb("mupi", [128, 128], F32)
        selvf = self.sb("selvf", [16, 128], F32)
        selv = self.sb("selv", [16, 128], BF16)
        self.dma(maskR[:], maskR_d[:, :], w=["maskR"])
        self.dma(mlow[:], mlow_d[:, :], w=["mlow"])
        self.dma(mup[:], mup_d[:, :], w=["mup"])
        self.dma(mupi[:], mupi_d[:, :], w=["mupi"])
        self.dma(selvf[:], selv_d[:, :], w=["selvf"])
        self.op("dve", lambda e: e.tensor_copy(out=selv[:], in_=selvf[:]), r=["selvf"], w=["selv"])
        segm = self.sb("segm", [128, H * SBK], F32)
        self.op("pool", lambda e: e.memset(segm[:], 1.0), w=["segm"])
        self.op("pool", lambda e: e.memset(segm[:, :].rearrange("p (n t) -> p n t", t=L)[:, :, 0:1], 0.0), w=["segm"])
        Tst = self.sb("Tst", [128, W], F32)
        Tbf = self.sb("Tbf", [128, W], BF16)
        Tw = self.sb("Tw", [128, W], F32)
        self.op("dve", lambda e: e.memset(Tst[:], 0.0), w=["Tst"])
        self.op("pool", lambda e: e.memset(Tbf[:], 0.0), w=["Tbf"])
        stn = ("a", "r", "b", "k", "lw", "cl", "e", "dd")
        stg_ = {n: self.sb("bb_" + n, [128, H, SBK], F32) for n in stn}
        bnames = ("at", "rt", "bt", "kt", "bh", "kh")
        blk = [{n: self.sb("blk%d_%s" % (i, n), [128, NCH, 128], BF16) for n in bnames} for i in range(2)]
        for i in range(2):
            for n in bnames:
                self.op("pool", lambda e: e.memset(blk[i][n][:], 0.0), w=["blk%d_%s" % (i, n)])
        PLp = [self.sb("PLp%d" % i, [128, H, NCH], F32) for i in range(2)]

        flat = lambda t: t[:, :, :].rearrange("p h s -> p (h s)")
        perm = lambda t, dd: t[dd * 64:(dd + 1) * 64, :, :].rearrange("p h (c t) -> p c h t", t=L)

        def bview(bi, n, dd):
            return blk[bi][n][dd * 64:(dd + 1) * 64, :, :].rearrange("p c (h x) -> p c h x", x=16)[:, :, :, dd * L:(dd + 1) * L]

        def prep(b):
            bi = b % 2
            s0 = b * SBK
            srcs = (("a", self.AFs, "AFs"), ("r", self.RFs, "RFs"), ("b", self.BFs, "BFs"), ("k", self.KFs, "KFs"), ("lw", self.LWs, "LWs"))
            for d in range(2):
                for (n, scr, sn) in srcs:
                    self.dma(stg_[n][d * 64:(d + 1) * 64, :, :], scr[d].rearrange("(h j) s -> j h s", j=64)[:, :, s0:s0 + SBK],
                             r=["%s%d" % (sn, d)], w=["bb_" + n])
            cl, lw, e_, dd_ = stg_["cl"], stg_["lw"], stg_["e"], stg_["dd"]
            self.op("dve", lambda e: e.tensor_tensor_scan(out=flat(cl), data0=segm[:, :], data1=flat(lw), initial=0.0,
                                                          op0=ALU.mult, op1=ALU.add), r=["segm", "bb_lw"], w=["bb_cl"])
            eng = ["dve", "pool"]
            cnt = [0]

            def prod(dst, src):
                for dd in range(2):
                    en = eng[cnt[0] % 2]
                    cnt[0] += 1
                    self.op(en, lambda e: e.tensor_tensor(out=bview(bi, dst, dd), in0=perm(stg_[src], dd), in1=perm(e_, dd), op=ALU.mult),
                            r=["bb_" + src, "bb_e"], w=["blk%d_%s" % (bi, dst)])

            self.op("pool", lambda e: e.tensor_tensor(out=flat(dd_), in0=flat(cl), in1=flat(lw), op=ALU.subtract),
                    r=["bb_cl", "bb_lw"], w=["bb_dd"])
            self.op("act", lambda e: e.activation(out=flat(e_), in_=flat(dd_), func=AF.Exp), r=["bb_dd"], w=["bb_e"])
            prod("at", "a")
            self.op("act", lambda e: e.activation(out=flat(e_), in_=flat(cl), func=AF.Exp), r=["bb_cl"], w=["bb_e"])
            prod("rt", "r")
            self.op("act", lambda e: e.activation(out=flat(e_), in_=flat(cl), func=AF.Exp, scale=-1.0), r=["bb_cl"], w=["bb_e"])
            prod("bt", "b")
            prod("kt", "k")
            cl4 = cl[:, :, :].rearrange("p h (c t) -> p h c t", t=L)
            cll = cl4[:, :, :, L - 1:L]
            cll_bc = bass.AP(cll.tensor, cll.offset, [list(x_) for x_ in cll.ap[:-1]] + [[0, L]])
            self.op("dve", lambda e: e.tensor_tensor(out=dd_[:, :, :].rearrange("p h (c t) -> p h c t", t=L), in0=cll_bc, in1=cl4,
                                                     op=ALU.subtract), r=["bb_cl"], w=["bb_dd"])
            self.op("act", lambda e: e.activation(out=flat(e_), in_=flat(dd_), func=AF.Exp), r=["bb_dd"], w=["bb_e"])
            prod("bh", "b")
            prod("kh", "k")
            self.op("act", lambda e: e.activation(out=PLp[bi][:, :, :], in_=cl4[:, :, :, L - 1], func=AF.Exp), r=["bb_cl"], w=["PLp%d" % bi])

        D4, D2 = 4, 2
        NCB = 8
        V8h = [self.sb("V8h%d" % i, [16, NCB, W], BF16) for i in range(2)]
        Vm = [self.sb("Vm%d" % i, [128, W], BF16) for i in range(D4)]
        Makb = [self.sb("Makb%d" % i, [128, 128], BF16) for i in range(D4)]
        ArbT = [self.sb("ArbT%d" % i, [128, 128], BF16) for i in range(D4)]
        ArkT = [self.sb("ArkT%d" % i, [128, 128], BF16) for i in range(D4)]
        Nl = [self.sb("Nl%d" % i, [128, 128], F32) for i in range(D2)]
        Nu = [self.sb("Nu%d" % i, [128, 128], F32) for i in range(D2)]
        Nu2 = [self.sb("Nu2_%d" % i, [128, 128], F32) for i in range(D2)]
        N2 = [self.sb("N2_%d" % i, [128, 128], F32) for i in range(D2)]
        N4 = [self.sb("N4_%d" % i, [128, 128], F32) for i in range(D2)]
        S1 = [self.sb("S1_%d" % i, [128, 128], F32) for i in range(D2)]
        S2 = [self.sb("S2_%d" % i, [128, 128], F32) for i in range(D2)]
        MinvT = [self.sb("MinvT%d" % i, [128, 128], BF16) for i in range(D2)]
        CT = [self.sb("CT%d" % i, [128, 128], BF16) for i in range(D2)]
        BKT = [self.sb("BKT%d" % i, [128, 256], BF16) for i in range(D2)]
        Um0 = self.sb("Um0", [128, W], BF16)
        Um = self.sb("Um", [128, W], BF16)
        Yf = [self.sb("Yf%d" % i, [128, NCB, W], BF16) for i in range(2)]
        sbk = [0]

        def sbank():
            i = 4 + sbk[0]
            sbk[0] = (sbk[0] + 1) % 4
            return self.ps[i], "ps%d" % i

        def bl(q, n):
            return blk[(q // NCH) % 2][n][:, q % NCH, :], "blk%d_%s" % ((q // NCH) % 2, n)

        SLOTS = ("pe_a", "dve_a", "act_a", "pe_b", "dve_b", "pe_c", "act_b")

        def st1(q, sl):
            i4, i2 = q % D4, q % D2
            s0 = q * L
            A_, An = bl(q, "at")
            R_, Rn = bl(q, "rt")
            B_, Bn = bl(q, "bt")
            K_, Kn = bl(q, "kt")
            needy = q >= QY

            def pe_part():
                vi = (q // NCB) % 2
                if q % NCB == 0:
                    for d in range(2):
                        self.dma(V8h[vi][d * L:(d + 1) * L, :, :],
                                 self.VTs[d][s0:s0 + L * NCB, :].rearrange("(c t) x -> t c x", t=L), r=["VTs%d" % d], w=["V8h%d" % vi])
                res = {}
                pv, pvn = sbank()
                self.op("pe", lambda e: e.matmul(pv[:, :], lhsT=selv[0:16, :], rhs=V8h[vi][0:16, q % NCB, :], start=True, stop=True),
                        r=["selv", "V8h%d" % vi], w=[pvn])
                res["v"] = (pv, pvn)
                specs = [("n", A_, An, B_, Bn), ("nu", B_, Bn, A_, An), ("mk", A_, An, K_, Kn)]
                if needy:
                    specs += [("rb", B_, Bn, R_, Rn), ("rk", K_, Kn, R_, Rn)]
                st1.pending = (res, specs)

            def dve_part():
                res, specs = st1.pending
                pv, pvn = res["v"]
                self.op("dve", lambda e: e.tensor_tensor(out=Vm[i4][:, :], in0=pv[:, :], in1=maskR[:, :], op=ALU.mult),
                        r=[pvn, "maskR"], w=["Vm%d" % i4])
                outs = {"n": (Nl[i2], "Nl%d" % i2, mlow, "mlow"), "nu": (Nu[i2], "Nu%d" % i2, mup, "mup"),
                        "mk": (Makb[i4], "Makb%d" % i4, mlow, "mlow"), "rb": (ArbT[i4], "ArbT%d" % i4, mupi, "mupi"),
                        "rk": (ArkT[i4], "ArkT%d" % i4, mupi, "mupi")}
                for (nm, l_, ln, r_, rn) in specs:
                    pp, ppn = sbank()
                    self.op("pe", lambda e: e.matmul(pp[:, 0:128], lhsT=l_, rhs=r_, start=True, stop=True), r=[ln, rn], w=[ppn])
                    o_, on, m_, mn = outs[nm]
                    self.op("dve", lambda e: e.tensor_tensor(out=o_[:, :], in0=pp[:, 0:128], in1=m_[:, :], op=ALU.mult),
                            r=[ppn, mn], w=[on])
            sl["pe_a"].append(pe_part)
            sl["dve_a"].append(dve_part)

        def st2(q, sl):
            i2 = q % D2

            def pe_part():
                pa, pan = sbank()
                self.op("pe", lambda e: e.matmul(pa[:, 0:128], lhsT=Nl[i2][:, :], rhs=Nu[i2][:, :], start=True, stop=True),
                        r=["Nl%d" % i2, "Nu%d" % i2], w=[pan])
                self.op("act", lambda e: e.copy(out=Nu2[i2][:, :], in_=pa[:, 0:128]), r=[pan], w=["Nu2_%d" % i2])
                pb_, pbn = sbank()
                self.op("pe", lambda e: e.matmul(pb_[:, 0:128], lhsT=Nu[i2][:, :], rhs=Nl[i2][:, :], start=True, stop=True),
                        r=["Nl%d" % i2, "Nu%d" % i2], w=[pbn])
                self.op("act", lambda e: e.copy(out=N2[i2][:, :], in_=pb_[:, 0:128]), r=[pbn], w=["N2_%d" % i2])
                self.op("dve", lambda e: e.tensor_tensor(out=S1[i2][:, :], in0=Nu[i2][:, :], in1=idf[:, :], op=ALU.add),
                        r=["Nu%d" % i2, "idf"], w=["S1_%d" % i2])

            def pe_b():
                pc, pcn = sbank()
                self.op("pe", lambda e: e.matmul(pc[:, 0:128], lhsT=Nu2[i2][:, :], rhs=N2[i2][:, :], start=True, stop=True),
                        r=["Nu2_%d" % i2, "N2_%d" % i2], w=[pcn])
                self.op("act", lambda e: e.copy(out=N4[i2][:, :], in_=pc[:, 0:128]), r=[pcn], w=["N4_%d" % i2])
            sl["act_a"].append(pe_part)
            sl["pe_b"].append(pe_b)

        def st3(q, sl):
            i4, i2 = q % D4, q % D2
            Bh_, Bhn = bl(q, "bh")
            Kh_, Khn = bl(q, "kh")

            def part_a():
                pd_, pdn = sbank()
                self.op("pe", lambda e: e.matmul(pd_[:, 0:128], lhsT=N2[i2][:, :], rhs=S1[i2][:, :], start=True, stop=True),
                        r=["N2_%d" % i2, "S1_%d" % i2], w=[pdn])
                self.op("dve", lambda e: e.tensor_tensor(out=S2[i2][:, :], in0=pd_[:, 0:128], in1=S1[i2][:, :], op=ALU.add),
                        r=[pdn, "S1_%d" % i2], w=["S2_%d" % i2])

            def part_b():
                pe_, pen = sbank()
                self.op("pe", lambda e: e.matmul(pe_[:, 0:128], lhsT=N4[i2][:, :], rhs=S2[i2][:, :], start=True, stop=True),
                        r=["N4_%d" % i2, "S2_%d" % i2], w=[pen])
                self.op("dve", lambda e: e.tensor_tensor(out=MinvT[i2][:, :], in0=pe_[:, 0:128], in1=S2[i2][:, :], op=ALU.add),
                        r=[pen, "S2_%d" % i2], w=["MinvT%d" % i2])

            def part_c():
                pf, pfn = sbank()
                self.op("pe", lambda e: e.matmul(pf[:, 0:128], lhsT=Makb[i4][:, :], rhs=MinvT[i2][:, :], start=True, stop=True),
                        r=["Makb%d" % i4, "MinvT%d" % i2], w=[pfn])
                self.op("act", lambda e: e.copy(out=CT[i2][:, :], in_=pf[:, 0:128]), r=[pfn], w=["CT%d" % i2])
                pt, ptn = sbank()
                ptb = pt[:].bitcast(BF16)
                self.op("pe", lambda e: e.transpose(out=ptb[:, 0:128], in_=Bh_, identity=idb[:]), r=[Bhn, "idb"], w=[ptn])
                self.op("pe", lambda e: e.transpose(out=ptb[:, 128:256], in_=Kh_, identity=idb[:]), r=[Khn, "idb"], w=[ptn])
                self.op("act", lambda e: e.copy(out=BKT[i2][:, :], in_=ptb[:, 0:256]), r=[ptn], w=["BKT%d" % i2])
            sl["dve_a"].insert(0, part_a)
            sl["dve_b"].append(part_b)
            sl["pe_c"].append(part_c)

        def flush(sl, names):
            for n in names:
                for f in sl[n]:
                    f()
                sl[n] = []

        def new_slots():
            return {n: [] for n in SLOTS}

        prep(0)
        if self.stage == 24:
            for n in ("cl", "lw", "e", "a"):
                o = self.nc.dram_tensor("dbg_" + n, [128, H * SBK], F32, kind="ExternalOutput").ap()
                self.dma(o, flat(stg_[n]), r=["bb_" + n], w=["dbg_" + n])
            for n in ("at", "bt", "bh", "rt"):
                o = self.nc.dram_tensor("dbg_blk_" + n, [128, NCH * 128], BF16, kind="ExternalOutput").ap()
                self.dma(o, blk[0][n][:, :, :].rearrange("p c r -> p (c r)"), r=["blk0_" + n], w=["dbg_blk_" + n])
            o = self.nc.dram_tensor("dbg_PL", [128, H * NCH], F32, kind="ExternalOutput").ap()
            self.dma(o, PLp[0][:, :, :].rearrange("p h c -> p (h c)"), r=["PLp0"], w=["dbg_PL"])
            o = self.nc.dram_tensor("dbg_segm", [128, H * SBK], F32, kind="ExternalOutput").ap()
            self.dma(o, segm[:, :], r=["segm"], w=["dbg_segm"])
            self.end_phase()
            return
        if NBLK > 1:
            prep(1)
        for (fn, q_) in ((st1, 0), (st2, 0), (st3, 0), (st1, 1), (st2, 1), (st1, 2)):
            if q_ < NQ:
                sl = new_slots()
                fn(q_, sl)
                flush(sl, SLOTS)
        if self.stage == 25:
            for (n, t_, dt_) in (("Nl", Nl[0], F32), ("Nu", Nu[0], F32), ("Nu2", Nu2[0], F32), ("N2", N2[0], F32), ("N4", N4[0], F32),
                                 ("S1", S1[0], F32), ("S2", S2[0], F32), ("MinvT", MinvT[0], BF16), ("CT", CT[0], BF16),
                                 ("Makb", Makb[0], BF16), ("BKT", BKT[0], BF16), ("Vm", Vm[0], BF16)):
                shp = [128, t_[:, :].shape[1]]
                o = self.nc.dram_tensor("dbg_" + n, shp, dt_, kind="ExternalOutput").ap()
                bn_ = {"Nl": "Nl0", "Nu": "Nu0", "Nu2": "Nu2_0", "N2": "N2_0", "N4": "N4_0", "S1": "S1_0", "S2": "S2_0", "MinvT": "MinvT0",
                       "CT": "CT0", "Makb": "Makb0", "BKT": "BKT0", "Vm": "Vm0"}[n]
                self.dma(o, t_[:, :], r=[bn_], w=["dbg_" + n])
            for n in ("at", "bt", "kt", "bh", "kh", "rt"):
                o = self.nc.dram_tensor("dbg_blk_" + n, [128, NCH * 128], BF16, kind="ExternalOutput").ap()
                self.dma(o, blk[0][n][:, :, :].rearrange("p c r -> p (c r)"), r=["blk0_" + n], w=["dbg_blk_" + n])
            self.end_phase()
            return
        pu, pun = self.ps[0], "ps0"
        pU, pUn = self.ps[1], "ps1"
        pd, pdn = self.ps[2], "ps2"
        py, pyn = self.ps[3], "ps3"
        for q in range(NQ):
            b_, c_ = q // NCH, q % NCH
            if c_ == NCH - 6 and b_ + 2 < NBLK + 0 and (b_ + 2) * NCH < NQ + NCH:
                pass
            if c_ == 4 and b_ >= 1 and b_ + 1 < NBLK:
                prep(b_ + 1)
            sl = new_slots()
            if q + 3 < NQ:
                st1(q + 3, sl)
            if q + 2 < NQ:
                st2(q + 2, sl)
            if q + 1 < NQ:
                st3(q + 1, sl)
            i4, i2, yi = q % D4, q % D2, q % 2
            bi = b_ % 2
            needy = q >= QY
            A_, An = bl(q, "at")
            R_, Rn = bl(q, "rt")
            pl = PLp[bi][:, :, c_:c_ + 1]
            pl_bc = bass.AP(pl.tensor, pl.offset, [list(pl.ap[0]), list(pl.ap[1]), [0, 64]])
            self.op("pool", lambda e: e.tensor_tensor(out=Tw[:, :].rearrange("p (h j) -> p h j", j=64),
                                                      in0=Tst[:, :].rearrange("p (h j) -> p h j", j=64), in1=pl_bc, op=ALU.mult),
                    r=["Tst", "PLp%d" % bi], w=["Tw"])
            self.op("pe", lambda e: e.matmul(pu[:, :], lhsT=A_, rhs=Tbf[:, :], start=True, stop=True), r=[An, "Tbf"], w=[pun])
            if needy:
                self.op("pe", lambda e: e.matmul(py[:, :], lhsT=R_, rhs=Tbf[:, :], start=True, stop=False), r=[Rn, "Tbf"], w=[pyn])
            self.op("pe", lambda e: e.matmul(pU[:, :], lhsT=CT[i2][:, :], rhs=Vm[i4][:, :], start=True, stop=False),
                    r=["CT%d" % i2, "Vm%d" % i4], w=[pUn])
            self.op("pe", lambda e: e.matmul(pd[:, :], lhsT=BKT[i2][:, 128:256], rhs=Vm[i4][:, :], start=True, stop=False),
                    r=["BKT%d" % i2, "Vm%d" % i4], w=[pdn])
            flush(sl, ["pe_a"])
            self.op("dve", lambda e: e.tensor_tensor(out=Um0[:, :], in0=pu[:, :], in1=maskR[:, :], op=ALU.mult),
                    r=[pun, "maskR"], w=["Um0"])
            flush(sl, ["dve_a", "act_a"])
            self.op("pe", lambda e: e.matmul(pU[:, :], lhsT=MinvT[i2][:, :], rhs=Um0[:, :], start=False, stop=True),
                    r=["MinvT%d" % i2, "Um0"], w=[pUn])
            flush(sl, ["pe_b"])
            self.op("act", lambda e: e.copy(out=Um[:, :], in_=pU[:, :]), r=[pUn], w=["Um"])
            self.op("pe", lambda e: e.matmul(pd[:, :], lhsT=BKT[i2][:, 0:128], rhs=Um[:, :], start=False, stop=True),
                    r=["BKT%d" % i2, "Um"], w=[pdn])
            if needy:
                self.op("pe", lambda e: e.matmul(py[:, :], lhsT=ArbT[i4][:, :], rhs=Um[:, :], start=False, stop=False),
                        r=["ArbT%d" % i4, "Um"], w=[pyn])
                self.op("pe", lambda e: e.matmul(py[:, :], lhsT=ArkT[i4][:, :], rhs=Vm[i4][:, :], start=False, stop=True),
                        r=["ArkT%d" % i4, "Vm%d" % i4], w=[pyn])
            flush(sl, ["dve_b"])
            self.op("dve", lambda e: e.tensor_tensor(out=Tst[:, :], in0=Tw[:, :], in1=pd[:, :], op=ALU.add), r=["Tw", pdn], w=["Tst"])
            flush(sl, ["pe_c"])
            self.op("act", lambda e: e.copy(out=Tbf[:, :], in_=Tst[:, :]), r=["Tst"], w=["Tbf"])
            flush(sl, ["act_b"])
            if needy:
                yi = (q // NCB) % 2
                self.op("act", lambda e: e.copy(out=Yf[yi][:, q % NCB, :], in_=py[:, :]), r=[pyn], w=["Yf%d" % yi])
                if q % NCB == NCB - 1:
                    s0 = (q - (NCB - 1)) * L
                    for h in range(H):
                        for d in range(2):
                            p0 = h * 16 + d * L
                            self.dma(YTs[d, s0:s0 + L * NCB, h * 64:(h + 1) * 64].rearrange("(c t) x -> t c x", t=L),
                                     Yf[yi][p0:p0 + L, :, h * 64:(h + 1) * 64], r=["Yf%d" % yi], w=["YTs"])
        if self.stage == 26:
            for (n, t_, bn_, dt_) in (("Tst", Tst, "Tst", F32), ("Um", Um, "Um", BF16), ("Um0", Um0, "Um0", BF16), ("Tw", Tw, "Tw", F32)):
                o = self.nc.dram_tensor("dbg_" + n, [128, W], dt_, kind="ExternalOutput").ap()
                self.dma(o, t_[:, :], r=[bn_], w=["dbg_" + n])
        self.end_phase()

    def phase_S5(self):
        nc = self.nc
        idf, idb = self.idf, self.idb
        s5p_d = self.inp("s5p", [128, 5, 512])
        s5s_d = self.inp("s5s", [128, 3, 64])
        ctn_d = self.inp("ctn", [128, 64, 16])
        s5D_d = self.inp("s5D", [128, 4])
        jsw_d = self.inp("jsw", [128, 128])
        gmask_d = self.inp("gmask", [128, 8])
        self.YAs = YAs = self.scratch("YAs", [W, NX], F32)
        self.begin_phase()
        TWO_PI = 2.0 * math.pi
        jsw = self.sb("jsws", [128, 128], F32)
        gmask = self.sb("gmasks", [128, 8], F32)
        s5D = self.sb("s5Ds", [128, 4], F32)
        self.dma(jsw[:], jsw_d[:, :], w=["jsws"])
        self.dma(gmask[:], gmask_d[:, :], w=["gmasks"])
        self.dma(s5D[:], s5D_d[:, :], w=["s5Ds"])
        pb = self.sb("s5pb", [128, 5, 512], F32)
        psm = self.sb("s5ps", [128, 3, 64], F32)
        self.dma(pb[:], s5p_d[:, :, :], w=["s5pb"])
        self.dma(psm[:], s5s_d[:, :, :], w=["s5ps"])
        tmp = [self.sb("s5t%d" % i, [128, 512], F32) for i in range(8)]
        tmi = self.sb("s5ti", [128, 512], mybir.dt.int32)

        def dv(fn, r, w):
            self.op("dve", fn, r=r, w=w)

        def frac_sin(dst, dn, t_ap, tn, n, tA, tAn, tB, tBn, add):
            dv(lambda e: e.tensor_scalar(out=tA[:, 0:n], in0=t_ap, scalar1=add, scalar2=None, op0=ALU.add), [tn], [tAn])
            dv(lambda e: e.tensor_copy(out=tmi[:, 0:n], in_=tA[:, 0:n]), [tAn], ["s5ti"])
            dv(lambda e: e.tensor_copy(out=tB[:, 0:n], in_=tmi[:, 0:n]), ["s5ti"], [tBn])
            dv(lambda e: e.tensor_tensor(out=tA[:, 0:n], in0=tA[:, 0:n], in1=tB[:, 0:n], op=ALU.subtract), [tAn, tBn], [tAn])
            dv(lambda e: e.tensor_scalar(out=tB[:, 0:n], in0=tA[:, 0:n], scalar1=0.5, scalar2=None, op0=ALU.is_gt), [tAn], [tBn])
            dv(lambda e: e.tensor_tensor(out=tA[:, 0:n], in0=tA[:, 0:n], in1=tB[:, 0:n], op=ALU.subtract), [tAn, tBn], [tAn])
            dv(lambda e: e.tensor_scalar(out=tB[:, 0:n], in0=tA[:, 0:n], scalar1=-0.5, scalar2=None, op0=ALU.is_lt), [tAn], [tBn])
            dv(lambda e: e.tensor_tensor(out=tA[:, 0:n], in0=tA[:, 0:n], in1=tB[:, 0:n], op=ALU.add), [tAn, tBn], [tAn])
            self.op("act", lambda e: e.activation(out=dst, in_=tA[:, 0:n], func=AF.Sin, scale=TWO_PI), r=[tAn], w=[dn])

        def abar(lre, lim, ldt, srcn, n, ar, arn, ai, ain):
            t0_, t1_, t2_, t3_ = tmp[0], tmp[1], tmp[2], tmp[3]
            self.op("act", lambda e: e.activation(out=t0_[:, 0:n], in_=ldt, func=AF.Exp), r=[srcn], w=["s5t0"])
            dv(lambda e: e.tensor_tensor(out=t1_[:, 0:n], in0=t0_[:, 0:n], in1=lre, op=ALU.mult), ["s5t0", srcn], ["s5t1"])
            self.op("act", lambda e: e.activation(out=t1_[:, 0:n], in_=t1_[:, 0:n], func=AF.Exp), r=["s5t1"], w=["s5t1"])
            dv(lambda e: e.scalar_tensor_tensor(out=t0_[:, 0:n], in0=t0_[:, 0:n], scalar=1.0 / TWO_PI, in1=lim,
                                                op0=ALU.mult, op1=ALU.mult), ["s5t0", srcn], ["s5t0"])
            frac_sin(ai, ain, t0_[:, 0:n], "s5t0", n, t2_, "s5t2", t3_, "s5t3", 0.0)
            frac_sin(ar, arn, t0_[:, 0:n], "s5t0", n, t2_, "s5t2", t3_, "s5t3", 0.25)
            dv(lambda e: e.tensor_tensor(out=ai, in0=ai, in1=t1_[:, 0:n], op=ALU.mult), [ain, "s5t1"], [ain])
            dv(lambda e: e.tensor_tensor(out=ar, in0=ar, in1=t1_[:, 0:n], op=ALU.mult), [arn, "s5t1"], [arn])

        arc = self.sb("arc", [128, 64], F32)
        ais = self.sb("ais", [128, 64], F32)
        abar(psm[:, 0, :], psm[:, 1, :], psm[:, 2, :], "s5ps", 64, arc[:, :], "arc", ais[:, :], "ais")
        dv(lambda e: e.tensor_scalar(out=ais[64:128, :], in0=ais[64:128, :], scalar1=-1.0, scalar2=None, op0=ALU.mult), ["ais"], ["ais"])
        arB = self.sb("arB", [128, 512], F32)
        aiB = self.sb("aiB", [128, 512], F32)
        abar(pb[:, 0, :], pb[:, 1, :], pb[:, 2, :], "s5pb", 512, arB[:, :], "arB", aiB[:, :], "aiB")
        lre, lim, bre, bim = pb[:, 0, :], pb[:, 1, :], pb[:, 3, :], pb[:, 4, :]
        den, zre, zim, t6 = tmp[4], tmp[5], tmp[6], tmp[7]
        dv(lambda e: e.tensor_tensor(out=den[:], in0=lre, in1=lre, op=ALU.mult), ["s5pb"], ["s5t4"])
        dv(lambda e: e.tensor_tensor(out=t6[:], in0=lim, in1=lim, op=ALU.mult), ["s5pb"], ["s5t7"])
        dv(lambda e: e.tensor_tensor(out=den[:], in0=den[:], in1=t6[:], op=ALU.add), ["s5t4", "s5t7"], ["s5t4"])
        dv(lambda e: e.reciprocal(out=den[:], in_=den[:]), ["s5t4"], ["s5t4"])
        dv(lambda e: e.tensor_scalar(out=arB[:], in0=arB[:], scalar1=-1.0, scalar2=None, op0=ALU.add), ["arB"], ["arB"])
        dv(lambda e: e.tensor_tensor(out=zre[:], in0=arB[:], in1=lre, op=ALU.mult), ["arB", "s5pb"], ["s5t5"])
        dv(lambda e: e.tensor_tensor(out=t6[:], in0=aiB[:], in1=lim, op=ALU.mult), ["aiB", "s5pb"], ["s5t7"])
        dv(lambda e: e.tensor_tensor(out=zre[:], in0=zre[:], in1=t6[:], op=ALU.add), ["s5t5", "s5t7"], ["s5t5"])
        dv(lambda e: e.tensor_tensor(out=zre[:], in0=zre[:], in1=den[:], op=ALU.mult), ["s5t5", "s5t4"], ["s5t5"])
        dv(lambda e: e.tensor_tensor(out=zim[:], in0=aiB[:], in1=lre, op=ALU.mult), ["aiB", "s5pb"], ["s5t6"])
        dv(lambda e: e.tensor_tensor(out=t6[:], in0=arB[:], in1=lim, op=ALU.mult), ["arB", "s5pb"], ["s5t7"])
        dv(lambda e: e.tensor_tensor(out=zim[:], in0=zim[:], in1=t6[:], op=ALU.subtract), ["s5t6", "s5t7"], ["s5t6"])
        dv(lambda e: e.tensor_tensor(out=zim[:], in0=zim[:], in1=den[:], op=ALU.mult), ["s5t6", "s5t4"], ["s5t6"])
        BT = self.sb("BTcat", [128, 8, 128], F32)
        z3 = lambda t: t[:].rearrange("p (a b) -> p a b", a=8)
        p3 = lambda ap_: ap_.rearrange("p (a b) -> p a b", a=8)
        t0_, t1_ = tmp[0], tmp[1]
        dv(lambda e: e.tensor_tensor(out=t0_[:], in0=zre[:], in1=bre, op=ALU.mult), ["s5t5", "s5pb"], ["s5t0"])
        dv(lambda e: e.tensor_tensor(out=t1_[:], in0=zim[:], in1=bim, op=ALU.mult), ["s5t6", "s5pb"], ["s5t1"])
        dv(lambda e: e.tensor_tensor(out=BT[:, :, 0:64], in0=z3(t0_), in1=z3(t1_), op=ALU.subtract), ["s5t0", "s5t1"], ["BTcat"])
        dv(lambda e: e.tensor_tensor(out=t0_[:], in0=zre[:], in1=bim, op=ALU.mult), ["s5t5", "s5pb", "BTcat"], ["s5t0"])
        dv(lambda e: e.tensor_tensor(out=t1_[:], in0=zim[:], in1=bre, op=ALU.mult), ["s5t6", "s5pb", "BTcat"], ["s5t1"])
        dv(lambda e: e.tensor_tensor(out=BT[:, :, 64:128], in0=z3(t0_), in1=z3(t1_), op=ALU.add), ["s5t0", "s5t1"], ["BTcat"])
        CT = self.sb("CTs", [128, 64, 16], F32)
        self.dma(CT[:], ctn_d[:, :, :], w=["CTs"])
        dv(lambda e: e.tensor_scalar(out=CT[64:128, :, :], in0=CT[64:128, :, :], scalar1=-1.0, scalar2=None, op0=ALU.mult), ["CTs"], ["CTs"])
        CTpad = [self.sb("CTpad%d" % i, [128, 128], BF16) for i in range(8)]
        for i in range(8):
            self.op("pool", lambda e: e.memset(CTpad[i][:], 0.0), w=["CTpad%d" % i])
        if self.stage == 31:
            self.end_phase()
            return
        uT = self.sb("uT", [128, 4, NT], BF16)
        for gt in range(4):
            self.dma(uT[:, gt, :], self.projT[gt * 128:(gt + 1) * 128, :], r=["projT"], w=["uT"])
        Sp = [self.sb("S5S%d" % i, [128, NT], BF16) for i in range(2)]
        yacc = self.sb("yacc", [128, NX], F32)
        lhsB = self.sb("lhsB", [128, 128], BF16)
        Qf = [self.sb("Qf%d" % i, [128, 128], F32) for i in range(2)]
        Pf = [self.sb("Pf%d" % i, [128, 128], F32) for i in range(2)]
        Qb = self.sb("Qb", [128, 13, 128], BF16)
        import os
        NLEV = 13
        LEVRUN = int(os.environ.get('S5_LEV', '13'))
        MAXIT = int(os.environ.get('S5_MAXIT', '64'))
        SKIPSQ = int(os.environ.get('S5_SKIPSQ', '0'))
        itc = [0]
        blocks = [(c0, min(c0 + 512, NT)) for c0 in range(0, NT, 512)]
        evi = [0]

        def evac(dst_ap, dn, src_ap, sn):
            evi[0] ^= 1
            if evi[0]:
                self.op("act", lambda e: e.copy(out=dst_ap, in_=src_ap), r=[sn], w=[dn])
            else:
                self.op("dve", lambda e: e.tensor_copy(out=dst_ap, in_=src_ap), r=[sn], w=[dn])

        for gt in range(4):
            dv(lambda e: e.tensor_scalar(out=yacc[:, :], in0=uT[:, gt, NCTX:NT], scalar1=s5D[:, gt:gt + 1], scalar2=None, op0=ALU.mult),
               ["uT", "s5Ds"], ["yacc"])
            for d in range(2):
                for gl in range(8):
                    g = gt * 8 + gl
                    dg = d * 32 + g
                    itc[0] += 1
                    if itc[0] > MAXIT:
                        continue
                    dgt = d * 4 + gt
                    dv(lambda e: e.tensor_scalar(out=lhsB[:, :], in0=BT[:, dgt, :], scalar1=gmask[:, gl:gl + 1], scalar2=None, op0=ALU.mult),
                       ["BTcat", "gmasks"], ["lhsB"])
                    self.op("pool", lambda e: e.tensor_copy(out=CTpad[gl][:, gl * 16:(gl + 1) * 16], in_=CT[:, dg, :]),
                            r=["CTs"], w=["CTpad%d" % gl])
                    tj = tmp[2]
                    dv(lambda e: e.tensor_scalar(out=tj[:, 0:128], in0=jsw[:, :], scalar1=ais[:, dg:dg + 1], scalar2=None, op0=ALU.mult),
                       ["jsws", "ais"], ["s5t2"])
                    dv(lambda e: e.scalar_tensor_tensor(out=Qf[0][:, :], in0=idf[:, :], scalar=arc[:, dg:dg + 1], in1=tj[:, 0:128],
                                                        op0=ALU.mult, op1=ALU.add), ["idf", "arc", "s5t2"], ["Qf0"])
                    dv(lambda e: e.scalar_tensor_tensor(out=Pf[0][:, :], in0=idf[:, :], scalar=arc[:, dg:dg + 1], in1=tj[:, 0:128],
                                                        op0=ALU.mult, op1=ALU.subtract), ["idf", "arc", "s5t2"], ["Pf0"])
                    self.op("act", lambda e: e.copy(out=Qb[:, 0, :], in_=Qf[0][:, :]), r=["Qf0"], w=["Qb"])
                    for m in range(1, NLEV if not SKIPSQ else 1):
                        a_, b_ = (m - 1) % 2, m % 2
                        pq, pqn = self.pbank()
                        pq2, pqn2 = self.pbank()
                        self.op("pe", lambda e: e.matmul(pq[:, 0:128], lhsT=Pf[a_][:, :], rhs=Qf[a_][:, :], start=True, stop=True),
                                r=["Pf%d" % a_, "Qf%d" % a_], w=[pqn])
                        self.op("pe", lambda e: e.matmul(pq2[:, 0:128], lhsT=Qf[a_][:, :], rhs=Pf[a_][:, :], start=True, stop=True),
                                r=["Pf%d" % a_, "Qf%d" % a_], w=[pqn2])
                        self.op("act", lambda e: e.copy(out=Qf[b_][:, :], in_=pq[:, 0:128]), r=[pqn], w=["Qf%d" % b_])
                        self.op("dve", lambda e: e.tensor_copy(out=Pf[b_][:, :], in_=pq2[:, 0:128]), r=[pqn2], w=["Pf%d" % b_])
                        self.op("dve", lambda e: e.tensor_copy(out=Qb[:, m, :], in_=pq[:, 0:128]), r=[pqn], w=["Qb"])
                    cur = 0
                    for (c0, c1) in [(0, NCTX)] + [(NCTX + i * 512, NCTX + (i + 1) * 512) for i in range(8)]:
                        n = c1 - c0
                        if d == 0:
                            o0 = c0
                        else:
                            o0 = NX + c0 if c0 < NCTX else c0 - NCTX
                        px, pxn = self.pbank()
                        self.op("pe", lambda e: e.matmul(px[:, 0:n], lhsT=lhsB[:, :], rhs=uT[:, gt, c0:c1], start=True, stop=True),
                                r=["lhsB", "uT"], w=[pxn])
                        evac(Sp[cur][:, o0:o0 + n], "S5S%d" % cur, px[:, 0:n], pxn)
                    for m in range(LEVRUN):
                        sh = 1 << m
                        nxt = 1 - cur
                        for (c0, c1) in blocks:
                            n = c1 - c0
                            pl, pln = self.pbank()
                            if d == 0:
                                lo = max(c0, sh)
                                has = lo < c1
                                self.op("pe", lambda e: e.matmul(pl[:, 0:n], lhsT=idb[:, :], rhs=Sp[cur][:, c0:c1], start=True, stop=not has),
                                        r=["idb", "S5S%d" % cur], w=[pln])
                                if has:
                                    self.op("pe", lambda e: e.matmul(pl[:, lo - c0:n], lhsT=Qb[:, m, :], rhs=Sp[cur][:, lo - sh:c1 - sh],
                                                                     start=False, stop=True), r=["Qb", "S5S%d" % cur], w=[pln])
                            else:
                                hi = min(c1, NT - sh)
                                has = hi > c0
                                self.op("pe", lambda e: e.matmul(pl[:, 0:n], lhsT=idb[:, :], rhs=Sp[cur][:, c0:c1], start=True, stop=not has),
                                        r=["idb", "S5S%d" % cur], w=[pln])
                                if has:
                                    self.op("pe", lambda e: e.matmul(pl[:, 0:hi - c0], lhsT=Qb[:, m, :], rhs=Sp[cur][:, c0 + sh:hi + sh],
                                                                     start=False, stop=True), r=["Qb", "S5S%d" % cur], w=[pln])
                            evac(Sp[nxt][:, c0:c1], "S5S%d" % nxt, pl[:, 0:n], pln)
                        cur = nxt
                    for tb in range(8):
                        c0 = (NCTX if d == 0 else 0) + tb * 512
                        py, pyn = self.pbank()
                        self.op("pe", lambda e: e.matmul(py[:, :], lhsT=CTpad[gl][:, :], rhs=Sp[cur][:, c0:c0 + 512], start=True, stop=True),
                                r=["CTpad%d" % gl, "S5S%d" % cur], w=[pyn])
                        dv(lambda e: e.tensor_tensor(out=yacc[:, tb * 512:(tb + 1) * 512], in0=py[:, :], in1=yacc[:, tb * 512:(tb + 1) * 512],
                                                     op=ALU.add), [pyn, "yacc"], ["yacc"])
            self.dma(YAs[gt * 128:(gt + 1) * 128, :], yacc[:, :], r=["yacc"], w=["YAs"])
        self.end_phase()

    def phase_C1(self):
        nc = self.nc
        idb, bct = self.idb, self.bct
        wglu_d = self.inp("s5_w_glu", [W, W])
        wproj_d = self.inp("s5_w_proj", [W, D])
        wo_d = self.inp("rwkv_w_o", [W, D])
        wout_d = self.inp("w_out", [D, D])
        g2_d = self.inp("rwkv_g2", [128, W])
        ln_d = self.inp("lnrows", [2, D])
        jrev_d = self.inp("jrev", [128, 128])
        self.x2s = x2s = self.scratch("x2s", [NX, D], F32)
        self.bcast_dram_row(0, ln_d[0:1, :], "ln_in")
        self.bcast_dram_row(1, ln_d[1:2, :], "ln_in")
        self.begin_phase()
        norm_T, hT, hTn = self.norm_tiles("c")
        NG = IN_COLS - 2304
        wgate = self.sb("wgate", [128, KT, NG], BF16)
        wv = self.w_in.rearrange("(k p) n -> p k n", p=128)
        for k in range(KT):
            for c0 in range(0, NG, NG // 2):
                self.wload(wgate[:, k, c0:c0 + NG // 2], wv[:, k, 2304 + c0:2304 + c0 + NG // 2], "wgate")
        wglu = self.sb("wglu", [128, 4, W], BF16)
        self.wload(wglu[:], wglu_d.rearrange("(k p) n -> p k n", p=128), "wglu", shape3=(4, W))
        wproj = self.sb("wproj", [128, 4, D], BF16)
        wo = self.sb("wo", [128, 4, D], BF16)
        for k in range(0, 4, 2):
            self.wload(wproj[:, k:k + 2, :], wproj_d.rearrange("(k p) n -> p k n", p=128)[:, k:k + 2, :], "wproj", shape3=(2, D))
            self.wload(wo[:, k:k + 2, :], wo_d.rearrange("(k p) n -> p k n", p=128)[:, k:k + 2, :], "wo", shape3=(2, D))
        wout = self.sb("wout", [128, KT, D], BF16)
        for k in range(0, KT, 2):
            self.wload(wout[:, k:k + 2, :], wout_d.rearrange("(k p) n -> p k n", p=128)[:, k:k + 2, :], "wout", shape3=(2, D))
        g2b = self.sb("g2b", [128, W], BF16)
        self.wload(g2b[:], g2_d[:, :], "g2b")
        jrf = self.sb("jrf", [128, 128], F32)
        jrb = self.sb("jrb", [128, 128], BF16)
        self.dma(jrf[:], jrev_d[:, :], w=["jrf"])
        self.op("dve", lambda e: e.tensor_copy(out=jrb[:], in_=jrf[:]), r=["jrf"], w=["jrb"])
        epsln = self.sb("epsln", [128, 1], F32)
        self.op("dve", lambda e: e.memset(epsln[:], 64e-5), w=["epsln"])
        xs1 = [self.sb("xs1_%d" % i, [128, D], F32) for i in range(2)]
        sgate = self.sb("sgate", [128, 2048], F32)
        sgdT = self.sb("sgdT", [128, 128], BF16)
        gsb = self.sb("gsb", [128, W], F32)
        ya4 = self.sb("ya4", [128, W], F32)
        gt1 = self.sb("gt1", [128, W], F32)
        guh = self.sb("guh", [128, W], F32)
        yag = self.sb("yag", [128, W], BF16)
        sig = self.sb("sigz", [128, W], F32)
        ya2T = self.sb("ya2T", [128, W], BF16)
        ma = self.sb("ma", [128, D], F32)
        yf = self.sb("yf", [128, W], BF16)
        yb = self.sb("yb", [128, W], BF16)
        bon = self.sb("bon", [128, W], BF16)
        ysum = self.sb("ysum", [128, W], F32)
        cen = self.sb("cen", [128, W], F32)
        sq = self.sb("sqc", [128, W], F32)
        st8 = self.sb("st8", [128, 4, 8], F32)
        ynb = self.sb("ynb", [128, W], BF16)
        ybT = self.sb("ybT", [128, W], BF16)
        tmpm = self.sb("tmpm", [128, W], F32)
        mg = self.sb("mg", [128, D], BF16)
        mT = self.sb("mT", [128, D], BF16)
        x2t = self.sb("x2t", [128, D], F32)
        GC = math.sqrt(2.0 / math.pi)

        def bc_last(ap2, n):
            return bass.AP(ap2.tensor, ap2.offset, [list(x_) for x_ in ap2.ap] + [[0, n]])

        v3 = lambda t: t[:, :].rearrange("p (h j) -> p h j", j=64)
        k3 = lambda t: t[:, :].rearrange("p (k c) -> p k c", c=128)
        YAv = self.YAs.rearrange("(k p) t -> p k t", p=128)
        for tt in range(NX // 128):
            xt0 = tt * 128
            t0 = NCTX + xt0
            xi = tt % 2
            xsn = "xs1_%d" % xi
            xs_ = xs1[xi]
            self.dma(xs_[:, :], self.x1s[t0:t0 + 128, :], r=["x1s"], w=[xsn])
            norm_T(xs_[:, :], xsn, 3, 4, 0)
            pt, pn = self.pbank()
            for k in range(KT):
                self.op("pe", lambda e: e.matmul(pt[:, 0:128], lhsT=wgate[:, k, 0:128], rhs=hT[:, k, 0:128],
                                                 start=(k == 0), stop=(k == KT - 1)), r=["wgate", hTn], w=[pn])
            self.op("act", lambda e: e.activation(out=sgdT[:, :], in_=pt[:, 0:128], func=AF.Sigmoid), r=[pn], w=["sgdT"])
            pt, pn = self.pbank()
            self.op("pe", lambda e: e.matmul(pt[:, :], lhsT=sgdT[:, :], rhs=g2b[:, :], start=True, stop=True), r=["sgdT", "g2b"], w=[pn])
            self.op("act", lambda e: e.copy(out=gsb[:, :], in_=pt[:, :]), r=[pn], w=["gsb"])
            for cb in range(4):
                pt, pn = self.pbank()
                for k in range(KT):
                    self.op("pe", lambda e: e.matmul(pt[:, :], lhsT=hT[:, k, 0:128], rhs=wgate[:, k, 128 + cb * 512:128 + (cb + 1) * 512],
                                                     start=(k == 0), stop=(k == KT - 1)), r=["wgate", hTn], w=[pn])
                self.op("act", lambda e: e.activation(out=sgate[:, cb * 512:(cb + 1) * 512], in_=pt[:, :], func=AF.Sigmoid),
                        r=[pn], w=["sgate"])
            self.dma(k3(ya4), YAv[:, :, xt0:xt0 + 128], r=["YAs"], w=["ya4"])
            self.op("pool", lambda e: e.tensor_tensor(out=gt1[:, :], in0=ya4[:, :], in1=ya4[:, :], op=ALU.mult), r=["ya4"], w=["gt1"])
            self.op("dve", lambda e: e.tensor_scalar(out=gt1[:, :], in0=gt1[:, :], scalar1=0.044715, scalar2=1.0, op0=ALU.mult, op1=ALU.add),
                    r=["gt1"], w=["gt1"])
            self.op("dve", lambda e: e.tensor_tensor(out=gt1[:, :], in0=gt1[:, :], in1=ya4[:, :], op=ALU.mult), r=["gt1", "ya4"], w=["gt1"])
            self.op("act", lambda e: e.activation(out=gt1[:, :], in_=gt1[:, :], func=AF.Tanh, scale=GC), r=["gt1"], w=["gt1"])
            self.op("pool", lambda e: e.tensor_scalar(out=guh[:, :], in0=ya4[:, :], scalar1=0.5, scalar2=None, op0=ALU.mult), r=["ya4"], w=["guh"])
            self.op("dve", lambda e: e.scalar_tensor_tensor(out=yag[:, :], in0=gt1[:, :], scalar=1.0, in1=guh[:, :], op0=ALU.add, op1=ALU.mult),
                    r=["gt1", "guh"], w=["yag"])
            for ct in range(4):
                pt, pn = self.pbank()
                for k in range(4):
                    self.op("pe", lambda e: e.matmul(pt[:, 0:128], lhsT=wglu[:, k, ct * 128:(ct + 1) * 128], rhs=k3(yag)[:, k, :],
                                                     start=(k == 0), stop=(k == 3)), r=["wglu", "yag"], w=[pn])
                self.op("act", lambda e: e.activation(out=sig[:, ct * 128:(ct + 1) * 128], in_=pt[:, 0:128], func=AF.Sigmoid), r=[pn], w=["sigz"])
            self.op("dve", lambda e: e.tensor_tensor(out=ya2T[:, :], in0=yag[:, :], in1=sig[:, :], op=ALU.mult), r=["yag", "sigz"], w=["ya2T"])
            for hlf in range(2):
                pt, pn = self.pbank()
                for k in range(4):
                    self.op("pe", lambda e: e.matmul(pt[:, :], lhsT=k3(ya2T)[:, k, :], rhs=wproj[:, k, hlf * 512:(hlf + 1) * 512],
                                                     start=(k == 0), stop=(k == 3)), r=["ya2T", "wproj"], w=[pn])
                self.op("dve", lambda e: e.tensor_tensor(out=ma[:, hlf * 512:(hlf + 1) * 512], in0=pt[:, :],
                                                         in1=sgate[:, hlf * 512:(hlf + 1) * 512], op=ALU.mult), r=[pn, "sgate"], w=["ma"])
            sA = NT + 128 - t0
            self.dma(yf[:, :], self.YTs[0, t0:t0 + 128, :], r=["YTs"], w=["yf"])
            self.dma(yb[:, :], self.YTs[1, sA:sA + 128, :], r=["YTs"], w=["yb"])
            self.dma(bon[:, :], self.BONs[t0:t0 + 128, :], r=["BONs"], w=["bon"])
            pt, pn = self.pbank()
            self.op("pe", lambda e: e.matmul(pt[:, :], lhsT=jrb[:, :], rhs=yb[:, :], start=True, stop=True), r=["jrb", "yb"], w=[pn])
            self.op("dve", lambda e: e.tensor_tensor(out=ysum[:, :], in0=pt[:, :], in1=yf[:, :], op=ALU.add), r=[pn, "yf"], w=["ysum"])
            self.op("dve", lambda e: e.tensor_reduce(out=st8[:, 0, :], in_=v3(ysum), axis=AX.X, op=ALU.add), r=["ysum"], w=["st8"])
            self.op("dve", lambda e: e.tensor_scalar(out=st8[:, 1, :], in0=st8[:, 0, :], scalar1=1.0 / 64, scalar2=None, op0=ALU.mult),
                    r=["st8"], w=["st8"])
            self.op("dve", lambda e: e.tensor_tensor(out=v3(cen), in0=v3(ysum), in1=bc_last(st8[:, 1, :], 64), op=ALU.subtract),
                    r=["ysum", "st8"], w=["cen"])
            self.op("pool", lambda e: e.tensor_tensor(out=sq[:, :], in0=cen[:, :], in1=cen[:, :], op=ALU.mult), r=["cen"], w=["sqc"])
            self.op("dve", lambda e: e.tensor_reduce(out=st8[:, 2, :], in_=v3(sq), axis=AX.X, op=ALU.add), r=["sqc", "st8"], w=["st8"])
            self.op("act", lambda e: e.activation(out=st8[:, 3, :], in_=st8[:, 2, :], func=AF.Sqrt, scale=1.0 / 64, bias=epsln[:, 0:1]),
                    r=["st8", "epsln"], w=["st8"])
            self.op("dve", lambda e: e.reciprocal(out=st8[:, 3, :], in_=st8[:, 3, :]), r=["st8"], w=["st8"])
            self.op("dve", lambda e: e.tensor_tensor(out=v3(cen), in0=v3(cen), in1=bc_last(st8[:, 3, :], 64), op=ALU.mult),
                    r=["cen", "st8"], w=["cen"])
            self.op("pool", lambda e: e.tensor_tensor(out=cen[:, :], in0=cen[:, :], in1=bct[0][:, 0:W], op=ALU.mult), r=["cen", "bc0"], w=["cen"])
            self.op("pool", lambda e: e.tensor_tensor(out=cen[:, :], in0=cen[:, :], in1=bct[1][:, 0:W], op=ALU.add), r=["cen", "bc1"], w=["cen"])
            self.op("dve", lambda e: e.tensor_tensor(out=cen[:, :], in0=cen[:, :], in1=bon[:, :], op=ALU.add), r=["cen", "bon"], w=["cen"])
            self.op("dve", lambda e: e.tensor_tensor(out=ynb[:, :], in0=cen[:, :], in1=gsb[:, :], op=ALU.mult), r=["cen", "gsb"], w=["ynb"])
            pt, pn = self.pbank()
            ptb = pt[:].bitcast(BF16)
            for k in range(4):
                self.op("pe", lambda e: e.transpose(out=ptb[:, k * 128:(k + 1) * 128], in_=ynb[:, k * 128:(k + 1) * 128], identity=idb[:]),
                        r=["ynb", "idb"], w=[pn])
            self.op("act", lambda e: e.copy(out=ybT[:, :], in_=ptb[:, 0:W]), r=[pn], w=["ybT"])
            for hlf in range(2):
                pt, pn = self.pbank()
                for k in range(4):
                    self.op("pe", lambda e: e.matmul(pt[:, :], lhsT=k3(ybT)[:, k, :], rhs=wo[:, k, hlf * 512:(hlf + 1) * 512],
                                                     start=(k == 0), stop=(k == 3)), r=["ybT", "wo"], w=[pn])
                self.op("dve", lambda e: e.tensor_tensor(out=tmpm[:, :], in0=pt[:, :], in1=sgate[:, 1024 + hlf * 512:1024 + (hlf + 1) * 512],
                                                         op=ALU.mult), r=[pn, "sgate"], w=["tmpm"])
                self.op("pool", lambda e: e.tensor_tensor(out=mg[:, hlf * 512:(hlf + 1) * 512], in0=tmpm[:, :],
                                                          in1=ma[:, hlf * 512:(hlf + 1) * 512], op=ALU.add), r=["tmpm", "ma"], w=["mg"])
            pt, pn = self.pbank()
            ptb = pt[:].bitcast(BF16)
            for k in range(KT):
                self.op("pe", lambda e: e.transpose(out=ptb[:, k * 128:(k + 1) * 128], in_=mg[:, k * 128:(k + 1) * 128], identity=idb[:]),
                        r=["mg", "idb"], w=[pn])
            self.op("act", lambda e: e.copy(out=mT[:, :], in_=ptb[:, :]), r=[pn], w=["mT"])
            for hlf in range(2):
                pt, pn = self.pbank()
                for k in range(KT):
                    self.op("pe", lambda e: e.matmul(pt[:, :], lhsT=k3(mT)[:, k, :], rhs=wout[:, k, hlf * 512:(hlf + 1) * 512],
                                                     start=(k == 0), stop=(k == KT - 1)), r=["mT", "wout"], w=[pn])
                self.op("dve", lambda e: e.tensor_tensor(out=tmpm[:, :], in0=pt[:, :], in1=bct[5][:, hlf * 512:(hlf + 1) * 512], op=ALU.mult),
                        r=[pn, "bc5"], w=["tmpm"])
                self.op("pool", lambda e: e.tensor_tensor(out=x2t[:, hlf * 512:(hlf + 1) * 512], in0=tmpm[:, :],
                                                          in1=xs_[:, hlf * 512:(hlf + 1) * 512], op=ALU.add), r=["tmpm", xsn], w=["x2t"])
            self.dma(x2s[xt0:xt0 + 128, :], x2t[:, :], r=["x2t"], w=["x2s"])
        self.end_phase()

    def phase_C2(self):
        out, x2s, bct = self.out, self.x2s, self.bct
        self.make_mod_tiles(0, 2, 2, 0, 0.5)
        self.bcast_dram_row(6, self.din["final_g"][:, :], "fg_in")
        supers = [(i * 512, 4) for i in range(NX // 512)]
        st = {}

        def epi(xs, si, t0, tt, norm_T):
            if "ot" not in st:
                st["ot"] = [self.sb("ot%d" % i, [128, D], F32) for i in range(2)]
            oi = self.rot.get("ot", 0)
            self.rot["ot"] = 1 - oi
            ot = st["ot"][oi]
            ssq = norm_T(xs[:, tt, :], "xs", None, None, 0)
            self.op("dve", lambda e: e.scalar_tensor_tensor(out=ot[:, :], in0=xs[:, tt, :], scalar=ssq[:, 2:3], in1=bct[6][:, :],
                                                            op0=ALU.mult, op1=ALU.mult), r=["xs", "ssqd", "bc6"], w=["ot%d" % oi])
            self.dma(out[t0 + tt * 128:t0 + (tt + 1) * 128, :], ot[:, :], r=["ot%d" % oi], w=["out"])

        self.ffn_phase(1, lambda t0, n: (x2s[t0:t0 + n, :], "x2s"), supers, lambda si: (0, 1, 2), epi, "d")

    def finish(self):
        self.S.drain_all("sp")
        print("ninst", self.S.ninst, "nwait", self.S.nwait)
        if self.scope is not None:
            self.scope.close()
            self.scope = None
        self.es.close()


def host_inputs(inputs, b):
    f = lambda a: np.ascontiguousarray(np.asarray(a, dtype=np.float32))
    m = {
        "x": f(inputs["x"][b]),
        "ctx": f(inputs["ctx"][b]),
        "cvec": f(np.stack([np.asarray(inputs["c"])[b], np.asarray(inputs["c_ctx"])], 0)),
        "w_mod": f(inputs["w_mod"][0]),
        "b_mod": f(np.asarray(inputs["b_mod"])[0][None, :]),
        "norm_g": f(inputs["norm_g"][0]),
        "final_g": f(np.asarray(inputs["final_g"])[None, :]),
        "ffn_w_gate": f(inputs["ffn_w_gate"][0]),
        "ffn_w_up": f(inputs["ffn_w_up"][0]),
        "ffn_w_down": f(inputs["ffn_w_down"][0]),
        "w_in": f(inputs["w_in"][0]),
        "ident": np.eye(128, dtype=np.float32),
    }
    cv = np.asarray(inputs["rwkv_conv"], np.float32)[0].reshape(9, 3, 4, 128)
    m["convw"] = f(cv.transpose(3, 1, 2, 0).reshape(128, 12, 9))
    m["w0T"] = f(np.asarray(inputs["rwkv_w0"], np.float32)[0].reshape(2, 4, 128).transpose(2, 0, 1).reshape(128, 8))
    m["a0T"] = f(np.asarray(inputs["rwkv_a0"], np.float32)[0].reshape(2, 4, 128).transpose(2, 0, 1).reshape(128, 8))
    v4 = np.stack([np.asarray(inputs[k], np.float32)[0].reshape(-1) for k in
                   ("rwkv_k_k", "rwkv_k_a", "rwkv_r_k", "rwkv_ln_g", "rwkv_ln_b")], 0)
    m["vec4"] = f(v4.reshape(5, 4, 128).transpose(2, 0, 1))
    m["w2"] = f(np.asarray(inputs["rwkv_w2"], np.float32)[0].reshape(128, 512))
    m["a2"] = f(np.asarray(inputs["rwkv_a2"], np.float32)[0].reshape(128, 512))
    blk = np.zeros((128, 128), np.float32); blk[:64, :64] = 1; blk[64:, 64:] = 1
    m["blk1"] = blk
    mk = np.zeros((16, 512), np.float32)
    for dh in range(16):
        h_ = dh % 8
        mk[dh, h_ * 64:(h_ + 1) * 64] = 1
    m["mask16"] = mk
    rr = np.arange(128)
    hh, dd, tt_ = rr // 16, (rr // 8) % 2, rr % 8
    mR = np.zeros((128, 512), np.float32)
    for r_ in range(128):
        mR[r_, hh[r_] * 64:(hh[r_] + 1) * 64] = 1
    m["maskR"] = mR
    same = (rr[:, None] // 8) == (rr[None, :] // 8)
    m["mlow"] = (same & (tt_[:, None] > tt_[None, :])).astype(np.float32)
    m["mup"] = (same & (tt_[:, None] < tt_[None, :])).astype(np.float32)
    m["mupi"] = (same & (tt_[:, None] <= tt_[None, :])).astype(np.float32)
    sv = np.zeros((16, 128), np.float32)
    for r_ in range(128):
        sv[dd[r_] * 8 + tt_[r_], r_] = 1
    m["selv"] = sv
    def big(a):
        a = np.asarray(a, np.float32)[0]
        if a.ndim == 3:
            a = np.broadcast_to(a[..., None], a.shape + (16,))
        a = a.reshape(2, 4, 8, 64, 16)
        return a.transpose(2, 4, 0, 1, 3).reshape(128, 512)
    ldt = np.broadcast_to(np.asarray(inputs["s5_log_dt"], np.float32)[:, :, :, None], (1, 2, 32, 64))
    m["s5p"] = f(np.stack([big(inputs["s5_A_re"]), big(inputs["s5_A_im"]), big(ldt), big(inputs["s5_B_re"]), big(inputs["s5_B_im"])], 1))
    def small(a):
        a = np.asarray(a, np.float32)[0].reshape(64, 64).T
        return np.concatenate([a, a], 0)
    m["s5s"] = f(np.stack([small(inputs["s5_A_re"]), small(inputs["s5_A_im"]), small(ldt)], 1))
    cre = np.asarray(inputs["s5_C_re"], np.float32)[0].reshape(64, 16, 64).transpose(2, 0, 1)
    cim = np.asarray(inputs["s5_C_im"], np.float32)[0].reshape(64, 16, 64).transpose(2, 0, 1)
    m["ctn"] = f(np.concatenate([cre, cim], 0))
    m["s5D"] = f(np.asarray(inputs["s5_D"], np.float32)[0].reshape(4, 128).T)
    js = np.zeros((128, 128), np.float32)
    for p_ in range(64):
        js[p_, 64 + p_] = 1; js[64 + p_, p_] = 1
    m["jsw"] = js
    gm = np.zeros((128, 8), np.float32)
    for p_ in range(128):
        gm[p_, p_ // 16] = 1
    m["gmask"] = gm
    m["s5_w_glu"] = f(inputs["s5_w_glu"][0])
    m["s5_w_proj"] = f(inputs["s5_w_proj"][0])
    m["rwkv_w_o"] = f(inputs["rwkv_w_o"][0])
    m["w_out"] = f(inputs["w_out"][0])
    m["rwkv_g2"] = f(inputs["rwkv_g2"][0])
    ln = np.zeros((2, 1024), np.float32)
    ln[0, :512] = np.asarray(inputs["rwkv_ln_g"], np.float32)[0]
    ln[1, :512] = np.asarray(inputs["rwkv_ln_b"], np.float32)[0]
    m["lnrows"] = ln
    m["jrev"] = np.ascontiguousarray(np.eye(128, dtype=np.float32)[::-1])
    return m


def run(inputs, stage=99, cores=8):
    kb = K(stage)
    kb.build()
    in_maps = []
    for b in range(cores):
        m = host_inputs(inputs, b)
        in_maps.append({k: v for k, v in m.items() if k in kb.din})
    res = run_bass_kernel_spmd(kb.nc, in_maps, core_ids=list(range(cores)))
    return res


def kernel(**inputs):
    res = run(inputs)
    outs = [np.asarray(r["out"], dtype=np.float32) for r in res.results]
    return np.stack(outs, 0)
```
